# Optimizing a Trainium2 kernel written in Bass

```python
import math
import jax, jax.numpy as jnp
from jax import lax
import numpy as np

D_MODEL = 1024
BATCH = 32
SEQ = 2048
DEPTH = 2

N_A_LAYERS = DEPTH // 2
N_B_LAYERS = DEPTH - N_A_LAYERS
D_FF = 2816
RMS_EPS = 1e-6

S5_GROUP_CH = 16
S5_GROUPS = D_MODEL // S5_GROUP_CH
S5_STATE = 64
DT_MIN = 0.001
DT_MAX = 0.1

N_HEADS = 16
HEAD_DIM = D_MODEL // N_HEADS
N_KV_HEADS = 4
Q_PER_KV = N_HEADS // N_KV_HEADS
CMP_LEN = 32
CMP_STRIDE = 16
CMP_HIDDEN = 2 * HEAD_DIM
SEL_LEN = 64
SEL_TOPN = 8
WIN = 512
WIN_QB = 128
SEL_QB = 64
N_BRANCH = 3
ATTN_SCALE = HEAD_DIM ** -0.5

NUM_BUCKETS = 32
MAX_DISTANCE = 128

NEG_INF = -1e30
BIG = 1e9

kernel_name = 'yoco_s5_nsa_macaron_trunk'


def _rmsnorm(x, g):
    xf = x.astype(jnp.float32)
    y = xf * lax.rsqrt(jnp.mean(xf * xf, axis=-1, keepdims=True) + RMS_EPS)
    return (y * g.astype(jnp.float32)).astype(x.dtype)


def _swiglu(x, w_in, w_out):
    a, b = jnp.split(x @ w_in, 2, axis=-1)
    return (jax.nn.silu(a) * b) @ w_out


def _rel_bucket(dist):
    n = jnp.maximum(dist, 0)
    max_exact = NUM_BUCKETS // 2
    logv = jnp.log(jnp.maximum(n, 1).astype(jnp.float32) / max_exact) / math.log(MAX_DISTANCE / max_exact)
    large = jnp.minimum(max_exact + (logv * (NUM_BUCKETS - max_exact)).astype(jnp.int32), NUM_BUCKETS - 1)
    return jnp.where(n < max_exact, n, large)


def _masked_softmax(logits, mask):
    return jax.nn.softmax(jnp.where(mask, logits, NEG_INF), axis=-1)


def _s5_mixer(u, a_re, a_im, log_dt, b_re, b_im, c_re, c_im, d_skip, w_glu):
    bsz, seq, _ = u.shape
    ug = u.reshape(bsz, seq, S5_GROUPS, S5_GROUP_CH)
    dt = jnp.exp(log_dt)[:, None]
    mag = jnp.exp(a_re * dt)
    ab_re = mag * jnp.cos(a_im * dt)
    ab_im = mag * jnp.sin(a_im * dt)
    den = a_re * a_re + a_im * a_im
    z_re = ((ab_re - 1.0) * a_re + ab_im * a_im) / den
    z_im = (ab_im * a_re - (ab_re - 1.0) * a_im) / den
    bb_re = z_re[..., None] * b_re - z_im[..., None] * b_im
    bb_im = z_re[..., None] * b_im + z_im[..., None] * b_re
    bu_re = jnp.einsum('bsgh,gph->bsgp', ug, bb_re)
    bu_im = jnp.einsum('bsgh,gph->bsgp', ug, bb_im)
    shape = (1, seq, S5_GROUPS, S5_STATE)
    la_re = jnp.broadcast_to(ab_re, shape)
    la_im = jnp.broadcast_to(ab_im, shape)

    def combine(e1, e2):
        ar1, ai1, br1, bi1 = e1
        ar2, ai2, br2, bi2 = e2
        return (ar2 * ar1 - ai2 * ai1,
                ar2 * ai1 + ai2 * ar1,
                ar2 * br1 - ai2 * bi1 + br2,
                ar2 * bi1 + ai2 * br1 + bi2)

    _, _, x_re, x_im = lax.associative_scan(combine, (la_re, la_im, bu_re, bu_im), axis=1)
    y = (jnp.einsum('bsgp,ghp->bsgh', x_re, c_re) - jnp.einsum('bsgp,ghp->bsgh', x_im, c_im)
         + d_skip * ug)
    y = jax.nn.gelu(y.reshape(bsz, seq, D_MODEL))
    return y * jax.nn.sigmoid(y @ w_glu)


def _shared_kv(h, kv_norm, w_kv, k_norm_cmp, k_norm_slc, k_norm_win,
               cmp_pos_k, cmp_pos_v, cmp_k_w1, cmp_k_w2, cmp_v_w1, cmp_v_w2):
    bsz, seq, _ = h.shape
    kv = (_rmsnorm(h, kv_norm) @ w_kv).reshape(bsz, seq, 2 * N_BRANCH, N_KV_HEADS, HEAD_DIM)
    n_cmp = (seq - CMP_LEN) // CMP_STRIDE + 1
    blk = jnp.arange(n_cmp)[:, None] * CMP_STRIDE + jnp.arange(CMP_LEN)[None, :]

    def compress(src, pos, w1, w2):
        blocks = src[:, blk] + pos[None, None, :, None, :]
        hid = jax.nn.gelu(jnp.einsum('bclgd,ldh->bcgh', blocks, w1))
        return jnp.einsum('bcgh,hd->bcgd', hid, w2)

    k_cmp = _rmsnorm(compress(kv[:, :, 0], cmp_pos_k, cmp_k_w1, cmp_k_w2), k_norm_cmp)
    v_cmp = compress(kv[:, :, 1], cmp_pos_v, cmp_v_w1, cmp_v_w2)
    k_slc = _rmsnorm(kv[:, :, 2], k_norm_slc)
    v_slc = kv[:, :, 3]
    k_win = _rmsnorm(kv[:, :, 4], k_norm_win)
    v_win = kv[:, :, 5]
    return (k_cmp, v_cmp, k_slc, v_slc, k_win, v_win)


def _compressed_branch(q, k_cmp, v_cmp, rel_bias):
    seq = q.shape[1]
    n_cmp = k_cmp.shape[1]
    n_sel = seq // SEL_LEN
    t = jnp.arange(seq)
    c_start = jnp.arange(n_cmp) * CMP_STRIDE
    dist = t[:, None] - (c_start + CMP_LEN - 1)[None, :]
    valid = dist >= 0
    bias = rel_bias[_rel_bucket(dist)].transpose(2, 0, 1).reshape(N_KV_HEADS, Q_PER_KV, seq, n_cmp)
    logits = jnp.einsum('bsgrd,bcgd->bgrsc', q, k_cmp).astype(jnp.float32) * ATTN_SCALE + bias
    p = _masked_softmax(logits, valid) * jnp.any(valid, axis=-1)[:, None].astype(jnp.float32)
    o = jnp.einsum('bgrsc,bcgd->bsgrd', p.astype(v_cmp.dtype), v_cmp)
    j_start = jnp.arange(n_sel) * SEL_LEN
    ov = jnp.clip(jnp.minimum(c_start[:, None] + CMP_LEN, j_start[None, :] + SEL_LEN)
                  - jnp.maximum(c_start[:, None], j_start[None, :]), 0, None)
    overlap = ov.astype(jnp.float32) / CMP_LEN
    p_slc = jnp.einsum('bgrsc,cj->bgsj', p, overlap)
    return o, p_slc


def _select_blocks(p_slc):
    seq, n_sel = p_slc.shape[2], p_slc.shape[3]
    k = min(SEL_TOPN, n_sel)
    t = jnp.arange(seq)[:, None]
    j = jnp.arange(n_sel)[None, :]
    cur = t // SEL_LEN
    forced = (j == 0) | (j == cur) | (j == cur - 1)
    causal = j * SEL_LEN <= t
    score = jnp.where(forced, BIG, jnp.where(causal, p_slc, -BIG))
    _, idx = lax.top_k(score, k)
    return idx


def _selected_branch(q, k_slc, v_slc, idx, rel_bias):
    bsz, seq = q.shape[0], q.shape[1]
    n_top = idx.shape[-1]
    kl = n_top * SEL_LEN
    n_ch = seq // SEL_QB
    kst = k_slc.transpose(0, 2, 1, 3)
    vst = v_slc.transpose(0, 2, 1, 3)
    table = rel_bias.reshape(NUM_BUCKETS, N_KV_HEADS, Q_PER_KV).transpose(1, 0, 2)
    gather = jax.vmap(jax.vmap(lambda a, i: a[i]))
    qc = q.reshape(bsz, n_ch, SEL_QB, N_KV_HEADS, Q_PER_KV, HEAD_DIM).transpose(1, 0, 2, 3, 4, 5)
    ic = idx.reshape(bsz, N_KV_HEADS, n_ch, SEL_QB, n_top).transpose(2, 0, 1, 3, 4)

    def step(args):
        ci, qb, ib = args
        tok = (ib[..., None] * SEL_LEN + jnp.arange(SEL_LEN)).reshape(bsz, N_KV_HEADS, SEL_QB * kl)
        kg = gather(kst, tok).reshape(bsz, N_KV_HEADS, SEL_QB, kl, HEAD_DIM)
        vg = gather(vst, tok).reshape(bsz, N_KV_HEADS, SEL_QB, kl, HEAD_DIM)
        tok = tok.reshape(bsz, N_KV_HEADS, SEL_QB, kl)
        qpos = ci * SEL_QB + jnp.arange(SEL_QB)
        dist = qpos[None, None, :, None] - tok
        bias = table[jnp.arange(N_KV_HEADS)[None, :, None, None], _rel_bucket(dist)]
        logits = (jnp.einsum('bqgrd,bgqkd->bgrqk', qb, kg).astype(jnp.float32) * ATTN_SCALE
                  + bias.transpose(0, 1, 4, 2, 3))
        p = _masked_softmax(logits, (dist >= 0)[:, :, None])
        return jnp.einsum('bgrqk,bgqkd->bqgrd', p.astype(vg.dtype), vg)

    out = lax.map(step, (jnp.arange(n_ch), qc, ic))
    return out.transpose(1, 0, 2, 3, 4, 5).reshape(bsz, seq, N_KV_HEADS, Q_PER_KV, HEAD_DIM)


def _window_branch(q, k_win, v_win, rel_bias):
    bsz, seq = q.shape[0], q.shape[1]
    n_blk = seq // WIN_QB
    span = WIN + WIN_QB
    kp = jnp.pad(k_win, ((0, 0), (WIN, 0), (0, 0), (0, 0)))
    vp = jnp.pad(v_win, ((0, 0), (WIN, 0), (0, 0), (0, 0)))
    dist = WIN + jnp.arange(WIN_QB)[:, None] - jnp.arange(span)[None, :]
    band = (dist >= 0) & (dist < WIN)
    bias = rel_bias[_rel_bucket(dist)].transpose(2, 0, 1).reshape(N_KV_HEADS, Q_PER_KV, WIN_QB, span)
    qb_all = q.reshape(bsz, n_blk, WIN_QB, N_KV_HEADS, Q_PER_KV, HEAD_DIM).transpose(1, 0, 2, 3, 4, 5)

    def step(args):
        bi, qb = args
        start = bi * WIN_QB
        kb = lax.dynamic_slice_in_dim(kp, start, span, axis=1)
        vb = lax.dynamic_slice_in_dim(vp, start, span, axis=1)
        kpos = start - WIN + jnp.arange(span)
        mask = band & (kpos >= 0)[None, :]
        logits = jnp.einsum('bqgrd,bkgd->bgrqk', qb, kb).astype(jnp.float32) * ATTN_SCALE + bias
        p = _masked_softmax(logits, mask)
        return jnp.einsum('bgrqk,bkgd->bqgrd', p.astype(vb.dtype), vb)

    out = lax.map(step, (jnp.arange(n_blk), qb_all))
    return out.transpose(1, 0, 2, 3, 4, 5).reshape(bsz, seq, N_KV_HEADS, Q_PER_KV, HEAD_DIM)


def _nsa_mixer(u, kv, w_qg, q_norm, w_o, rel_bias):
    k_cmp, v_cmp, k_slc, v_slc, k_win, v_win = kv
    bsz, seq, _ = u.shape
    qg = u @ w_qg
    q = _rmsnorm(qg[..., :N_HEADS * HEAD_DIM].reshape(bsz, seq, N_KV_HEADS, Q_PER_KV, HEAD_DIM), q_norm)
    gates = jax.nn.sigmoid(qg[..., N_HEADS * HEAD_DIM:].reshape(bsz, seq, N_KV_HEADS, Q_PER_KV, N_BRANCH))
    o_cmp, p_slc = _compressed_branch(q, k_cmp, v_cmp, rel_bias)
    idx = _select_blocks(p_slc)
    o_slc = _selected_branch(q, k_slc, v_slc, idx, rel_bias)
    o_win = _window_branch(q, k_win, v_win, rel_bias)
    o = gates[..., 0:1] * o_cmp + gates[..., 1:2] * o_slc + gates[..., 2:3] * o_win
    return o.reshape(bsz, seq, D_MODEL) @ w_o


def setup_inputs(seed: int = 0) -> dict:
    key = jax.random.key(seed)
    ks = iter(jax.random.split(key, 40))

    def nrm(shape, scale):
        return jax.random.normal(next(ks), shape, jnp.float32) * scale

    def gain(shape):
        return 1.0 + nrm(shape, 0.02)

    na, nb = N_A_LAYERS, N_B_LAYERS
    g, p, ch = S5_GROUPS, S5_STATE, S5_GROUP_CH
    return {
        'x': nrm((BATCH, SEQ, D_MODEL), 1.0),
        'rel_bias': nrm((NUM_BUCKETS, N_HEADS), 0.5),
        'ffn1_norm': gain((DEPTH, D_MODEL)),
        'ffn1_w_in': nrm((DEPTH, D_MODEL, 2 * D_FF), D_MODEL ** -0.5),
        'ffn1_w_out': nrm((DEPTH, D_FF, D_MODEL), D_FF ** -0.5),
        'mix_norm': gain((DEPTH, D_MODEL)),
        'ffn2_norm': gain((DEPTH, D_MODEL)),
        'ffn2_w_in': nrm((DEPTH, D_MODEL, 2 * D_FF), D_MODEL ** -0.5),
        'ffn2_w_out': nrm((DEPTH, D_FF, D_MODEL), D_FF ** -0.5),
        's5_a_re': -0.5 + nrm((na, g, p), 0.01),
        's5_a_im': jnp.pi * jnp.arange(p, dtype=jnp.float32)[None, None, :] + nrm((na, g, p), 0.01),
        's5_log_dt': jax.random.uniform(next(ks), (na, g), jnp.float32, math.log(DT_MIN), math.log(DT_MAX)),
        's5_b_re': nrm((na, g, p, ch), (2.0 * ch) ** -0.5),
        's5_b_im': nrm((na, g, p, ch), (2.0 * ch) ** -0.5),
        's5_c_re': nrm((na, g, ch, p), (2.0 * p) ** -0.5),
        's5_c_im': nrm((na, g, ch, p), (2.0 * p) ** -0.5),
        's5_d': nrm((na, g, ch), 1.0),
        's5_w_glu': nrm((na, D_MODEL, D_MODEL), D_MODEL ** -0.5),
        'kv_norm': gain((D_MODEL,)),
        'w_kv': nrm((D_MODEL, 2 * N_BRANCH * N_KV_HEADS * HEAD_DIM), D_MODEL ** -0.5),
        'k_norm_cmp': gain((HEAD_DIM,)),
        'k_norm_slc': gain((HEAD_DIM,)),
        'k_norm_win': gain((HEAD_DIM,)),
        'cmp_pos_k': nrm((CMP_LEN, HEAD_DIM), 0.1),
        'cmp_pos_v': nrm((CMP_LEN, HEAD_DIM), 0.1),
        'cmp_k_w1': nrm((CMP_LEN, HEAD_DIM, CMP_HIDDEN), (CMP_LEN * HEAD_DIM) ** -0.5),
        'cmp_k_w2': nrm((CMP_HIDDEN, HEAD_DIM), CMP_HIDDEN ** -0.5),
        'cmp_v_w1': nrm((CMP_LEN, HEAD_DIM, CMP_HIDDEN), (CMP_LEN * HEAD_DIM) ** -0.5),
        'cmp_v_w2': nrm((CMP_HIDDEN, HEAD_DIM), CMP_HIDDEN ** -0.5),
        'w_qg': nrm((nb, D_MODEL, N_HEADS * HEAD_DIM + N_BRANCH * N_HEADS), D_MODEL ** -0.5),
        'q_norm': gain((nb, HEAD_DIM)),
        'w_o': nrm((nb, D_MODEL, D_MODEL), D_MODEL ** -0.5),
    }


def reference(x, rel_bias, ffn1_norm, ffn1_w_in, ffn1_w_out, mix_norm, ffn2_norm, ffn2_w_in, ffn2_w_out,
              s5_a_re, s5_a_im, s5_log_dt, s5_b_re, s5_b_im, s5_c_re, s5_c_im, s5_d, s5_w_glu,
              kv_norm, w_kv, k_norm_cmp, k_norm_slc, k_norm_win, cmp_pos_k, cmp_pos_v,
              cmp_k_w1, cmp_k_w2, cmp_v_w1, cmp_v_w2, w_qg, q_norm, w_o):
    h = x
    kv = None
    for layer in range(DEPTH):
        h = h + 0.5 * _swiglu(_rmsnorm(h, ffn1_norm[layer]), ffn1_w_in[layer], ffn1_w_out[layer])
        u = _rmsnorm(h, mix_norm[layer])
        if layer < N_A_LAYERS:
            a = layer
            h = h + _s5_mixer(u, s5_a_re[a], s5_a_im[a], s5_log_dt[a], s5_b_re[a], s5_b_im[a],
                              s5_c_re[a], s5_c_im[a], s5_d[a], s5_w_glu[a])
        else:
            b = layer - N_A_LAYERS
            h = h + _nsa_mixer(u, kv, w_qg[b], q_norm[b], w_o[b], rel_bias)
        h = h + 0.5 * _swiglu(_rmsnorm(h, ffn2_norm[layer]), ffn2_w_in[layer], ffn2_w_out[layer])
        if layer == N_A_LAYERS - 1:
            kv = _shared_kv(h, kv_norm, w_kv, k_norm_cmp, k_norm_slc, k_norm_win,
                            cmp_pos_k, cmp_pos_v, cmp_k_w1, cmp_k_w2, cmp_v_w1, cmp_v_w2)
    return h
```

```python
import contextlib
import math
import numpy as np
import concourse.bass as bass
import concourse.mybir as mybir
from concourse.bass_utils import run_bass_kernel_spmd

F32 = mybir.dt.float32
BF16 = mybir.dt.bfloat16
AF = mybir.ActivationFunctionType
ALU = mybir.AluOpType

ENGS = ("pe", "act", "dve", "pool", "sp")

D = 1024
DFF = 2816
NPAIR = DFF // 128
RMS_EPS = 1e-6


class LT:
    __slots__ = ("name", "w", "rs", "sem")

    def __init__(self, name=""):
        self.name = name
        self.w = None
        self.rs = {}
        self.sem = None


def lts(n, name=""):
    return [LT(name + str(i)) for i in range(n)]


class Prog:
    def __init__(self, nc):
        self.nc = nc
        self.ops = {e: [] for e in ENGS}
        self.dma_cnt = {}
        self.n_dma_sems = 0
        self.final_waits = {}

    def _deps(self, eng, reads, writes):
        deps = {}

        def add(ev):
            if ev is None:
                return
            k, v = ev
            if k == eng and eng == "pe":
                return
            if deps.get(k, -1) < v:
                deps[k] = v

        for t in reads:
            add(t.w)
        for t in writes:
            add(t.w)
            for k, v in t.rs.items():
                add((k, v))
        return deps

    def _commit(self, ev, reads, writes):
        k, v = ev
        for t in reads:
            if t.rs.get(k, -1) < v:
                t.rs[k] = v
        for t in writes:
            t.w = ev
            t.rs = {}

    def op(self, eng, fn, reads=(), writes=()):
        deps = self._deps(eng, reads, writes)
        idx = len(self.ops[eng])
        self.ops[eng].append({"fn": fn, "deps": deps, "sig": False, "dma": None})
        self._commit((eng, idx), reads, writes)

    def dma(self, eng, fn, home, reads=(), writes=()):
        deps = self._deps(eng, reads, writes)
        if home.sem is None:
            home.sem = {}
        if eng not in home.sem:
            home.sem[eng] = self.n_dma_sems
            self.n_dma_sems += 1
            self.dma_cnt[home.sem[eng]] = 0
        sid = home.sem[eng]
        self.dma_cnt[sid] += 16
        ev = (("d", sid), self.dma_cnt[sid])
        self.ops[eng].append({"fn": fn, "deps": deps, "sig": False, "dma": ev})
        self._commit(ev, reads, writes)
        return ev

    def wait_all_dma(self, eng="sp"):
        self.final_waits[eng] = dict(self.dma_cnt)

    def emit(self):
        nc = self.nc
        for e in ENGS:
            for o in self.ops[e]:
                for k, v in o["deps"].items():
                    if isinstance(k, str):
                        self.ops[k][v]["sig"] = True
        sigcount = {}
        for e in ENGS:
            c = 0
            arr = []
            for o in self.ops[e]:
                if o["sig"]:
                    c += 1
                arr.append(c)
            sigcount[e] = arr
        with contextlib.ExitStack() as st:
            esem = {e: st.enter_context(nc.semaphore("s_" + e)) for e in ENGS}
            dsem = [st.enter_context(nc.semaphore("d%d" % i)) for i in range(self.n_dma_sems)]
            block = st.enter_context(nc.Block())

            def run(e, eng):
                waited = {}
                for o in self.ops[e]:
                    for k, v in o["deps"].items():
                        if isinstance(k, str):
                            sem, val = esem[k], sigcount[k][v]
                        else:
                            sem, val = dsem[k[1]], v
                        if waited.get(k, -1) >= val:
                            continue
                        waited[k] = val
                        eng.wait_ge(sem, val)
                    inst = o["fn"](eng)
                    if o["dma"] is not None:
                        inst.then_inc(dsem[o["dma"][0][1]], 16)
                    elif o["sig"]:
                        inst.then_inc(esem[e], 1)
                if e in self.final_waits:
                    for s, v in self.final_waits[e].items():
                        eng.wait_ge(dsem[s], v)

            @block.tensor
            def _(eng):
                run("pe", eng)

            @block.scalar
            def _(eng):
                run("act", eng)

            @block.vector
            def _(eng):
                run("dve", eng)

            @block.gpsimd
            def _(eng):
                run("pool", eng)

            @block.sync
            def _(eng):
                run("sp", eng)


def _rel_bucket_np(dist):
    n = np.maximum(dist, 0)
    logv = np.log(np.maximum(n, 1).astype(np.float32) / np.float32(16)) / np.float32(math.log(8.0))
    large = np.minimum(16 + (logv * np.float32(16)).astype(np.int32), 31)
    return np.where(n < 16, n, large)


def host_consts():
    c = {}
    c["ident_f"] = np.eye(128, dtype=np.float32)
    c["ones_d"] = np.full((128, 128), 1.0 / D, dtype=np.float32)
    blk = np.zeros((128, 128), np.float32)
    blk[:64, :64] = 1.0 / 64
    blk[64:, 64:] = 1.0 / 64
    c["nsa_consts"] = np.concatenate([np.eye(128, dtype=np.float32), np.eye(128, dtype=np.float32)[::-1], blk], axis=1)
    x = np.arange(4096)
    dist = x - 2064
    bucket = _rel_bucket_np(dist)
    oha = np.zeros((2, 33, 4096), np.float32)
    for v in range(2):
        valid = (dist >= 0) & ((dist < 512) if v == 1 else True)
        oha[v, bucket[valid], x[valid]] = 1.0
        oha[v, 32, x[~valid]] = 1.0
    c["oha"] = oha
    eb = np.zeros((32, 2048), np.float32)
    k = np.arange(2048)
    eb[k // 64, k] = 1.0
    c["eblk"] = eb
    p_, T_, j_ = np.meshgrid(np.arange(128), np.arange(16), np.arange(32), indexing="ij")
    t_ = 128 * T_ + p_
    cur = t_ // 64
    forced = (j_ == 0) | (j_ == cur) | (j_ == cur - 1)
    causal = j_ * 64 <= t_
    caus01 = (causal & ~forced).astype(np.float32)
    addm = np.where(forced, 1e9, np.where(causal, 0.0, -1e9)).astype(np.float32)
    c["seltab"] = np.concatenate([caus01.reshape(128, 512), addm.reshape(128, 512)], axis=1)
    cs = np.arange(128)[:, None] * 16
    js = np.arange(32)[None, :] * 64
    ov = np.clip(np.minimum(cs + 32, js + 64) - np.maximum(cs, js), 0, None).astype(np.float32) / 32.0
    ovm = np.concatenate([np.ones((128, 1), np.float32), ov], axis=1)
    ovm[127, :] = 0.0
    c["ovm"] = ovm
    return c


def build(NSEQ=4, S=2048, parts=("ffn",)):
    NT = S // 512
    nc = bass.Bass("TRN2", target_bir_lowering=False)

    def din(name, shape, dt=F32):
        return nc.dram_tensor(name, list(shape), dt, kind="ExternalInput").ap()

    x_d = din("x", [NSEQ * S, D])
    y_d = nc.dram_tensor("y", [NSEQ * S, D], F32, kind="ExternalOutput").ap()
    w_in_d = [din("ffn1_w_in", [2, D, 2 * DFF]), din("ffn2_w_in", [2, D, 2 * DFF])]
    w_out_d = [din("ffn1_w_out", [2, DFF, D]), din("ffn2_w_out", [2, DFF, D])]
    gains_d = din("gains", [128, 7 * 8])
    ident_d = din("ident_f", [128, 128])
    s5R_d = din("s5R", [128, 5 * 512])
    s5S_d = din("s5S", [128, 3 * 32 + 2 * 512])
    s5M_d = din("s5M", [128, 2 + 8])
    wglu_d = din("s5_w_glu", [1, D, D])
    wkv_d = din("w_kv", [D, 1536])
    wqg_d = din("w_qg", [1, D, 1072])
    wo_d = din("w_o", [1, D, D])
    w1k_d = din("cmp_k_w1", [32, 64, 128])
    w1v_d = din("cmp_v_w1", [32, 64, 128])
    w2k_d = din("cmp_k_w2", [128, 64])
    w2v_d = din("cmp_v_w2", [128, 64])
    posk_d = din("posT_k", [64, 32])
    posv_d = din("posT_v", [64, 32])
    relb_d = din("rel_bias", [32, 16])
    oha_d = din("oha", [2, 33, 4096])
    nsac_d = din("nsa_consts", [128, 384])
    eblk_d = din("eblk", [32, 2048])
    seltab_d = din("seltab", [128, 1024])
    ngn_d = din("ngn", [128, 8])
    ovm_d = din("ovm", [128, 33])
    ones_d = din("ones_d", [128, 128])

    st = contextlib.ExitStack()
    with st:
        def sb(name, shape, dt):
            return st.enter_context(nc.sbuf_tensor(name, list(shape), dt))

        H = sb("H", [128, 8 * S], F32)
        XN = sb("XN", [128, 8 * S], BF16)
        ARENA = sb("ARENA", [128, 45056], BF16)
        IDF = sb("IDF", [128, 128], F32)
        ONES = sb("ONES", [128, 128], BF16)
        GAINS = sb("GAINS", [128, 56], F32)
        SQ = sb("SQ", [128, 2 * 512], BF16)
        NCB = sb("NCB", [128, 3 * 128], BF16)
        EBLK = sb("EBLK", [32, 2048], BF16)
        SELTAB = sb("SELTAB", [128, 1024], F32)
        NGN = sb("NGN", [128, 8], F32)
        RSTD = sb("RSTD", [128, 512], F32)
        EPSC = sb("EPSC", [128, 1], F32)
        AF32 = ARENA[:].bitcast(F32)
        XIN = [AF32[:, 20480 + i * 1024:20480 + (i + 1) * 1024] for i in range(2)]
        SILU = [ARENA[:, 31488 + i * 512:31488 + (i + 1) * 512] for i in range(2)]
        PS = [st.enter_context(nc.psum_tensor("PS%d" % i, [128, 512], F32)) for i in range(8)]

        H3 = H[:].rearrange("p (c t) -> p c t", c=8)
        XN3 = XN[:].rearrange("p (c t) -> p c t", c=8)

        P = Prog(nc)
        L_H = [[LT("h%d_%d" % (c, t)) for t in range(NT)] for c in range(8)]
        L_XN = [[LT("xn%d_%d" % (c, t)) for t in range(NT)] for c in range(8)]
        L_PS = lts(8, "ps")
        L_IDF, L_ONES, L_GAINS, L_RSTD, L_NCB, L_EBLK, L_SELTAB, L_NGN = [LT(n) for n in "idf ones gains rstd ncb eblk seltab ngn".split()]
        L_SQ = lts(2, "sq")
        L_XIN = lts(2, "xin")
        L_SILU = lts(2, "silu")
        L_Y = LT("ydram")

        P.dma("sp", lambda e: e.dma_start(out=IDF[:], in_=ident_d), L_IDF, writes=[L_IDF])
        P.dma("pool", lambda e: e.dma_start(out=ONES[:], in_=ones_d), L_ONES, writes=[L_ONES])
        P.dma("sp", lambda e: e.dma_start(out=GAINS[:], in_=gains_d), L_GAINS, writes=[L_GAINS])

        L_EPS = LT("eps")
        P.op("dve", lambda e: e.memset(EPSC[:], RMS_EPS), writes=[L_EPS])

        G_OFF = 0
        G3 = ARENA[:, G_OFF:G_OFF + 11 * S].rearrange("p (c t) -> p c t", c=11)
        WIN_OFF = 11 * 2048
        WIN = [ARENA[:, WIN_OFF + i * 2048: WIN_OFF + (i + 1) * 2048].rearrange("p (k n) -> p k n", k=8) for i in range(3)]
        WOUT_OFF = WIN_OFF + 3 * 2048
        WOUT = [ARENA[:, WOUT_OFF + i * 1408: WOUT_OFF + (i + 1) * 1408].rearrange("p (k n) -> p k n", k=11) for i in range(2)]
        L_G = [[LT("g%d_%d" % (c, t)) for t in range(NT)] for c in range(11)]
        L_WIN = lts(3, "win")
        L_WOUT = lts(2, "wout")

        ARENA_LTS = []
        BARD = sb("BARD", [128, 1], F32)

        def arena_barrier():
            P.op("dve", lambda e: e.memset(BARD[:], 0.0), writes=ARENA_LTS)

        def rmsnorm(gcol):
            for tt in range(NT):
                ts = slice(tt * 512, (tt + 1) * 512)
                for c in range(8):
                    b = c % 2
                    P.op("act", lambda e, c=c, ts=ts, b=b: e.activation(out=SQ[:, b * 512:(b + 1) * 512], in_=H3[:, c, ts], func=AF.Square),
                         reads=[L_H[c][tt]], writes=[L_SQ[b]])
                    P.op("pe", lambda e, c=c, b=b: e.matmul(PS[6][:, :], lhsT=ONES[:], rhs=SQ[:, b * 512:(b + 1) * 512], start=(c == 0), stop=(c == 7)),
                         reads=[L_ONES, L_SQ[b]], writes=[L_PS[6]])
                P.op("act", lambda e: e.activation(out=RSTD[:], in_=PS[6][:, :], func=AF.Sqrt, bias=EPSC[:, 0:1], scale=1.0),
                     reads=[L_PS[6], L_EPS], writes=[L_RSTD])
                P.op("dve", lambda e: e.reciprocal(RSTD[:], RSTD[:]), reads=[L_RSTD], writes=[L_RSTD])
                for c in range(8):
                    P.op("dve", lambda e, c=c, ts=ts: e.scalar_tensor_tensor(out=XN3[:, c, ts], in0=H3[:, c, ts], scalar=GAINS[:, gcol + c:gcol + c + 1],
                                                                            in1=RSTD[:], op0=ALU.mult, op1=ALU.mult),
                         reads=[L_H[c][tt], L_RSTD, L_GAINS], writes=[L_XN[c][tt]])

        ffn_ctr = [0, 0, 0]

        ARENA_LTS.extend([t for r in L_G for t in r] + L_WIN + L_WOUT + L_XIN + L_SILU)

        def ffn(which, layer):
            arena_barrier()
            w_in = w_in_d[which][layer].rearrange("(k p) n -> p k n", p=128)
            w_out = w_out_d[which][layer].rearrange("(k p) n -> p k n", p=128)

            def load_win(i):
                b = ffn_ctr[0] % 3
                ffn_ctr[0] += 1
                P.dma("pool", lambda e, b=b, i=i: e.dma_start(out=WIN[b][:, :, 0:128], in_=w_in[:, :, i * 128:(i + 1) * 128]), L_WIN[b], writes=[L_WIN[b]])
                P.dma("pool", lambda e, b=b, i=i: e.dma_start(out=WIN[b][:, :, 128:256], in_=w_in[:, :, DFF + i * 128:DFF + (i + 1) * 128]), L_WIN[b], writes=[L_WIN[b]])
                return b

            def load_wout(hh, m):
                b = ffn_ctr[1] % 2
                ffn_ctr[1] += 1
                P.dma("pool", lambda e, b=b: e.dma_start(out=WOUT[b][:, :, :], in_=w_out[:, hh * 11:(hh + 1) * 11, m * 128:(m + 1) * 128]), L_WOUT[b], writes=[L_WOUT[b]])
                return b

            for hh in range(2):
                pend = [load_win(hh * 11 + 0), load_win(hh * 11 + 1)]
                for il in range(11):
                    b = pend.pop(0)
                    if il + 2 < 11:
                        pend.append(load_win(hh * 11 + il + 2))
                    for tt in range(NT):
                        ts = slice(tt * 512, (tt + 1) * 512)
                        q = ffn_ctr[2] % 2
                        ffn_ctr[2] += 1
                        pa, pb = PS[2 * q], PS[2 * q + 1]
                        for half, pt, lp in ((0, pa, L_PS[2 * q]), (1, pb, L_PS[2 * q + 1])):
                            for k in range(8):
                                P.op("pe", lambda e, k=k, half=half, pt=pt, b=b, ts=ts: e.matmul(pt[:, :], lhsT=WIN[b][:, k, half * 128:(half + 1) * 128], rhs=XN3[:, k, ts],
                                                                                                 start=(k == 0), stop=(k == 7)),
                                     reads=[L_WIN[b], L_XN[k][tt]], writes=[lp])
                        P.op("act", lambda e, q=q, pa=pa: e.activation(out=SILU[q], in_=pa[:, :], func=AF.Silu), reads=[L_PS[2 * q]], writes=[L_SILU[q]])
                        P.op("dve", lambda e, q=q, pb=pb, il=il, ts=ts: e.tensor_tensor(out=G3[:, il, ts], in0=SILU[q], in1=pb[:, :], op=ALU.mult),
                             reads=[L_SILU[q], L_PS[2 * q + 1]], writes=[L_G[il][tt]])
                pend = [load_wout(hh, 0)]
                for m in range(8):
                    b = pend.pop(0)
                    if m + 1 < 8:
                        pend.append(load_wout(hh, m + 1))
                    for tt in range(NT):
                        ts = slice(tt * 512, (tt + 1) * 512)
                        q = ffn_ctr[2] % 2
                        ffn_ctr[2] += 1
                        po, lpo = PS[4 + q], L_PS[4 + q]
                        for k in range(11):
                            P.op("pe", lambda e, k=k, po=po, b=b, ts=ts: e.matmul(po[:, :], lhsT=WOUT[b][:, k, :], rhs=G3[:, k, ts], start=(k == 0), stop=(k == 10)),
                                 reads=[L_WOUT[b], L_G[k][tt]], writes=[lpo])
                        P.op("dve", lambda e, po=po, m=m, ts=ts: e.scalar_tensor_tensor(out=H3[:, m, ts], in0=po[:, :], scalar=0.5, in1=H3[:, m, ts], op0=ALU.mult, op1=ALU.add),
                             reads=[lpo, L_H[m][tt]], writes=[L_H[m][tt]])


        NLEV = int(round(math.log2(S)))
        XPt = [AF32[:, i * 2048:i * 2048 + S] for i in range(4)]
        L_XP = lts(4, "xp")
        XB = [ARENA[:, 16384 + i * 2048:16384 + i * 2048 + S] for i in range(2)]
        L_XB = lts(2, "xb")
        WGLU = ARENA[:, 20480:28672].rearrange("p (k n) -> p k n", k=8)
        L_WGLU = LT("wglu")
        TB0 = 28672
        BBt = [ARENA[:, TB0 + i * 1024:TB0 + (i + 1) * 1024] for i in range(2)]
        CTt = [ARENA[:, TB0 + 2048 + i * 1024:TB0 + 2048 + (i + 1) * 1024] for i in range(2)]
        FB0 = (TB0 + 4096) // 2
        ALt = [AF32[:, FB0 + i * 352:FB0 + (i + 1) * 352] for i in range(3)]
        S5M = AF32[:, FB0 + 1056:FB0 + 1066]
        S5S = AF32[:, FB0 + 1066:FB0 + 1066 + 1120]
        T5 = [AF32[:, FB0 + 2200 + i * 512:FB0 + 2200 + (i + 1) * 512] for i in range(4)]
        L_T5 = lts(4, "t5")
        L_PRE = LT("s5pre")

        def s5_scalars(F, are, aim, ldt, tm):
            def A(out, in_, func, **kw):
                P.op("act", lambda e: e.activation(out=out, in_=in_, func=func, **kw), reads=[L_PRE], writes=[L_PRE])

            def TT(out, a, b, op):
                P.op("dve", lambda e: e.tensor_tensor(out=out, in0=a, in1=b, op=op), reads=[L_PRE], writes=[L_PRE])

            def TS(out, a, s1, s2, op0, op1=None):
                if op1 is None:
                    P.op("dve", lambda e: e.tensor_scalar(out, a, s1, None, op0=op0), reads=[L_PRE], writes=[L_PRE])
                else:
                    P.op("dve", lambda e: e.tensor_scalar(out, a, s1, s2, op0=op0, op1=op1), reads=[L_PRE], writes=[L_PRE])
            dt, th, mag, sn, cs, t1, den, t2 = tm[4], tm[5], tm[6], tm[7], tm[0], tm[1], tm[2], tm[3]
            A(dt, ldt, AF.Exp)
            TT(th, aim, dt, ALU.mult)
            TT(mag, are, dt, ALU.mult)
            A(mag, mag, AF.Exp)
            I32 = mybir.dt.int32

            def wrap(out, shift):
                tf, ti = tm[1], tm[2].bitcast(I32)
                TS(out, th, shift, None, ALU.add)
                TS(tf, out, 1.0 / (2 * math.pi), None, ALU.mult)
                P.op("dve", lambda e: e.tensor_copy(ti, tf), reads=[L_PRE], writes=[L_PRE])
                P.op("dve", lambda e: e.tensor_copy(tf, ti), reads=[L_PRE], writes=[L_PRE])
                P.op("dve", lambda e: e.scalar_tensor_tensor(out=out, in0=tf, scalar=-2 * math.pi, in1=out, op0=ALU.mult, op1=ALU.add), reads=[L_PRE], writes=[L_PRE])
                TS(tf, out, math.pi, -2 * math.pi, ALU.is_gt, ALU.mult)
                TT(out, out, tf, ALU.add)
                TS(tf, out, -math.pi, 2 * math.pi, ALU.is_lt, ALU.mult)
                TT(out, out, tf, ALU.add)
                TS(out, out, math.pi, -math.pi, ALU.min, ALU.max)
            wrap(sn, 0.0)
            A(sn, sn, AF.Sin)
            wrap(cs, 0.5 * math.pi)
            A(cs, cs, AF.Sin)
            abr, abi = tm[0], tm[7]
            TT(abr, cs, mag, ALU.mult)
            TT(abi, sn, mag, ALU.mult)
            TT(den, are, are, ALU.mult)
            TT(t2, aim, aim, ALU.mult)
            TT(den, den, t2, ALU.add)
            P.op("dve", lambda e: e.reciprocal(den, den), reads=[L_PRE], writes=[L_PRE])
            TS(t1, abr, -1.0, None, ALU.add)
            zr, zi = tm[4], tm[5]
            TT(zr, t1, are, ALU.mult)
            TT(t2, abi, aim, ALU.mult)
            TT(zr, zr, t2, ALU.add)
            TT(zr, zr, den, ALU.mult)
            TT(zi, abi, are, ALU.mult)
            TT(t2, t1, aim, ALU.mult)
            TT(zi, zi, t2, ALU.subtract)
            TT(zi, zi, den, ALU.mult)
            return abr, abi, zr, zi

        def s5_precompute():
            def TT(out, a, b, op):
                P.op("dve", lambda e: e.tensor_tensor(out=out, in0=a, in1=b, op=op), reads=[L_PRE], writes=[L_PRE])
            RP = AF32[:, 0:2560]
            P.dma("sp", lambda e: e.dma_start(out=RP, in_=s5R_d), L_PRE, writes=[L_PRE] + L_XP)
            P.dma("sp", lambda e: e.dma_start(out=S5M, in_=s5M_d), L_PRE, writes=[L_PRE])
            P.dma("sp", lambda e: e.dma_start(out=S5S, in_=s5S_d), L_PRE, writes=[L_PRE])
            are, aim, ldt, bre, bim = [RP[:, i * 512:(i + 1) * 512] for i in range(5)]
            tm = [AF32[:, 2560 + i * 512:2560 + (i + 1) * 512] for i in range(8)]
            abr, abi, zr, zi = s5_scalars(512, are, aim, ldt, tm)
            bbr, bbi, t2 = tm[1], tm[2], tm[3]
            TT(bbr, zr, bre, ALU.mult)
            TT(t2, zi, bim, ALU.mult)
            TT(bbr, bbr, t2, ALU.subtract)
            TT(bbi, zr, bim, ALU.mult)
            TT(t2, zi, bre, ALU.mult)
            TT(bbi, bbi, t2, ALU.add)
            for ri, src in ((0, bbr), (1, bbi)):
                dst = BBt[ri].rearrange("p (k g n) -> p k g n", k=8, g=2)
                for gl in range(2):
                    P.op("dve", lambda e, dst=dst, gl=gl, src=src: e.tensor_scalar(dst[:, :, gl, :], src.rearrange("p (k n) -> p k n", k=8), S5M[:, gl:gl + 1], None, op0=ALU.mult),
                         reads=[L_PRE], writes=[L_PRE])
            sare, saim, sldt = [S5S[:, i * 32:(i + 1) * 32] for i in range(3)]
            scre = S5S[:, 96:96 + 512]
            scim = S5S[:, 96 + 512:96 + 1024]
            tm2 = [AF32[:, 2560 + 4096 + i * 32:2560 + 4096 + (i + 1) * 32] for i in range(10)]
            abr, abi, zr, zi = s5_scalars(32, sare, saim, sldt, tm2[:8])
            AL = [t.rearrange("p (a k) -> p a k", k=11) for t in ALt]
            P.op("dve", lambda e: e.tensor_copy(AL[0][:, :, 0], abr), reads=[L_PRE], writes=[L_PRE])
            P.op("dve", lambda e: e.tensor_copy(AL[1][:, :, 0], abi), reads=[L_PRE], writes=[L_PRE])
            for k in range(1, 11):
                pr, pi = AL[0][:, :, k - 1], AL[1][:, :, k - 1]
                TT(tm2[8], pr, pr, ALU.mult)
                TT(tm2[9], pi, pi, ALU.mult)
                TT(AL[0][:, :, k], tm2[8], tm2[9], ALU.subtract)
                TT(tm2[8], pr, pi, ALU.mult)
                P.op("dve", lambda e, k=k: e.tensor_scalar(AL[1][:, :, k], tm2[8], 2.0, None, op0=ALU.mult), reads=[L_PRE], writes=[L_PRE])
            P.op("dve", lambda e: e.tensor_scalar(ALt[2], ALt[1], -1.0, None, op0=ALU.mult), reads=[L_PRE], writes=[L_PRE])
            for ri, src, sc in ((0, scre, 1.0), (1, scim, -1.0)):
                dst = CTt[ri].rearrange("p (a g h) -> p a g h", a=32, g=2)
                P.op("dve", lambda e, ri=ri: e.memset(CTt[ri], 0.0), reads=[L_PRE], writes=[L_PRE])
                for gl in range(2):
                    ps_ = slice(64 * gl, 64 * gl + 64)
                    P.op("dve", lambda e, dst=dst, gl=gl, src=src, sc=sc, ps_=ps_: e.tensor_scalar(dst[ps_, :, gl, :], src.rearrange("p (a h) -> p a h", a=32)[ps_], sc, None, op0=ALU.mult),
                         reads=[L_PRE], writes=[L_PRE])

        s5_ctr = [0]

        ARENA_LTS.extend(L_XP + L_XB + [L_WGLU, L_PRE] + L_T5)

        def s5_mixer():
            arena_barrier()
            s5_precompute()
            P.dma("pool", lambda e: e.dma_start(out=WGLU[:, :, :], in_=wglu_d[0].rearrange("(k p) n -> p k n", p=128)), L_WGLU, writes=[L_WGLU])
            rmsnorm((0 * 3 + 1) * 8)
            for kc in range(8):
                for pl in range(4):
                    pair = kc * 4 + pl
                    rows = slice(32 * pl, 32 * pl + 32)
                    for tt in range(NT):
                        ts = slice(tt * 512, (tt + 1) * 512)
                        q = s5_ctr[0] % 2
                        s5_ctr[0] += 1
                        for ri in range(2):
                            pt, lp = PS[2 * q + ri], L_PS[2 * q + ri]
                            P.op("pe", lambda e, pt=pt, ri=ri, rows=rows, kc=kc, ts=ts, pl=pl: e.matmul(pt[:, :], lhsT=BBt[ri][rows, kc * 128:(kc + 1) * 128], rhs=XN3[rows, kc, ts],
                                                                                                   start=True, stop=True, tile_position=(32 * pl, 0)),
                                 reads=[L_PRE, L_XN[kc][tt]], writes=[lp])
                            P.op("act", lambda e, pt=pt, ri=ri, ts=ts: e.activation(out=XPt[ri][:, ts], in_=pt[:, :], func=AF.Copy), reads=[lp], writes=[L_XP[ri]])
                    cur = 0
                    for k in range(NLEV):
                        d = 1 << k
                        sr, si, dr, di = XPt[cur], XPt[cur + 1], XPt[2 - cur], XPt[3 - cur]
                        lsr, lsi, ldr, ldi = L_XP[cur], L_XP[cur + 1], L_XP[2 - cur], L_XP[3 - cur]
                        ar = ALt[0][:, pair * 11 + k:pair * 11 + k + 1]
                        ai = ALt[1][:, pair * 11 + k:pair * 11 + k + 1]
                        an = ALt[2][:, pair * 11 + k:pair * 11 + k + 1]
                        P.op("dve", lambda e, sr=sr, dr=dr, ar=ar, d=d: e.scalar_tensor_tensor(out=dr[:, d:], in0=sr[:, :S - d], scalar=ar, in1=sr[:, d:], op0=ALU.mult, op1=ALU.add),
                             reads=[lsr, L_PRE], writes=[ldr])
                        P.op("dve", lambda e, si=si, dr=dr, an=an, d=d: e.scalar_tensor_tensor(out=dr[:, d:], in0=si[:, :S - d], scalar=an, in1=dr[:, d:], op0=ALU.mult, op1=ALU.add),
                             reads=[lsi, L_PRE], writes=[ldr])
                        P.op("dve", lambda e, sr=sr, si=si, di=di, ai=ai, d=d: e.scalar_tensor_tensor(out=di[:, d:], in0=sr[:, :S - d], scalar=ai, in1=si[:, d:], op0=ALU.mult, op1=ALU.add),
                             reads=[lsr, lsi, L_PRE], writes=[ldi])
                        P.op("dve", lambda e, si=si, di=di, ar=ar, d=d: e.scalar_tensor_tensor(out=di[:, d:], in0=si[:, :S - d], scalar=ar, in1=di[:, d:], op0=ALU.mult, op1=ALU.add),
                             reads=[lsi, L_PRE], writes=[ldi])
                        P.op("act", lambda e, sr=sr, dr=dr, d=d: e.activation(out=dr[:, :d], in_=sr[:, :d], func=AF.Copy), reads=[lsr], writes=[ldr])
                        P.op("act", lambda e, si=si, di=di, d=d: e.activation(out=di[:, :d], in_=si[:, :d], func=AF.Copy), reads=[lsi], writes=[ldi])
                        cur = 2 - cur
                    for ri in range(2):
                        P.op("act", lambda e, ri=ri, cur=cur: e.activation(out=XB[ri], in_=XPt[cur + ri], func=AF.Copy), reads=[L_XP[cur + ri]], writes=[L_XB[ri]])
                    for tt in range(NT):
                        ts = slice(tt * 512, (tt + 1) * 512)
                        for ri in range(2):
                            P.op("pe", lambda e, tt=tt, ri=ri, pair=pair, pl=pl, ts=ts: e.matmul(PS[4 + tt][32 * pl:32 * pl + 32, :], lhsT=CTt[ri][:, pair * 32:(pair + 1) * 32], rhs=XB[ri][:, ts],
                                                                                            start=(ri == 0), stop=(ri == 1), tile_position=(0, 32 * pl)),
                                 reads=[L_PRE, L_XB[ri]], writes=[L_PS[4 + tt]])
                for tt in range(NT):
                    ts = slice(tt * 512, (tt + 1) * 512)
                    a, b = T5[0], T5[1]
                    la, lb = L_T5[0], L_T5[1]
                    P.op("dve", lambda e, kc=kc, ts=ts, tt=tt: e.scalar_tensor_tensor(out=T5[0], in0=XN3[:, kc, ts], scalar=S5M[:, 2 + kc:3 + kc], in1=PS[4 + tt][:, :], op0=ALU.mult, op1=ALU.add),
                         reads=[L_XN[kc][tt], L_PS[4 + tt], L_PRE], writes=[la])
                    P.op("act", lambda e: e.activation(out=T5[1], in_=T5[0], func=AF.Square), reads=[la], writes=[lb])
                    P.op("dve", lambda e: e.tensor_scalar(T5[1], T5[1], 0.044715, 1.0, op0=ALU.mult, op1=ALU.add), reads=[lb], writes=[lb])
                    P.op("dve", lambda e: e.tensor_tensor(out=T5[1], in0=T5[1], in1=T5[0], op=ALU.mult), reads=[la, lb], writes=[lb])
                    P.op("act", lambda e: e.activation(out=T5[1], in_=T5[1], func=AF.Sigmoid, scale=1.5957691216057308), reads=[lb], writes=[lb])
                    P.op("dve", lambda e, kc=kc, ts=ts: e.tensor_tensor(out=XN3[:, kc, ts], in0=T5[0], in1=T5[1], op=ALU.mult), reads=[la, lb], writes=[L_XN[kc][tt]])
            for m in range(8):
                for tt in range(NT):
                    ts = slice(tt * 512, (tt + 1) * 512)
                    q = s5_ctr[0] % 2
                    s5_ctr[0] += 1
                    for k in range(8):
                        P.op("pe", lambda e, q=q, k=k, m=m, ts=ts: e.matmul(PS[q][:, :], lhsT=WGLU[:, k, m * 128:(m + 1) * 128], rhs=XN3[:, k, ts], start=(k == 0), stop=(k == 7)),
                             reads=[L_WGLU, L_XN[k][tt]], writes=[L_PS[q]])
                    P.op("act", lambda e, q=q: e.activation(out=T5[2 + q], in_=PS[q][:, :], func=AF.Sigmoid), reads=[L_PS[q]], writes=[L_T5[2 + q]])
                    P.op("dve", lambda e, q=q, m=m, ts=ts: e.tensor_tensor(out=T5[2 + q], in0=T5[2 + q], in1=XN3[:, m, ts], op=ALU.mult), reads=[L_T5[2 + q], L_XN[m][tt]], writes=[L_T5[2 + q]])
                    P.op("dve", lambda e, q=q, m=m, ts=ts: e.tensor_tensor(out=H3[:, m, ts], in0=H3[:, m, ts], in1=T5[2 + q], op=ALU.mult if False else ALU.add), reads=[L_T5[2 + q], L_H[m][tt]], writes=[L_H[m][tt]])


        NKT = S // 128
        NQG = S // 512
        NCMP = S // 16 - 1
        OFFB = 2064
        IDB, JM, BLK64 = NCB[:, 0:128], NCB[:, 128:256], NCB[:, 256:384]
        KSL_d = nc.dram_tensor("KSL_d", [4, 128, S], BF16).ap()
        KWIN_d = nc.dram_tensor("KWIN_d", [4, 128, S], BF16).ap()
        VS_d = nc.dram_tensor("VS_d", [128, NKT * 8 * 65], BF16).ap()
        KC_d = nc.dram_tensor("KC_d", [128, 512], BF16).ap()
        VC_d = nc.dram_tensor("VC_d", [128, 4 * 97], BF16).ap()
        BV_d = nc.dram_tensor("BV_d", [2 * 16 * 4096], BF16).ap()
        L_KSLd, L_KWINd = lts(4, "ksld"), lts(4, "kwind")
        L_VSd, L_KCd, L_VCd, L_BVd = LT("vsd"), LT("kcd"), LT("vcd"), LT("bvd")

        def nsa_setup():
            P.dma("pool", lambda e: e.dma_start(out=NCB[:], in_=nsac_d), L_NCB, writes=[L_NCB])
            P.dma("pool", lambda e: e.dma_start(out=EBLK[:], in_=eblk_d), L_EBLK, writes=[L_EBLK])
            P.dma("sp", lambda e: e.dma_start(out=SELTAB[:], in_=seltab_d), L_SELTAB, writes=[L_SELTAB])
            P.dma("sp", lambda e: e.dma_start(out=NGN[:], in_=ngn_d), L_NGN, writes=[L_NGN])
            RBA = ARENA[0:33, 0:16]
            OHA = [ARENA[0:33, 16 + v * 4096:16 + (v + 1) * 4096] for v in range(2)]
            BVS = ARENA[0:16, 8208:8208 + 4096]
            L_RBA, L_OHA, L_BVS = LT("rba"), LT("oha"), LT("bvs")
            P.op("dve", lambda e: e.memset(ARENA[32:33, 0:16], -3750.0), writes=[L_RBA])
            P.dma("pool", lambda e: e.dma_start(out=ARENA[0:32, 0:16], in_=relb_d), L_RBA, writes=[L_RBA])
            for v in range(2):
                P.dma("pool", lambda e, v=v: e.dma_start(out=OHA[v], in_=oha_d[v]), L_OHA, writes=[L_OHA])
            for v in range(2):
                for xc in range(8):
                    P.op("pe", lambda e, v=v, xc=xc: e.matmul(PS[0][0:16, :], lhsT=RBA, rhs=OHA[v][:, xc * 512:(xc + 1) * 512], start=True, stop=True),
                         reads=[L_RBA, L_OHA], writes=[L_PS[0]])
                    P.op("act", lambda e, xc=xc: e.activation(out=BVS[:, xc * 512:(xc + 1) * 512], in_=PS[0][0:16, :], func=AF.Copy, scale=8.0),
                         reads=[L_PS[0]], writes=[L_BVS])
                P.dma("sp", lambda e, v=v: e.dma_start(out=BV_d[v * 65536:(v + 1) * 65536].rearrange("(h x) -> h x", h=16), in_=BVS), L_BVS, reads=[L_BVS], writes=[L_BVd])
            ARENA_LTS.extend([L_RBA, L_OHA, L_BVS])

        def headnorm(psrc, lpsrc, gcol, out_ap, out_lts, sq_ap, l_sq, rst_ap, l_rst, N):
            P.op("act", lambda e: e.activation(out=sq_ap, in_=psrc, func=AF.Square), reads=[lpsrc], writes=[l_sq])
            P.op("pe", lambda e: e.matmul(PS[5][:, 0:N], lhsT=BLK64, rhs=sq_ap, start=True, stop=True), reads=[L_NCB, l_sq], writes=[L_PS[5]])
            P.op("act", lambda e: e.activation(out=rst_ap, in_=PS[5][:, 0:N], func=AF.Sqrt, bias=EPSC[:, 0:1], scale=1.0), reads=[L_PS[5], L_EPS], writes=[l_rst])
            P.op("dve", lambda e: e.reciprocal(rst_ap, rst_ap), reads=[l_rst], writes=[l_rst])
            P.op("dve", lambda e: e.scalar_tensor_tensor(out=out_ap, in0=psrc, scalar=NGN[:, gcol:gcol + 1], in1=rst_ap, op0=ALU.mult, op1=ALU.mult),
                 reads=[lpsrc, l_rst, L_NGN], writes=out_lts)

        kv_ctr = [0]
        L_KV = {n: LT("kv_" + n) for n in "w1k w1v w2 pos wkvv srct vst hid kcs vcs sqk rstk pw1 tg".split()}
        L_WKS = lts(2, "wks")
        L_KST = lts(2, "kst")
        ARENA_LTS.extend(list(L_KV.values()) + L_WKS + L_KST)

        def kv_phase():
            arena_barrier()
            W1 = [ARENA[:, i * 4096:(i + 1) * 4096].rearrange("p (l h) -> p l h", l=32) for i in range(2)]
            W2K = ARENA[:, 8192:8320]
            W2V = ARENA[:, 8320:8384]
            POS = [ARENA[:, 8384 + i * 32:8384 + (i + 1) * 32] for i in range(2)]
            WKVV = ARENA[:, 8448:12544].rearrange("p (k n) -> p k n", k=8)
            WKS = [ARENA[:, 12544 + i * 1024:12544 + (i + 1) * 1024].rearrange("p (k n) -> p k n", k=8) for i in range(2)]
            SRCT = ARENA[:, 14592:14592 + S]
            KST = [ARENA[:, 16640 + i * 2048:16640 + i * 2048 + S] for i in range(2)]
            VST = ARENA[:, 20736:20736 + NKT * 520].rearrange("p (t s d) -> p t s d", t=NKT, s=8)
            HID = ARENA[:, 29056:29184]
            KCS = ARENA[:, 29312:29824]
            VCS = ARENA[:, 29824:29824 + 388].rearrange("p (g d) -> p g d", g=4)
            SQK = ARENA[:, 30224:30736]
            RSTK = AF32[:, 15400:15912]
            PW1 = AF32[:, 15912:15914]
            TG = [AF32[:, 15920 + i * 128:15920 + (i + 1) * 128] for i in range(2)]
            wkv = wkv_d.rearrange("(k p) n -> p k n", p=128)
            rmsnorm(48)
            for i, (wd, ln) in enumerate(((w1k_d, "w1k"), (w1v_d, "w1v"))):
                for hf in range(2):
                    P.dma("pool", lambda e, i=i, wd=wd, hf=hf: e.dma_start(out=W1[i][64 * hf:64 * hf + 64, :, :], in_=wd.rearrange("l d h -> d l h")), L_KV[ln], writes=[L_KV[ln]])
            for hf in range(2):
                P.dma("pool", lambda e, hf=hf: e.dma_start(out=W2K[:, 64 * hf:64 * hf + 64], in_=w2k_d), L_KV["w2"], writes=[L_KV["w2"]])
            P.dma("pool", lambda e: e.dma_start(out=W2V, in_=w2v_d), L_KV["w2"], writes=[L_KV["w2"]])
            for i, pd in enumerate((posk_d, posv_d)):
                P.dma("pool", lambda e, i=i, pd=pd: e.dma_start(out=POS[i][0:64, :], in_=pd), L_KV["pos"], writes=[L_KV["pos"]])
            P.dma("pool", lambda e: e.dma_start(out=WKVV[:, :, 0:256], in_=wkv[:, :, 768:1024]), L_KV["wkvv"], writes=[L_KV["wkvv"]])
            P.dma("pool", lambda e: e.dma_start(out=WKVV[:, :, 256:512], in_=wkv[:, :, 1280:1536]), L_KV["wkvv"], writes=[L_KV["wkvv"]])
            for i, ln in enumerate(("w1k", "w1v")):
                for l in range(32):
                    P.op("pe", lambda e, i=i, l=l: e.matmul(PS[7][:, 0:1], lhsT=W1[i][0:64, l, :], rhs=POS[i][0:64, l:l + 1], start=(l == 0), stop=(l == 31)),
                         reads=[L_KV[ln], L_KV["pos"]], writes=[L_PS[7]])
                P.op("act", lambda e, i=i: e.activation(out=PW1[:, i:i + 1], in_=PS[7][:, 0:1], func=AF.Copy), reads=[L_PS[7]], writes=[L_KV["pw1"]])
            P.op("dve", lambda e: e.memset(VST[:, :, :, 64:65], 1.0), writes=[L_KV["vst"]])
            for kt in range(NKT):
                tt = kt // 4
                q = kv_ctr[0] % 2
                kv_ctr[0] += 1
                for k in range(8):
                    P.op("pe", lambda e, q=q, k=k, kt=kt: e.matmul(PS[q][:, :], lhsT=XN3[:, k, kt * 128:(kt + 1) * 128], rhs=WKVV[:, k, :], start=(k == 0), stop=(k == 7)),
                         reads=[L_XN[k][tt], L_KV["wkvv"]], writes=[L_PS[q]])
                P.op("act", lambda e, q=q, kt=kt: e.activation(out=VST[:, kt, :, 0:64], in_=PS[q][:, :].rearrange("p (s d) -> p s d", s=8), func=AF.Copy),
                     reads=[L_PS[q]], writes=[L_KV["vst"]])
            P.dma("sp", lambda e: e.dma_start(out=VS_d, in_=ARENA[:, 20736:20736 + NKT * 520]), L_KV["vst"], reads=[L_KV["vst"]], writes=[L_VSd])

            def load_wks(col0, dup):
                b = kv_ctr[0] % 2
                kv_ctr[0] += 1
                if dup:
                    for hf in range(2):
                        P.dma("pool", lambda e, b=b, hf=hf: e.dma_start(out=WKS[b][:, :, 64 * hf:64 * hf + 64], in_=wkv[:, :, col0:col0 + 64]), L_WKS[b], writes=[L_WKS[b]])
                else:
                    P.dma("pool", lambda e, b=b: e.dma_start(out=WKS[b][:, :, :], in_=wkv[:, :, col0:col0 + 128]), L_WKS[b], writes=[L_WKS[b]])
                return b

            for slot, dst, ldst, gcol in ((2, KSL_d, L_KSLd, 1), (4, KWIN_d, L_KWINd, 2)):
                for g in range(4):
                    b = load_wks(slot * 256 + g * 64, True)
                    kb = kv_ctr[0] % 2
                    for tt in range(NT):
                        ts = slice(tt * 512, (tt + 1) * 512)
                        q = 2 + (kv_ctr[0] % 2)
                        kv_ctr[0] += 1
                        for k in range(8):
                            P.op("pe", lambda e, q=q, k=k, b=b, ts=ts: e.matmul(PS[q][:, :], lhsT=WKS[b][:, k, :], rhs=XN3[:, k, ts], start=(k == 0), stop=(k == 7)),
                                 reads=[L_WKS[b], L_XN[k][tt]], writes=[L_PS[q]])
                        headnorm(PS[q][:, :], L_PS[q], gcol, KST[kb][:, ts], [L_KST[kb]], SQK, L_KV["sqk"], RSTK, L_KV["rstk"], 512)
                    P.dma("sp", lambda e, kb=kb, dst=dst, g=g: e.dma_start(out=dst[g], in_=KST[kb]), L_KST[kb], reads=[L_KST[kb]], writes=[ldst[g]])

            P.op("dve", lambda e: e.memset(KCS, 0.0), writes=[L_KV["kcs"]])
            P.op("dve", lambda e: e.memset(ARENA[:, 29824:29824 + 388], 0.0), writes=[L_KV["vcs"]])
            for g in range(4):
                P.dma("pool", lambda e, g=g: e.dma_start(out=VCS[:, g, 64:97], in_=ovm_d), L_KV["vcs"], writes=[L_KV["vcs"]])
            for slot in range(2):
                ln = ("w1k", "w1v")[slot]
                for gp in range(2):
                    b = load_wks(slot * 256 + gp * 128, False)
                    for tt in range(NT):
                        ts = slice(tt * 512, (tt + 1) * 512)
                        q = 2 + (kv_ctr[0] % 2)
                        kv_ctr[0] += 1
                        for k in range(8):
                            P.op("pe", lambda e, q=q, k=k, b=b, ts=ts: e.matmul(PS[q][:, :], lhsT=WKS[b][:, k, :], rhs=XN3[:, k, ts], start=(k == 0), stop=(k == 7)),
                                 reads=[L_WKS[b], L_XN[k][tt]], writes=[L_PS[q]])
                        P.op("act", lambda e, q=q, ts=ts: e.activation(out=SRCT[:, ts], in_=PS[q][:, :], func=AF.Copy), reads=[L_PS[q]], writes=[L_KV["srct"]])
                    for gi in range(2):
                        g = 2 * gp + gi
                        rows = slice(64 * gi, 64 * gi + 64)
                        for l in range(32):
                            P.op("pe", lambda e, slot=slot, rows=rows, l=l: e.matmul(PS[4][:, 0:NCMP], lhsT=W1[slot][rows, l, :], rhs=SRCT[rows, l:l + 16 * (NCMP - 1) + 1:16],
                                                                                  start=(l == 0), stop=(l == 31)),
                                 reads=[L_KV[ln], L_KV["srct"]], writes=[L_PS[4]])
                        a_, b_ = TG[0][:, 0:NCMP], TG[1][:, 0:NCMP]
                        lt = L_KV["tg"]
                        P.op("act", lambda e, slot=slot, a_=a_: e.activation(out=a_, in_=PS[4][:, 0:NCMP], func=AF.Identity, bias=PW1[:, slot:slot + 1], scale=1.0),
                             reads=[L_PS[4], L_KV["pw1"]], writes=[lt])
                        P.op("act", lambda e, a_=a_, b_=b_: e.activation(out=b_, in_=a_, func=AF.Square), reads=[lt], writes=[lt])
                        P.op("dve", lambda e, b_=b_: e.tensor_scalar(b_, b_, 0.044715, 1.0, op0=ALU.mult, op1=ALU.add), reads=[lt], writes=[lt])
                        P.op("dve", lambda e, a_=a_, b_=b_: e.tensor_tensor(out=b_, in0=b_, in1=a_, op=ALU.mult), reads=[lt], writes=[lt])
                        P.op("act", lambda e, b_=b_: e.activation(out=b_, in_=b_, func=AF.Sigmoid, scale=1.5957691216057308), reads=[lt], writes=[lt])
                        P.op("dve", lambda e, a_=a_, b_=b_: e.tensor_tensor(out=HID[:, 0:NCMP], in0=a_, in1=b_, op=ALU.mult), reads=[lt], writes=[L_KV["hid"]])
                        if slot == 0:
                            P.op("pe", lambda e: e.matmul(PS[7][:, 0:NCMP], lhsT=W2K, rhs=HID[:, 0:NCMP], start=True, stop=True), reads=[L_KV["w2"], L_KV["hid"]], writes=[L_PS[7]])
                            headnorm(PS[7][:, 0:NCMP], L_PS[7], 3, KCS[:, g * 128:g * 128 + NCMP], [L_KV["kcs"]], SQK[:, 0:NCMP], L_KV["sqk"], RSTK[:, 0:NCMP], L_KV["rstk"], NCMP)
                        else:
                            P.op("pe", lambda e: e.matmul(PS[7][0:NCMP, 0:64], lhsT=HID[:, 0:NCMP], rhs=W2V, start=True, stop=True), reads=[L_KV["w2"], L_KV["hid"]], writes=[L_PS[7]])
                            P.op("act", lambda e, g=g: e.activation(out=VCS[0:NCMP, g, 0:64], in_=PS[7][0:NCMP, 0:64], func=AF.Copy), reads=[L_PS[7]], writes=[L_KV["vcs"]])
            P.dma("sp", lambda e: e.dma_start(out=KC_d, in_=KCS), L_KV["kcs"], reads=[L_KV["kcs"]], writes=[L_KCd])
            P.dma("sp", lambda e: e.dma_start(out=VC_d, in_=ARENA[:, 29824:29824 + 388]), L_KV["vcs"], reads=[L_KV["vcs"]], writes=[L_VCd])

        L_N = {n: LT("n_" + n) for n in "ksl kwin vsl vwin kc vc tcmp tsel twin selt wg sqq snb oacc pslc gates rstq sc top8 rden coef oc".split()}
        L_QT = lts(8, "qt")
        L_PT = lts(2, "pt")
        L_WQ = lts(2, "wq")
        L_WO = lts(2, "wo")
        ARENA_LTS.extend(list(L_N.values()) + L_QT + L_PT + L_WQ + L_WO)
        n_ctr = [0, 0, 0]

        def nsa_mixer():
            arena_barrier()
            QT = ARENA[:, 0:8 * S].rearrange("p (m t) -> p m t", m=8)
            KSLg = ARENA[:, 16384:16384 + S]
            KWINg = ARENA[:, 18432:18432 + S]
            VSLg = ARENA[:, 20480:20480 + NKT * 65].rearrange("p (t d) -> p t d", t=NKT)
            VWINg = ARENA[:, 21520:21520 + NKT * 65].rearrange("p (t d) -> p t d", t=NKT)
            KCg = ARENA[:, 22560:23072]
            VCg = ARENA[:, 23072:23072 + 388]
            TCMP = ARENA[:, 23464:23464 + S]
            TSEL = ARENA[:, 25512:25512 + 1152]
            TWIN = ARENA[:, 26664:26664 + 1408]
            PT = [ARENA[:, 28072 + i * 512:28072 + (i + 1) * 512] for i in range(2)]
            SELT = ARENA[0:32, 29096:29096 + S]
            WQ = [ARENA[:, 31144 + i * 1024:31144 + (i + 1) * 1024].rearrange("p (k n) -> p k n", k=8) for i in range(2)]
            WG = ARENA[:, 33192:33576].rearrange("p (k n) -> p k n", k=8)
            WO = [ARENA[:, 33576 + i * 1024:33576 + (i + 1) * 1024].rearrange("p (k n) -> p k n", k=8) for i in range(2)]
            SQQ = ARENA[:, 35624:36136]
            SNB = ARENA[:, 36136:36136 + NKT * 32]
            OACC = AF32[:, 18400:18400 + NKT * 64].rearrange("p (t d) -> p t d", t=NKT)
            PSLC = AF32[:, 19424:19424 + NKT * 32].rearrange("p (t j) -> p t j", t=NKT)
            GATES = AF32[:, 19936:19936 + NKT * 48].rearrange("p (t c) -> p t c", t=NKT)
            RSTQ = AF32[:, 20704:21216]
            SC = AF32[:, 21216:21216 + NKT * 32].rearrange("p (t j) -> p t j", t=NKT)
            TOP8 = AF32[:, 21728:21728 + NKT * 8]
            RDEN = AF32[:, 21856:21860]
            COEF = AF32[:, 21860:21864]
            OC = XN[:].rearrange("p (t c) -> p t c", t=NKT)
            L_OC = L_N["oc"]
            wqg = wqg_d[0].rearrange("(k p) n -> p k n", p=128)
            wo = wo_d[0].rearrange("(k p) n -> p k n", p=128)

            rmsnorm((1 * 3 + 1) * 8)
            P.dma("pool", lambda e: e.dma_start(out=WG[:, :, :], in_=wqg[:, :, 1024:1072]), L_N["wg"], writes=[L_N["wg"]])

            def load_wq(m):
                b = n_ctr[0] % 2
                n_ctr[0] += 1
                P.dma("pool", lambda e, b=b, m=m: e.dma_start(out=WQ[b][:, :, :], in_=wqg[:, :, m * 128:(m + 1) * 128]), L_WQ[b], writes=[L_WQ[b]])
                return b
            pend = [load_wq(0)]
            for m in range(8):
                b = pend.pop(0)
                if m + 1 < 8:
                    pend.append(load_wq(m + 1))
                for tt in range(NT):
                    ts = slice(tt * 512, (tt + 1) * 512)
                    q = n_ctr[1] % 2
                    n_ctr[1] += 1
                    for k in range(8):
                        P.op("pe", lambda e, q=q, k=k, b=b, ts=ts: e.matmul(PS[q][:, :], lhsT=WQ[b][:, k, :], rhs=XN3[:, k, ts], start=(k == 0), stop=(k == 7)),
                             reads=[L_WQ[b], L_XN[k][tt]], writes=[L_PS[q]])
                    headnorm(PS[q][:, :], L_PS[q], 0, QT[:, m, ts], [L_QT[m]], SQQ, L_N["sqq"], RSTQ, L_N["rstq"], 512)
            for T in range(NKT):
                tt = T // 4
                for k in range(8):
                    P.op("pe", lambda e, k=k, T=T: e.matmul(PS[4][:, 0:48], lhsT=XN3[:, k, T * 128:(T + 1) * 128], rhs=WG[:, k, :], start=(k == 0), stop=(k == 7)),
                         reads=[L_XN[k][tt], L_N["wg"]], writes=[L_PS[4]])
                P.op("act", lambda e, T=T: e.activation(out=GATES[:, T, :], in_=PS[4][:, 0:48], func=AF.Sigmoid), reads=[L_PS[4]], writes=[L_N["gates"]])
            P.op("dve", lambda e: e.memset(BARD[:], 0.0), writes=[t for r in L_XN for t in r] + [L_OC])
            P.dma("sp", lambda e: e.dma_start(out=KCg, in_=KC_d), L_N["kc"], reads=[L_KCd], writes=[L_N["kc"]])
            P.dma("sp", lambda e: e.dma_start(out=VCg, in_=VC_d), L_N["vc"], reads=[L_VCd], writes=[L_N["vc"]])
            VSd4 = VS_d.rearrange("p (t s d) -> p t s d", t=NKT, s=8)

            def sbank():
                q = n_ctr[1] % 2
                n_ctr[1] += 1
                return q

            def obank():
                q = 2 + n_ctr[2] % 2
                n_ctr[2] += 1
                return q

            def finalize(po, lpo, W, qg, h, br, mode):
                pv = po[:, 0:4 * W].rearrange("p (t w) -> p t w", t=4)
                T0 = 4 * qg
                P.op("dve", lambda e: e.tensor_scalar(RDEN, pv[:, :, 64], 1e-30, None, op0=ALU.max), reads=[lpo], writes=[L_N["rden"]])
                P.op("dve", lambda e: e.reciprocal(RDEN, RDEN), reads=[L_N["rden"]], writes=[L_N["rden"]])
                P.op("dve", lambda e: e.tensor_tensor(out=COEF, in0=RDEN, in1=GATES[:, T0:T0 + 4, h * 3 + br], op=ALU.mult), reads=[L_N["rden"], L_N["gates"]], writes=[L_N["coef"]])
                for qt in range(4):
                    T = T0 + qt
                    if mode == "oc":
                        P.op("dve", lambda e, qt=qt, T=T: e.tensor_scalar(OC[:, T, h * 64:(h + 1) * 64], pv[:, qt, 0:64], COEF[:, qt:qt + 1], None, op0=ALU.mult),
                             reads=[lpo, L_N["coef"]], writes=[L_OC])
                    elif mode == "set":
                        P.op("dve", lambda e, qt=qt, T=T: e.tensor_scalar(OACC[:, T, :], pv[:, qt, 0:64], COEF[:, qt:qt + 1], None, op0=ALU.mult),
                             reads=[lpo, L_N["coef"]], writes=[L_N["oacc"]])
                    else:
                        P.op("dve", lambda e, qt=qt, T=T: e.scalar_tensor_tensor(out=OACC[:, T, :], in0=pv[:, qt, 0:64], scalar=COEF[:, qt:qt + 1], in1=OACC[:, T, :], op0=ALU.mult, op1=ALU.add),
                             reads=[lpo, L_N["coef"], L_N["oacc"]], writes=[L_N["oacc"]])
                return pv

            for g in range(4):
                P.dma("sp", lambda e, g=g: e.dma_start(out=KSLg, in_=KSL_d[g]), L_N["ksl"], reads=[L_KSLd[g]], writes=[L_N["ksl"]])
                P.dma("sp", lambda e, g=g: e.dma_start(out=KWINg, in_=KWIN_d[g]), L_N["kwin"], reads=[L_KWINd[g]], writes=[L_N["kwin"]])
                P.dma("sp", lambda e, g=g: e.dma_start(out=VSLg, in_=VSd4[:, :, g, :]), L_N["vsl"], reads=[L_VSd], writes=[L_N["vsl"]])
                P.dma("sp", lambda e, g=g: e.dma_start(out=VWINg, in_=VSd4[:, :, 4 + g, :]), L_N["vwin"], reads=[L_VSd], writes=[L_N["vwin"]])
                for r in range(4):
                    h = 4 * g + r
                    m, rows = h // 2, slice(64 * (h % 2), 64 * (h % 2) + 64)
                    P.dma("sp", lambda e, h=h: e.dma_start(out=TCMP, in_=bass.AP(BV_d.tensor, h * 4096 + OFFB - 2063, [[16, 128], [1, S]])), L_N["tcmp"], reads=[L_BVd], writes=[L_N["tcmp"]])
                    for qg in range(NQG):
                        qs = slice(qg * 512, (qg + 1) * 512)
                        sq_, oq = sbank(), obank()
                        P.op("pe", lambda e, sq_=sq_, rows=rows, m=m, qs=qs, g=g: e.matmul(PS[sq_][:, :], lhsT=KCg[rows, g * 128:(g + 1) * 128], rhs=QT[rows, m, qs], start=True, stop=False),
                             reads=[L_N["kc"], L_QT[m]], writes=[L_PS[sq_]])
                        P.op("pe", lambda e, sq_=sq_, qs=qs: e.matmul(PS[sq_][:, :], lhsT=JM, rhs=TCMP[:, qs], start=False, stop=True), reads=[L_NCB, L_N["tcmp"]], writes=[L_PS[sq_]])
                        P.op("act", lambda e, sq_=sq_: e.activation(out=PT[sq_], in_=PS[sq_][:, :], func=AF.Exp, scale=0.125), reads=[L_PS[sq_]], writes=[L_PT[sq_]])
                        for qt in range(4):
                            P.op("pe", lambda e, oq=oq, sq_=sq_, qt=qt, g=g: e.matmul(PS[oq][:, qt * 97:(qt + 1) * 97], lhsT=PT[sq_][:, qt * 128:(qt + 1) * 128], rhs=VCg[:, g * 97:(g + 1) * 97], start=True, stop=True),
                                 reads=[L_PT[sq_], L_N["vc"]], writes=[L_PS[oq]])
                        pv = finalize(PS[oq], L_PS[oq], 97, qg, h, 0, "oc")
                        for qt in range(4):
                            T = 4 * qg + qt
                            if r == 0:
                                P.op("dve", lambda e, pv=pv, qt=qt, T=T: e.tensor_scalar(PSLC[:, T, :], pv[:, qt, 65:97], RDEN[:, qt:qt + 1], None, op0=ALU.mult),
                                     reads=[L_PS[oq], L_N["rden"]], writes=[L_N["pslc"]])
                            else:
                                P.op("dve", lambda e, pv=pv, qt=qt, T=T: e.scalar_tensor_tensor(out=PSLC[:, T, :], in0=pv[:, qt, 65:97], scalar=RDEN[:, qt:qt + 1], in1=PSLC[:, T, :], op0=ALU.mult, op1=ALU.add),
                                     reads=[L_PS[oq], L_N["rden"], L_N["pslc"]], writes=[L_N["pslc"]])
                SCf = AF32[:, 21216:21216 + NKT * 32]
                PSLCf = AF32[:, 19424:19424 + NKT * 32]
                P.op("dve", lambda e: e.tensor_tensor(out=SCf, in0=PSLCf, in1=SELTAB[:, 0:NKT * 32], op=ALU.mult), reads=[L_N["pslc"], L_SELTAB], writes=[L_N["sc"]])
                P.op("dve", lambda e: e.tensor_tensor(out=SCf, in0=SCf, in1=SELTAB[:, 512:512 + NKT * 32], op=ALU.add), reads=[L_N["sc"], L_SELTAB], writes=[L_N["sc"]])
                for T in range(NKT):
                    P.op("dve", lambda e, T=T: e.max(TOP8[:, T * 8:(T + 1) * 8], SC[:, T, :]), reads=[L_N["sc"]], writes=[L_N["top8"]])
                for T in range(NKT):
                    P.op("dve", lambda e, T=T: e.tensor_scalar(SC[:, T, :], SC[:, T, :], TOP8[:, T * 8 + 7:T * 8 + 8], None, op0=ALU.is_ge), reads=[L_N["sc"], L_N["top8"]], writes=[L_N["sc"]])
                P.op("dve", lambda e: e.tensor_scalar(SNB, SCf, -1.0, 30000.0, op0=ALU.add, op1=ALU.mult), reads=[L_N["sc"]], writes=[L_N["snb"]])
                PSB = PS[6][:, :].bitcast(BF16)
                for T4 in range(NKT // 4):
                    for ti in range(4):
                        T = T4 * 4 + ti
                        P.op("pe", lambda e, T=T, ti=ti: e.transpose(PSB[0:32, ti * 128:(ti + 1) * 128], SNB[:, T * 32:(T + 1) * 32], IDB), reads=[L_N["snb"], L_NCB], writes=[L_PS[6]])
                    P.op("act", lambda e, T4=T4: e.activation(out=SELT[:, T4 * 512:(T4 + 1) * 512], in_=PSB[0:32, 0:512], func=AF.Copy), reads=[L_PS[6]], writes=[L_N["selt"]])
                for r in range(4):
                    h = 4 * g + r
                    m, rows = h // 2, slice(64 * (h % 2), 64 * (h % 2) + 64)
                    P.dma("sp", lambda e, h=h: e.dma_start(out=TSEL, in_=bass.AP(BV_d.tensor, h * 4096 + OFFB - 511, [[1, 128], [1, 1152]])), L_N["tsel"], reads=[L_BVd], writes=[L_N["tsel"]])
                    P.dma("sp", lambda e, h=h: e.dma_start(out=TWIN, in_=bass.AP(BV_d.tensor, (16 + h) * 4096 + OFFB - 511, [[1, 128], [1, 1408]])), L_N["twin"], reads=[L_BVd], writes=[L_N["twin"]])
                    for br, Kg, lK, Vg, lV, TB, lT in ((1, KSLg, L_N["ksl"], VSLg, L_N["vsl"], TSEL, L_N["tsel"]), (2, KWINg, L_N["kwin"], VWINg, L_N["vwin"], TWIN, L_N["twin"])):
                        for qg in range(NQG):
                            qs = slice(qg * 512, (qg + 1) * 512)
                            oq = obank()
                            kt_lo = 0 if br == 1 else max(0, 4 * qg - 4)
                            bank_used = [False]
                            for kt in range(kt_lo, 4 * qg + 4):
                                dl = 4 * qg - kt
                                col0 = 128 * ((min(dl, 2) if br == 1 else dl) + 3)
                                sq_ = sbank()
                                ks = slice(kt * 128, (kt + 1) * 128)
                                P.op("pe", lambda e, sq_=sq_, rows=rows, m=m, qs=qs, ks=ks, Kg=Kg: e.matmul(PS[sq_][:, :], lhsT=Kg[rows, ks], rhs=QT[rows, m, qs], start=True, stop=False),
                                     reads=[lK, L_QT[m]], writes=[L_PS[sq_]])
                                P.op("pe", lambda e, sq_=sq_, col0=col0, TB=TB, br=br: e.matmul(PS[sq_][:, :], lhsT=JM, rhs=TB[:, col0:col0 + 512], start=False, stop=(br == 2)),
                                     reads=[L_NCB, lT], writes=[L_PS[sq_]])
                                if br == 1:
                                    P.op("pe", lambda e, sq_=sq_, ks=ks, qs=qs: e.matmul(PS[sq_][:, :], lhsT=EBLK[0:32, ks], rhs=SELT[:, qs], start=False, stop=True),
                                         reads=[L_EBLK, L_N["selt"]], writes=[L_PS[sq_]])
                                P.op("act", lambda e, sq_=sq_: e.activation(out=PT[sq_], in_=PS[sq_][:, :], func=AF.Exp, scale=0.125), reads=[L_PS[sq_]], writes=[L_PT[sq_]])
                                for qt in range(4):
                                    T = 4 * qg + qt
                                    if T < kt or (br == 2 and T > kt + 4):
                                        continue
                                    first = not bank_used[0]
                                    bank_used[0] = True
                                    P.op("pe", lambda e, oq=oq, sq_=sq_, qt=qt, kt=kt, Vg=Vg, first=first, T=T: e.matmul(PS[oq][:, qt * 65:(qt + 1) * 65], lhsT=PT[sq_][:, qt * 128:(qt + 1) * 128], rhs=Vg[:, kt, :],
                                                                                                              start=first, stop=(kt == T), skip_group_check=True),
                                         reads=[L_PT[sq_], lV], writes=[L_PS[oq]])
                            finalize(PS[oq], L_PS[oq], 65, qg, h, br, "set" if br == 1 else "add")
                    for qg in range(NQG):
                        T0 = 4 * qg
                        P.op("dve", lambda e, T0=T0, h=h: e.tensor_tensor(out=OC[:, T0:T0 + 4, h * 64:(h + 1) * 64], in0=OACC[:, T0:T0 + 4, :], in1=OC[:, T0:T0 + 4, h * 64:(h + 1) * 64], op=ALU.add),
                             reads=[L_N["oacc"], L_OC], writes=[L_OC])
            OT = QT
            PSB = PS[6][:, :].bitcast(BF16)
            PSB2 = PS[7][:, :].bitcast(BF16)
            for T in range(NKT):
                for mg in range(2):
                    pb, lpb = (PSB, L_PS[6]) if mg == 0 else (PSB2, L_PS[7])
                    for mi in range(4):
                        mm_ = mg * 4 + mi
                        P.op("pe", lambda e, pb=pb, mi=mi, T=T, mm_=mm_: e.transpose(pb[:, mi * 128:(mi + 1) * 128], OC[:, T, mm_ * 128:(mm_ + 1) * 128], IDB), reads=[L_OC, L_NCB], writes=[lpb])
                    eng = "act" if mg == 0 else "dve"
                    if mg == 0:
                        P.op("act", lambda e, pb=pb, T=T, mg=mg: e.activation(out=OT[:, mg * 4:(mg + 1) * 4, T * 128:(T + 1) * 128], in_=pb[:, 0:512].rearrange("p (m t) -> p m t", m=4), func=AF.Copy),
                             reads=[lpb], writes=L_QT[mg * 4:(mg + 1) * 4])
                    else:
                        P.op("dve", lambda e, pb=pb, T=T, mg=mg: e.tensor_copy(OT[:, mg * 4:(mg + 1) * 4, T * 128:(T + 1) * 128], pb[:, 0:512].rearrange("p (m t) -> p m t", m=4)),
                             reads=[lpb], writes=L_QT[mg * 4:(mg + 1) * 4])

            def load_wo(mo):
                b = n_ctr[0] % 2
                n_ctr[0] += 1
                P.dma("pool", lambda e, b=b, mo=mo: e.dma_start(out=WO[b][:, :, :], in_=wo[:, :, mo * 128:(mo + 1) * 128]), L_WO[b], writes=[L_WO[b]])
                return b
            pend = [load_wo(0)]
            for mo in range(8):
                b = pend.pop(0)
                if mo + 1 < 8:
                    pend.append(load_wo(mo + 1))
                for tt in range(NT):
                    ts = slice(tt * 512, (tt + 1) * 512)
                    q = sbank()
                    for k in range(8):
                        P.op("pe", lambda e, q=q, k=k, b=b, ts=ts: e.matmul(PS[q][:, :], lhsT=WO[b][:, k, :], rhs=OT[:, k, ts], start=(k == 0), stop=(k == 7)),
                             reads=[L_WO[b], L_QT[k]], writes=[L_PS[q]])
                    P.op("dve", lambda e, q=q, mo=mo, ts=ts: e.tensor_tensor(out=H3[:, mo, ts], in0=H3[:, mo, ts], in1=PS[q][:, :], op=ALU.add),
                         reads=[L_PS[q], L_H[mo][tt]], writes=[L_H[mo][tt]])
            P.op("dve", lambda e: e.memset(BARD[:], 0.0), writes=[t for r in L_XN for t in r] + [L_OC])

        if "nsa" in parts:
            nsa_setup()
        for s in range(NSEQ):
            arena_barrier()
            for t128 in range(S // 128):
                b = t128 % 2
                tt = t128 // 4
                row0 = s * S + t128 * 128
                P.dma("sp", lambda e, b=b, row0=row0: e.dma_start(out=XIN[b], in_=x_d[row0:row0 + 128, :]), L_XIN[b], writes=[L_XIN[b]])
                for cg in range(2):
                    pq = 6 + cg
                    for ci in range(4):
                        c = cg * 4 + ci
                        P.op("pe", lambda e, b=b, c=c, ci=ci, pq=pq: e.transpose(PS[pq][:, ci * 128:(ci + 1) * 128], XIN[b][:, c * 128:(c + 1) * 128], IDF[:]),
                             reads=[L_XIN[b], L_IDF], writes=[L_PS[pq]])
                    P.op("act" if cg == 0 else "dve",
                         (lambda e, cg=cg, pq=pq, t128=t128: e.activation(out=H3[:, cg * 4:(cg + 1) * 4, t128 * 128:(t128 + 1) * 128], in_=PS[pq][:, :].rearrange("p (c t) -> p c t", c=4), func=AF.Copy)) if cg == 0 else
                         (lambda e, cg=cg, pq=pq, t128=t128: e.tensor_copy(H3[:, cg * 4:(cg + 1) * 4, t128 * 128:(t128 + 1) * 128], PS[pq][:, :].rearrange("p (c t) -> p c t", c=4))),
                         reads=[L_PS[pq]], writes=[L_H[cg * 4 + ci][tt] for ci in range(4)])

            for layer in range(2):
                rmsnorm((layer * 3 + 0) * 8)
                ffn(0, layer)
                if layer == 0 and "s5" in parts:
                    s5_mixer()
                if layer == 1 and "nsa" in parts:
                    nsa_mixer()
                rmsnorm((layer * 3 + 2) * 8)
                ffn(1, layer)
                if layer == 0 and "nsa" in parts:
                    kv_phase()

            arena_barrier()
            for t128 in range(S // 128):
                b = t128 % 2
                tt = t128 // 4
                row0 = s * S + t128 * 128
                for cg in range(2):
                    pq = 6 + cg
                    for ci in range(4):
                        c = cg * 4 + ci
                        P.op("pe", lambda e, c=c, ci=ci, pq=pq, t128=t128: e.transpose(PS[pq][:, ci * 128:(ci + 1) * 128], H3[:, c, t128 * 128:(t128 + 1) * 128], IDF[:]),
                             reads=[L_H[c][tt], L_IDF], writes=[L_PS[pq]])
                    P.op("act" if cg == 0 else "dve",
                         (lambda e, b=b, cg=cg, pq=pq: e.activation(out=XIN[b][:, cg * 512:(cg + 1) * 512], in_=PS[pq][:, :], func=AF.Copy)) if cg == 0 else
                         (lambda e, b=b, cg=cg, pq=pq: e.tensor_copy(XIN[b][:, cg * 512:(cg + 1) * 512], PS[pq][:, :])),
                         reads=[L_PS[pq]], writes=[L_XIN[b]])
                P.dma("sp", lambda e, b=b, row0=row0: e.dma_start(out=y_d[row0:row0 + 128, :], in_=XIN[b]), L_XIN[b], reads=[L_XIN[b]], writes=[L_Y])

        P.wait_all_dma("sp")
        P.emit()
    return nc


def prep_inputs(inp):
    g = np.stack([inp["ffn1_norm"], inp["mix_norm"], inp["ffn2_norm"]], axis=1)
    gains = np.ascontiguousarray(g.reshape(2, 3, 8, 128).transpose(3, 0, 1, 2).reshape(128, 48)).astype(np.float32)
    gains = np.ascontiguousarray(np.concatenate([gains, inp["kv_norm"].reshape(8, 128).T], axis=1)).astype(np.float32)
    m = dict(host_consts())
    m["gains"] = gains
    a_re, a_im, ldt = inp["s5_a_re"][0], inp["s5_a_im"][0], inp["s5_log_dt"][0]
    b_re, b_im, c_re, c_im = inp["s5_b_re"][0], inp["s5_b_im"][0], inp["s5_c_re"][0], inp["s5_c_im"][0]
    pl, gl, hp, kc, nn = np.meshgrid(np.arange(4), np.arange(2), np.arange(16), np.arange(8), np.arange(64), indexing="ij")
    gg = 8 * kc + 2 * pl + gl
    R = np.stack([a_re[gg, nn], a_im[gg, nn], ldt[gg], b_re[gg, nn, hp], b_im[gg, nn, hp]], axis=0)
    m["s5R"] = np.ascontiguousarray(R.reshape(5, 128, 512).transpose(1, 0, 2).reshape(128, 2560)).astype(np.float32)
    gl2, n2, pr2 = np.meshgrid(np.arange(2), np.arange(64), np.arange(32), indexing="ij")
    g2 = 2 * pr2 + gl2
    S3 = np.stack([a_re[g2, n2], a_im[g2, n2], ldt[g2]], axis=0).reshape(3, 128, 32).transpose(1, 0, 2).reshape(128, 96)
    gl3, n3, pr3, h3 = np.meshgrid(np.arange(2), np.arange(64), np.arange(32), np.arange(16), indexing="ij")
    g3 = 2 * pr3 + gl3
    C2 = np.stack([c_re[g3, h3, n3], c_im[g3, h3, n3]], axis=0).reshape(2, 128, 512).transpose(1, 0, 2).reshape(128, 1024)
    m["s5S"] = np.ascontiguousarray(np.concatenate([S3, C2], axis=1)).astype(np.float32)
    p = np.arange(128)
    maskR = np.stack([((p // 16) % 2 == 0), ((p // 16) % 2 == 1)], axis=1).astype(np.float32)
    dsk = inp["s5_d"][0].reshape(8, 128).T
    m["s5M"] = np.ascontiguousarray(np.concatenate([maskR, dsk], axis=1)).astype(np.float32)
    m["s5_w_glu"] = np.ascontiguousarray(inp["s5_w_glu"], dtype=np.float32)
    for k in ("w_kv", "w_qg", "w_o", "cmp_k_w1", "cmp_v_w1", "cmp_k_w2", "cmp_v_w2", "rel_bias"):
        m[k] = np.ascontiguousarray(inp[k], dtype=np.float32)
    m["posT_k"] = np.ascontiguousarray(inp["cmp_pos_k"].T).astype(np.float32)
    m["posT_v"] = np.ascontiguousarray(inp["cmp_pos_v"].T).astype(np.float32)
    dup = lambda v: np.concatenate([v, v])
    ngn = np.zeros((128, 8), np.float32)
    ngn[:, 0] = dup(inp["q_norm"][0])
    ngn[:, 1] = dup(inp["k_norm_slc"])
    ngn[:, 2] = dup(inp["k_norm_win"])
    ngn[:, 3] = dup(inp["k_norm_cmp"])
    m["ngn"] = ngn
    for k in ("ffn1_w_in", "ffn2_w_in", "ffn1_w_out", "ffn2_w_out"):
        m[k] = np.ascontiguousarray(inp[k], dtype=np.float32)
    return m


_NC_CACHE = {}


def kernel(**inp):
    x = np.ascontiguousarray(inp["x"], dtype=np.float32)
    B, S, Dm = x.shape
    n = 8
    per = B // n
    shared = prep_inputs(inp)
    key = (per, S)
    if key not in _NC_CACHE:
        _NC_CACHE[key] = build(NSEQ=per, S=S, parts=("ffn", "s5", "nsa"))
    nc = _NC_CACHE[key]
    in_maps = []
    for i in range(n):
        m = dict(shared)
        m["x"] = x[i * per:(i + 1) * per].reshape(per * S, Dm)
        in_maps.append(m)
    res = run_bass_kernel_spmd(nc, in_maps, core_ids=list(range(n)))
    out = np.concatenate([r["y"].reshape(per, S, Dm) for r in res.results], axis=0)
    return out.astype(np.float32)
```

```python
import contextlib
import math
import numpy as np
import concourse.bass as bass
import concourse.mybir as mybir
from concourse.bass_utils import run_bass_kernel_spmd

F32 = mybir.dt.float32
BF16 = mybir.dt.bfloat16
AF = mybir.ActivationFunctionType
ALU = mybir.AluOpType

ENGS = ("pe", "act", "dve", "pool", "sp")

D = 1024
DFF = 2816
NPAIR = DFF // 128
RMS_EPS = 1e-6


class LT:
    __slots__ = ("name", "w", "rs", "sem")

    def __init__(self, name=""):
        self.name = name
        self.w = None
        self.rs = {}
        self.sem = None


def lts(n, name=""):
    return [LT(name + str(i)) for i in range(n)]


class Prog:
    def __init__(self, nc):
        self.nc = nc
        self.ops = {e: [] for e in ENGS}
        self.dma_cnt = {}
        self.n_dma_sems = 0
        self.final_waits = {}

    def _deps(self, eng, reads, writes):
        deps = {}

        def add(ev):
            if ev is None:
                return
            k, v = ev
            if k == eng and eng == "pe":
                return
            if deps.get(k, -1) < v:
                deps[k] = v

        for t in reads:
            add(t.w)
        for t in writes:
            add(t.w)
            for k, v in t.rs.items():
                add((k, v))
        return deps

    def _commit(self, ev, reads, writes):
        k, v = ev
        for t in reads:
            if t.rs.get(k, -1) < v:
                t.rs[k] = v
        for t in writes:
            t.w = ev
            t.rs = {}

    def op(self, eng, fn, reads=(), writes=()):
        deps = self._deps(eng, reads, writes)
        idx = len(self.ops[eng])
        self.ops[eng].append({"fn": fn, "deps": deps, "sig": False, "dma": None})
        self._commit((eng, idx), reads, writes)

    def dma(self, eng, fn, home, reads=(), writes=()):
        deps = self._deps(eng, reads, writes)
        if home.sem is None:
            home.sem = {}
        if eng not in home.sem:
            home.sem[eng] = self.n_dma_sems
            self.n_dma_sems += 1
            self.dma_cnt[home.sem[eng]] = 0
        sid = home.sem[eng]
        self.dma_cnt[sid] += 16
        ev = (("d", sid), self.dma_cnt[sid])
        self.ops[eng].append({"fn": fn, "deps": deps, "sig": False, "dma": ev})
        self._commit(ev, reads, writes)
        return ev

    def wait_all_dma(self, eng="sp"):
        self.final_waits[eng] = dict(self.dma_cnt)

    def emit(self):
        nc = self.nc
        for e in ENGS:
            for o in self.ops[e]:
                for k, v in o["deps"].items():
                    if isinstance(k, str):
                        self.ops[k][v]["sig"] = True
        sigcount = {}
        for e in ENGS:
            c = 0
            arr = []
            for o in self.ops[e]:
                if o["sig"]:
                    c += 1
                arr.append(c)
            sigcount[e] = arr
        with contextlib.ExitStack() as st:
            esem = {e: st.enter_context(nc.semaphore("s_" + e)) for e in ENGS}
            dsem = [st.enter_context(nc.semaphore("d%d" % i)) for i in range(self.n_dma_sems)]
            block = st.enter_context(nc.Block())

            def run(e, eng):
                waited = {}
                for o in self.ops[e]:
                    for k, v in o["deps"].items():
                        if isinstance(k, str):
                            sem, val = esem[k], sigcount[k][v]
                        else:
                            sem, val = dsem[k[1]], v
                        if waited.get(k, -1) >= val:
                            continue
                        waited[k] = val
                        eng.wait_ge(sem, val)
                    inst = o["fn"](eng)
                    if o["dma"] is not None:
                        inst.then_inc(dsem[o["dma"][0][1]], 16)
                    elif o["sig"]:
                        inst.then_inc(esem[e], 1)
                if e in self.final_waits:
                    for s, v in self.final_waits[e].items():
                        eng.wait_ge(dsem[s], v)

            @block.tensor
            def _(eng):
                run("pe", eng)

            @block.scalar
            def _(eng):
                run("act", eng)

            @block.vector
            def _(eng):
                run("dve", eng)

            @block.gpsimd
            def _(eng):
                run("pool", eng)

            @block.sync
            def _(eng):
                run("sp", eng)


def _rel_bucket_np(dist):
    n = np.maximum(dist, 0)
    logv = np.log(np.maximum(n, 1).astype(np.float32) / np.float32(16)) / np.float32(math.log(8.0))
    large = np.minimum(16 + (logv * np.float32(16)).astype(np.int32), 31)
    return np.where(n < 16, n, large)


def host_consts():
    c = {}
    c["ident_f"] = np.eye(128, dtype=np.float32)
    c["ones_d"] = np.full((128, 128), 1.0 / D, dtype=np.float32)
    blk = np.zeros((128, 128), np.float32)
    blk[:64, :64] = 1.0 / 64
    blk[64:, 64:] = 1.0 / 64
    c["nsa_consts"] = np.concatenate([np.eye(128, dtype=np.float32), np.eye(128, dtype=np.float32)[::-1], blk], axis=1)
    x = np.arange(4096)
    dist = x - 2064
    bucket = _rel_bucket_np(dist)
    oha = np.zeros((2, 33, 4096), np.float32)
    for v in range(2):
        valid = (dist >= 0) & ((dist < 512) if v == 1 else True)
        oha[v, bucket[valid], x[valid]] = 1.0
        oha[v, 32, x[~valid]] = 1.0
    c["oha"] = oha
    eb = np.zeros((32, 2048), np.float32)
    k = np.arange(2048)
    eb[k // 64, k] = 1.0
    c["eblk"] = eb
    p_, T_, j_ = np.meshgrid(np.arange(128), np.arange(16), np.arange(32), indexing="ij")
    t_ = 128 * T_ + p_
    cur = t_ // 64
    forced = (j_ == 0) | (j_ == cur) | (j_ == cur - 1)
    causal = j_ * 64 <= t_
    caus01 = (causal & ~forced).astype(np.float32)
    addm = np.where(forced, 1e9, np.where(causal, 0.0, -1e9)).astype(np.float32)
    c["seltab"] = np.concatenate([caus01.reshape(128, 512), addm.reshape(128, 512)], axis=1)
    cs = np.arange(128)[:, None] * 16
    js = np.arange(32)[None, :] * 64
    ov = np.clip(np.minimum(cs + 32, js + 64) - np.maximum(cs, js), 0, None).astype(np.float32) / 32.0
    ovm = np.concatenate([np.ones((128, 1), np.float32), ov], axis=1)
    ovm[127, :] = 0.0
    c["ovm"] = ovm
    return c


def build(NSEQ=4, S=2048, parts=("ffn",)):
    NT = S // 512
    nc = bass.Bass("TRN2", target_bir_lowering=False)

    def din(name, shape, dt=F32):
        return nc.dram_tensor(name, list(shape), dt, kind="ExternalInput").ap()

    x_d = din("x", [NSEQ * S, D])
    y_d = nc.dram_tensor("y", [NSEQ * S, D], F32, kind="ExternalOutput").ap()
    w_in_d = [din("ffn1_w_in", [2, D, 2 * DFF]), din("ffn2_w_in", [2, D, 2 * DFF])]
    w_out_d = [din("ffn1_w_out", [2, DFF, D]), din("ffn2_w_out", [2, DFF, D])]
    gains_d = din("gains", [128, 7 * 8])
    ident_d = din("ident_f", [128, 128])
    s5R_d = din("s5R", [128, 5 * 512])
    s5S_d = din("s5S", [128, 3 * 32 + 2 * 512])
    s5M_d = din("s5M", [128, 2 + 8])
    wglu_d = din("s5_w_glu", [1, D, D])
    wkv_d = din("w_kv", [D, 1536])
    wqg_d = din("w_qg", [1, D, 1072])
    wo_d = din("w_o", [1, D, D])
    w1k_d = din("cmp_k_w1", [32, 64, 128])
    w1v_d = din("cmp_v_w1", [32, 64, 128])
    w2k_d = din("cmp_k_w2", [128, 64])
    w2v_d = din("cmp_v_w2", [128, 64])
    posk_d = din("posT_k", [64, 32])
    posv_d = din("posT_v", [64, 32])
    relb_d = din("rel_bias", [32, 16])
    oha_d = din("oha", [2, 33, 4096])
    nsac_d = din("nsa_consts", [128, 384])
    eblk_d = din("eblk", [32, 2048])
    seltab_d = din("seltab", [128, 1024])
    ngn_d = din("ngn", [128, 8])
    ovm_d = din("ovm", [128, 33])
    ones_d = din("ones_d", [128, 128])

    st = contextlib.ExitStack()
    with st:
        def sb(name, shape, dt):
            return st.enter_context(nc.sbuf_tensor(name, list(shape), dt))

        H = sb("H", [128, 8 * S], F32)
        XN = sb("XN", [128, 8 * S], BF16)
        ARENA = sb("ARENA", [128, 45056], BF16)
        IDF = sb("IDF", [128, 128], F32)
        ONES = sb("ONES", [128, 128], BF16)
        GAINS = sb("GAINS", [128, 56], F32)
        SQ = sb("SQ", [128, 2 * 512], BF16)
        NCB = sb("NCB", [128, 3 * 128], BF16)
        EBLK = sb("EBLK", [32, 2048], BF16)
        SELTAB = sb("SELTAB", [128, 1024], F32)
        NGN = sb("NGN", [128, 8], F32)
        RSTD = sb("RSTD", [128, 512], F32)
        EPSC = sb("EPSC", [128, 1], F32)
        AF32 = ARENA[:].bitcast(F32)
        XIN = [AF32[:, 20480 + i * 1024:20480 + (i + 1) * 1024] for i in range(2)]
        SILU = [ARENA[:, 31488 + i * 512:31488 + (i + 1) * 512] for i in range(2)]
        PS = [st.enter_context(nc.psum_tensor("PS%d" % i, [128, 512], F32)) for i in range(8)]

        H3 = H[:].rearrange("p (c t) -> p c t", c=8)
        XN3 = XN[:].rearrange("p (c t) -> p c t", c=8)

        P = Prog(nc)
        L_H = [[LT("h%d_%d" % (c, t)) for t in range(NT)] for c in range(8)]
        L_XN = [[LT("xn%d_%d" % (c, t)) for t in range(NT)] for c in range(8)]
        L_PS = lts(8, "ps")
        L_IDF, L_ONES, L_GAINS, L_RSTD, L_NCB, L_EBLK, L_SELTAB, L_NGN = [LT(n) for n in "idf ones gains rstd ncb eblk seltab ngn".split()]
        L_SQ = lts(2, "sq")
        L_XIN = lts(2, "xin")
        L_SILU = lts(2, "silu")
        L_Y = LT("ydram")

        P.dma("sp", lambda e: e.dma_start(out=IDF[:], in_=ident_d), L_IDF, writes=[L_IDF])
        P.dma("pool", lambda e: e.dma_start(out=ONES[:], in_=ones_d), L_ONES, writes=[L_ONES])
        P.dma("sp", lambda e: e.dma_start(out=GAINS[:], in_=gains_d), L_GAINS, writes=[L_GAINS])

        L_EPS = LT("eps")
        P.op("dve", lambda e: e.memset(EPSC[:], RMS_EPS), writes=[L_EPS])

        G_OFF = 0
        G3 = ARENA[:, G_OFF:G_OFF + 11 * S].rearrange("p (c t) -> p c t", c=11)
        WIN_OFF = 11 * 2048
        WIN = [ARENA[:, WIN_OFF + i * 2048: WIN_OFF + (i + 1) * 2048].rearrange("p (k n) -> p k n", k=8) for i in range(3)]
        WOUT_OFF = WIN_OFF + 3 * 2048
        WOUT = [ARENA[:, WOUT_OFF + i * 1408: WOUT_OFF + (i + 1) * 1408].rearrange("p (k n) -> p k n", k=11) for i in range(2)]
        L_G = [[LT("g%d_%d" % (c, t)) for t in range(NT)] for c in range(11)]
        L_WIN = lts(3, "win")
        L_WOUT = lts(2, "wout")

        ARENA_LTS = []
        BARD = sb("BARD", [128, 1], F32)

        def arena_barrier():
            P.op("dve", lambda e: e.memset(BARD[:], 0.0), writes=ARENA_LTS)

        def rmsnorm(gcol):
            for tt in range(NT):
                ts = slice(tt * 512, (tt + 1) * 512)
                for c in range(8):
                    b = c % 2
                    P.op("act", lambda e, c=c, ts=ts, b=b: e.activation(out=SQ[:, b * 512:(b + 1) * 512], in_=H3[:, c, ts], func=AF.Square),
                         reads=[L_H[c][tt]], writes=[L_SQ[b]])
                    P.op("pe", lambda e, c=c, b=b: e.matmul(PS[6][:, :], lhsT=ONES[:], rhs=SQ[:, b * 512:(b + 1) * 512], start=(c == 0), stop=(c == 7)),
                         reads=[L_ONES, L_SQ[b]], writes=[L_PS[6]])
                P.op("act", lambda e: e.activation(out=RSTD[:], in_=PS[6][:, :], func=AF.Sqrt, bias=EPSC[:, 0:1], scale=1.0),
                     reads=[L_PS[6], L_EPS], writes=[L_RSTD])
                P.op("dve", lambda e: e.reciprocal(RSTD[:], RSTD[:]), reads=[L_RSTD], writes=[L_RSTD])
                for c in range(8):
                    P.op("dve", lambda e, c=c, ts=ts: e.scalar_tensor_tensor(out=XN3[:, c, ts], in0=H3[:, c, ts], scalar=GAINS[:, gcol + c:gcol + c + 1],
                                                                            in1=RSTD[:], op0=ALU.mult, op1=ALU.mult),
                         reads=[L_H[c][tt], L_RSTD, L_GAINS], writes=[L_XN[c][tt]])

        ffn_ctr = [0, 0, 0]

        ARENA_LTS.extend([t for r in L_G for t in r] + L_WIN + L_WOUT + L_XIN + L_SILU)

        def ffn(which, layer):
            arena_barrier()
            w_in = w_in_d[which][layer].rearrange("(k p) n -> p k n", p=128)
            w_out = w_out_d[which][layer].rearrange("(k p) n -> p k n", p=128)

            def load_win(i):
                b = ffn_ctr[0] % 3
                ffn_ctr[0] += 1
                P.dma("pool", lambda e, b=b, i=i: e.dma_start(out=WIN[b][:, :, 0:128], in_=w_in[:, :, i * 128:(i + 1) * 128]), L_WIN[b], writes=[L_WIN[b]])
                P.dma("pool", lambda e, b=b, i=i: e.dma_start(out=WIN[b][:, :, 128:256], in_=w_in[:, :, DFF + i * 128:DFF + (i + 1) * 128]), L_WIN[b], writes=[L_WIN[b]])
                return b

            def load_wout(hh, m):
                b = ffn_ctr[1] % 2
                ffn_ctr[1] += 1
                P.dma("pool", lambda e, b=b: e.dma_start(out=WOUT[b][:, :, :], in_=w_out[:, hh * 11:(hh + 1) * 11, m * 128:(m + 1) * 128]), L_WOUT[b], writes=[L_WOUT[b]])
                return b

            for hh in range(2):
                pend = [load_win(hh * 11 + 0), load_win(hh * 11 + 1)]
                for il in range(11):
                    b = pend.pop(0)
                    if il + 2 < 11:
                        pend.append(load_win(hh * 11 + il + 2))
                    for tt in range(NT):
                        ts = slice(tt * 512, (tt + 1) * 512)
                        q = ffn_ctr[2] % 2
                        ffn_ctr[2] += 1
                        pa, pb = PS[2 * q], PS[2 * q + 1]
                        for half, pt, lp in ((0, pa, L_PS[2 * q]), (1, pb, L_PS[2 * q + 1])):
                            for k in range(8):
                                P.op("pe", lambda e, k=k, half=half, pt=pt, b=b, ts=ts: e.matmul(pt[:, :], lhsT=WIN[b][:, k, half * 128:(half + 1) * 128], rhs=XN3[:, k, ts],
                                                                                                 start=(k == 0), stop=(k == 7)),
                                     reads=[L_WIN[b], L_XN[k][tt]], writes=[lp])
                        P.op("act", lambda e, q=q, pa=pa: e.activation(out=SILU[q], in_=pa[:, :], func=AF.Silu), reads=[L_PS[2 * q]], writes=[L_SILU[q]])
                        P.op("dve", lambda e, q=q, pb=pb, il=il, ts=ts: e.tensor_tensor(out=G3[:, il, ts], in0=SILU[q], in1=pb[:, :], op=ALU.mult),
                             reads=[L_SILU[q], L_PS[2 * q + 1]], writes=[L_G[il][tt]])
                pend = [load_wout(hh, 0)]
                for m in range(8):
                    b = pend.pop(0)
                    if m + 1 < 8:
                        pend.append(load_wout(hh, m + 1))
                    for tt in range(NT):
                        ts = slice(tt * 512, (tt + 1) * 512)
                        q = ffn_ctr[2] % 2
                        ffn_ctr[2] += 1
                        po, lpo = PS[4 + q], L_PS[4 + q]
                        for k in range(11):
                            P.op("pe", lambda e, k=k, po=po, b=b, ts=ts: e.matmul(po[:, :], lhsT=WOUT[b][:, k, :], rhs=G3[:, k, ts], start=(k == 0), stop=(k == 10)),
                                 reads=[L_WOUT[b], L_G[k][tt]], writes=[lpo])
                        P.op("dve", lambda e, po=po, m=m, ts=ts: e.scalar_tensor_tensor(out=H3[:, m, ts], in0=po[:, :], scalar=0.5, in1=H3[:, m, ts], op0=ALU.mult, op1=ALU.add),
                             reads=[lpo, L_H[m][tt]], writes=[L_H[m][tt]])


        NLEV = int(round(math.log2(S)))
        XPt = [AF32[:, i * 2048:i * 2048 + S] for i in range(4)]
        L_XP = lts(4, "xp")
        XB = [ARENA[:, 16384 + i * 2048:16384 + i * 2048 + S] for i in range(2)]
        L_XB = lts(2, "xb")
        WGLU = ARENA[:, 20480:28672].rearrange("p (k n) -> p k n", k=8)
        L_WGLU = LT("wglu")
        TB0 = 28672
        BBt = [ARENA[:, TB0 + i * 1024:TB0 + (i + 1) * 1024] for i in range(2)]
        CTt = [ARENA[:, TB0 + 2048 + i * 1024:TB0 + 2048 + (i + 1) * 1024] for i in range(2)]
        FB0 = (TB0 + 4096) // 2
        ALt = [AF32[:, FB0 + i * 352:FB0 + (i + 1) * 352] for i in range(3)]
        S5M = AF32[:, FB0 + 1056:FB0 + 1066]
        S5S = AF32[:, FB0 + 1066:FB0 + 1066 + 1120]
        T5 = [AF32[:, FB0 + 2200 + i * 512:FB0 + 2200 + (i + 1) * 512] for i in range(4)]
        L_T5 = lts(4, "t5")
        L_PRE = LT("s5pre")
        LCH = 8
        NCH = S // LCH
        LV1 = 3
        LV2 = int(round(math.log2(NCH)))
        PJt = [AF32[:, 20632 + i * 256:20632 + (i + 1) * 256] for i in range(3)]
        XS = [AF32[:, 21400 + i * 256:21400 + i * 256 + NCH] for i in range(4)]
        L_XS = lts(4, "xs")

        def s5_scalars(F, are, aim, ldt, tm):
            def A(out, in_, func, **kw):
                P.op("act", lambda e: e.activation(out=out, in_=in_, func=func, **kw), reads=[L_PRE], writes=[L_PRE])

            def TT(out, a, b, op):
                P.op("dve", lambda e: e.tensor_tensor(out=out, in0=a, in1=b, op=op), reads=[L_PRE], writes=[L_PRE])

            def TS(out, a, s1, s2, op0, op1=None):
                if op1 is None:
                    P.op("dve", lambda e: e.tensor_scalar(out, a, s1, None, op0=op0), reads=[L_PRE], writes=[L_PRE])
                else:
                    P.op("dve", lambda e: e.tensor_scalar(out, a, s1, s2, op0=op0, op1=op1), reads=[L_PRE], writes=[L_PRE])
            dt, th, mag, sn, cs, t1, den, t2 = tm[4], tm[5], tm[6], tm[7], tm[0], tm[1], tm[2], tm[3]
            A(dt, ldt, AF.Exp)
            TT(th, aim, dt, ALU.mult)
            TT(mag, are, dt, ALU.mult)
            A(mag, mag, AF.Exp)
            I32 = mybir.dt.int32

            def wrap(out, shift):
                tf, ti = tm[1], tm[2].bitcast(I32)
                TS(out, th, shift, None, ALU.add)
                TS(tf, out, 1.0 / (2 * math.pi), None, ALU.mult)
                P.op("dve", lambda e: e.tensor_copy(ti, tf), reads=[L_PRE], writes=[L_PRE])
                P.op("dve", lambda e: e.tensor_copy(tf, ti), reads=[L_PRE], writes=[L_PRE])
                P.op("dve", lambda e: e.scalar_tensor_tensor(out=out, in0=tf, scalar=-2 * math.pi, in1=out, op0=ALU.mult, op1=ALU.add), reads=[L_PRE], writes=[L_PRE])
                TS(tf, out, math.pi, -2 * math.pi, ALU.is_gt, ALU.mult)
                TT(out, out, tf, ALU.add)
                TS(tf, out, -math.pi, 2 * math.pi, ALU.is_lt, ALU.mult)
                TT(out, out, tf, ALU.add)
                TS(out, out, math.pi, -math.pi, ALU.min, ALU.max)
            wrap(sn, 0.0)
            A(sn, sn, AF.Sin)
            wrap(cs, 0.5 * math.pi)
            A(cs, cs, AF.Sin)
            abr, abi = tm[0], tm[7]
            TT(abr, cs, mag, ALU.mult)
            TT(abi, sn, mag, ALU.mult)
            TT(den, are, are, ALU.mult)
            TT(t2, aim, aim, ALU.mult)
            TT(den, den, t2, ALU.add)
            P.op("dve", lambda e: e.reciprocal(den, den), reads=[L_PRE], writes=[L_PRE])
            TS(t1, abr, -1.0, None, ALU.add)
            zr, zi = tm[4], tm[5]
            TT(zr, t1, are, ALU.mult)
            TT(t2, abi, aim, ALU.mult)
            TT(zr, zr, t2, ALU.add)
            TT(zr, zr, den, ALU.mult)
            TT(zi, abi, are, ALU.mult)
            TT(t2, t1, aim, ALU.mult)
            TT(zi, zi, t2, ALU.subtract)
            TT(zi, zi, den, ALU.mult)
            return abr, abi, zr, zi

        def s5_precompute():
            def TT(out, a, b, op):
                P.op("dve", lambda e: e.tensor_tensor(out=out, in0=a, in1=b, op=op), reads=[L_PRE], writes=[L_PRE])
            RP = AF32[:, 0:2560]
            P.dma("sp", lambda e: e.dma_start(out=RP, in_=s5R_d), L_PRE, writes=[L_PRE] + L_XP)
            P.dma("sp", lambda e: e.dma_start(out=S5M, in_=s5M_d), L_PRE, writes=[L_PRE])
            P.dma("sp", lambda e: e.dma_start(out=S5S, in_=s5S_d), L_PRE, writes=[L_PRE])
            are, aim, ldt, bre, bim = [RP[:, i * 512:(i + 1) * 512] for i in range(5)]
            tm = [AF32[:, 2560 + i * 512:2560 + (i + 1) * 512] for i in range(8)]
            abr, abi, zr, zi = s5_scalars(512, are, aim, ldt, tm)
            bbr, bbi, t2 = tm[1], tm[2], tm[3]
            TT(bbr, zr, bre, ALU.mult)
            TT(t2, zi, bim, ALU.mult)
            TT(bbr, bbr, t2, ALU.subtract)
            TT(bbi, zr, bim, ALU.mult)
            TT(t2, zi, bre, ALU.mult)
            TT(bbi, bbi, t2, ALU.add)
            for ri, src in ((0, bbr), (1, bbi)):
                dst = BBt[ri].rearrange("p (k g n) -> p k g n", k=8, g=2)
                for gl in range(2):
                    P.op("dve", lambda e, dst=dst, gl=gl, src=src: e.tensor_scalar(dst[:, :, gl, :], src.rearrange("p (k n) -> p k n", k=8), S5M[:, gl:gl + 1], None, op0=ALU.mult),
                         reads=[L_PRE], writes=[L_PRE])
            sare, saim, sldt = [S5S[:, i * 32:(i + 1) * 32] for i in range(3)]
            scre = S5S[:, 96:96 + 512]
            scim = S5S[:, 96 + 512:96 + 1024]
            tm2 = [AF32[:, 2560 + 4096 + i * 32:2560 + 4096 + (i + 1) * 32] for i in range(10)]
            abr, abi, zr, zi = s5_scalars(32, sare, saim, sldt, tm2[:8])
            AL = [t.rearrange("p (a k) -> p a k", k=11) for t in ALt]
            P.op("dve", lambda e: e.tensor_copy(AL[0][:, :, 0], abr), reads=[L_PRE], writes=[L_PRE])
            P.op("dve", lambda e: e.tensor_copy(AL[1][:, :, 0], abi), reads=[L_PRE], writes=[L_PRE])
            for k in range(1, 11):
                pr, pi = AL[0][:, :, k - 1], AL[1][:, :, k - 1]
                TT(tm2[8], pr, pr, ALU.mult)
                TT(tm2[9], pi, pi, ALU.mult)
                TT(AL[0][:, :, k], tm2[8], tm2[9], ALU.subtract)
                TT(tm2[8], pr, pi, ALU.mult)
                P.op("dve", lambda e, k=k: e.tensor_scalar(AL[1][:, :, k], tm2[8], 2.0, None, op0=ALU.mult), reads=[L_PRE], writes=[L_PRE])
            P.op("dve", lambda e: e.tensor_scalar(ALt[2], ALt[1], -1.0, None, op0=ALU.mult), reads=[L_PRE], writes=[L_PRE])
            PJ = [t.rearrange("p (a j) -> p a j", j=LCH) for t in PJt]
            P.op("dve", lambda e: e.tensor_copy(PJ[0][:, :, 0], abr), reads=[L_PRE], writes=[L_PRE])
            P.op("dve", lambda e: e.tensor_copy(PJ[1][:, :, 0], abi), reads=[L_PRE], writes=[L_PRE])
            for j in range(1, LCH):
                pr, pi = PJ[0][:, :, j - 1], PJ[1][:, :, j - 1]
                TT(tm2[8], pr, abr, ALU.mult)
                TT(tm2[9], pi, abi, ALU.mult)
                TT(PJ[0][:, :, j], tm2[8], tm2[9], ALU.subtract)
                TT(tm2[8], pr, abi, ALU.mult)
                TT(tm2[9], pi, abr, ALU.mult)
                TT(PJ[1][:, :, j], tm2[8], tm2[9], ALU.add)
            P.op("dve", lambda e: e.tensor_scalar(PJt[2], PJt[1], -1.0, None, op0=ALU.mult), reads=[L_PRE], writes=[L_PRE])
            for ri, src, sc in ((0, scre, 1.0), (1, scim, -1.0)):
                dst = CTt[ri].rearrange("p (a g h) -> p a g h", a=32, g=2)
                P.op("dve", lambda e, ri=ri: e.memset(CTt[ri], 0.0), reads=[L_PRE], writes=[L_PRE])
                for gl in range(2):
                    ps_ = slice(64 * gl, 64 * gl + 64)
                    P.op("dve", lambda e, dst=dst, gl=gl, src=src, sc=sc, ps_=ps_: e.tensor_scalar(dst[ps_, :, gl, :], src.rearrange("p (a h) -> p a h", a=32)[ps_], sc, None, op0=ALU.mult),
                         reads=[L_PRE], writes=[L_PRE])

        s5_ctr = [0]

        ARENA_LTS.extend(L_XP + L_XB + [L_WGLU, L_PRE] + L_T5 + L_XS)

        def s5_mixer():
            arena_barrier()
            s5_precompute()
            P.dma("pool", lambda e: e.dma_start(out=WGLU[:, :, :], in_=wglu_d[0].rearrange("(k p) n -> p k n", p=128)), L_WGLU, writes=[L_WGLU])
            rmsnorm((0 * 3 + 1) * 8)
            for kc in range(8):
                for pl in range(4):
                    pair = kc * 4 + pl
                    rows = slice(32 * pl, 32 * pl + 32)
                    for tt in range(NT):
                        ts = slice(tt * 512, (tt + 1) * 512)
                        q = s5_ctr[0] % 2
                        s5_ctr[0] += 1
                        for ri in range(2):
                            pt, lp = PS[2 * q + ri], L_PS[2 * q + ri]
                            P.op("pe", lambda e, pt=pt, ri=ri, rows=rows, kc=kc, ts=ts, pl=pl: e.matmul(pt[:, :], lhsT=BBt[ri][rows, kc * 128:(kc + 1) * 128], rhs=XN3[rows, kc, ts],
                                                                                                   start=True, stop=True, tile_position=(32 * pl, 0)),
                                 reads=[L_PRE, L_XN[kc][tt]], writes=[lp])
                            P.op("act", lambda e, pt=pt, ri=ri, ts=ts: e.activation(out=XPt[ri][:, ts], in_=pt[:, :], func=AF.Copy), reads=[lp], writes=[L_XP[ri]])
                    def V(t):
                        return t.rearrange("p (c j) -> p c j", j=LCH)

                    def cstt(out, in0, sc, in1, rd, wr):
                        P.op("dve", lambda e: e.scalar_tensor_tensor(out=out, in0=in0, scalar=sc, in1=in1, op0=ALU.mult, op1=ALU.add), reads=rd + [L_PRE], writes=wr)

                    def al(i, k):
                        return ALt[i][:, pair * 11 + k:pair * 11 + k + 1]
                    cur = 0
                    for k in range(LV1):
                        d = 1 << k
                        sr, si, dr, di = V(XPt[cur]), V(XPt[cur + 1]), V(XPt[2 - cur]), V(XPt[3 - cur])
                        lsr, lsi, ldr, ldi = L_XP[cur], L_XP[cur + 1], L_XP[2 - cur], L_XP[3 - cur]
                        cstt(dr[:, :, d:], sr[:, :, :LCH - d], al(0, k), sr[:, :, d:], [lsr], [ldr])
                        cstt(dr[:, :, d:], si[:, :, :LCH - d], al(2, k), dr[:, :, d:], [lsi], [ldr])
                        cstt(di[:, :, d:], sr[:, :, :LCH - d], al(1, k), si[:, :, d:], [lsr, lsi], [ldi])
                        cstt(di[:, :, d:], si[:, :, :LCH - d], al(0, k), di[:, :, d:], [lsi], [ldi])
                        P.op("act", lambda e, sr=sr, dr=dr, d=d: e.activation(out=dr[:, :, :d], in_=sr[:, :, :d], func=AF.Copy), reads=[lsr], writes=[ldr])
                        P.op("act", lambda e, si=si, di=di, d=d: e.activation(out=di[:, :, :d], in_=si[:, :, :d], func=AF.Copy), reads=[lsi], writes=[ldi])
                        cur = 2 - cur
                    Cr, Ci = V(XPt[cur]), V(XPt[cur + 1])
                    lcr, lci = L_XP[cur], L_XP[cur + 1]
                    P.op("act", lambda e, Cr=Cr: e.activation(out=XS[0], in_=Cr[:, :, LCH - 1], func=AF.Copy), reads=[lcr], writes=[L_XS[0]])
                    P.op("act", lambda e, Ci=Ci: e.activation(out=XS[1], in_=Ci[:, :, LCH - 1], func=AF.Copy), reads=[lci], writes=[L_XS[1]])
                    xc = 0
                    for k2 in range(LV2):
                        d = 1 << k2
                        kk = LV1 + k2
                        sr, si, dr, di = XS[xc], XS[xc + 1], XS[2 - xc], XS[3 - xc]
                        lsr, lsi, ldr, ldi = L_XS[xc], L_XS[xc + 1], L_XS[2 - xc], L_XS[3 - xc]
                        cstt(dr[:, d:], sr[:, :NCH - d], al(0, kk), sr[:, d:], [lsr], [ldr])
                        cstt(dr[:, d:], si[:, :NCH - d], al(2, kk), dr[:, d:], [lsi], [ldr])
                        cstt(di[:, d:], sr[:, :NCH - d], al(1, kk), si[:, d:], [lsr, lsi], [ldi])
                        cstt(di[:, d:], si[:, :NCH - d], al(0, kk), di[:, d:], [lsi], [ldi])
                        P.op("act", lambda e, sr=sr, dr=dr, d=d: e.activation(out=dr[:, :d], in_=sr[:, :d], func=AF.Copy), reads=[lsr], writes=[ldr])
                        P.op("act", lambda e, si=si, di=di, d=d: e.activation(out=di[:, :d], in_=si[:, :d], func=AF.Copy), reads=[lsi], writes=[ldi])
                        xc = 2 - xc
                    Xr, Xi = XS[xc], XS[xc + 1]
                    lxr, lxi = L_XS[xc], L_XS[xc + 1]
                    for j in range(LCH):
                        pj = [PJt[i][:, pair * LCH + j:pair * LCH + j + 1] for i in range(3)]
                        cstt(Cr[:, 1:, j], Xr[:, :NCH - 1], pj[0], Cr[:, 1:, j], [lxr], [lcr])
                        cstt(Cr[:, 1:, j], Xi[:, :NCH - 1], pj[2], Cr[:, 1:, j], [lxi], [lcr])
                        cstt(Ci[:, 1:, j], Xr[:, :NCH - 1], pj[1], Ci[:, 1:, j], [lxr], [lci])
                        cstt(Ci[:, 1:, j], Xi[:, :NCH - 1], pj[0], Ci[:, 1:, j], [lxi], [lci])
                    for ri in range(2):
                        P.op("act", lambda e, ri=ri, cur=cur: e.activation(out=XB[ri], in_=XPt[cur + ri], func=AF.Copy), reads=[L_XP[cur + ri]], writes=[L_XB[ri]])
                    for tt in range(NT):
                        ts = slice(tt * 512, (tt + 1) * 512)
                        for ri in range(2):
                            P.op("pe", lambda e, tt=tt, ri=ri, pair=pair, pl=pl, ts=ts: e.matmul(PS[4 + tt][32 * pl:32 * pl + 32, :], lhsT=CTt[ri][:, pair * 32:(pair + 1) * 32], rhs=XB[ri][:, ts],
                                                                                            start=(ri == 0), stop=(ri == 1), tile_position=(0, 32 * pl)),
                                 reads=[L_PRE, L_XB[ri]], writes=[L_PS[4 + tt]])
                for tt in range(NT):
                    ts = slice(tt * 512, (tt + 1) * 512)
                    a, b = T5[0], T5[1]
                    la, lb = L_T5[0], L_T5[1]
                    P.op("dve", lambda e, kc=kc, ts=ts, tt=tt: e.scalar_tensor_tensor(out=T5[0], in0=XN3[:, kc, ts], scalar=S5M[:, 2 + kc:3 + kc], in1=PS[4 + tt][:, :], op0=ALU.mult, op1=ALU.add),
                         reads=[L_XN[kc][tt], L_PS[4 + tt], L_PRE], writes=[la])
                    P.op("act", lambda e: e.activation(out=T5[1], in_=T5[0], func=AF.Square), reads=[la], writes=[lb])
                    P.op("dve", lambda e: e.tensor_scalar(T5[1], T5[1], 0.044715, 1.0, op0=ALU.mult, op1=ALU.add), reads=[lb], writes=[lb])
                    P.op("dve", lambda e: e.tensor_tensor(out=T5[1], in0=T5[1], in1=T5[0], op=ALU.mult), reads=[la, lb], writes=[lb])
                    P.op("act", lambda e: e.activation(out=T5[1], in_=T5[1], func=AF.Sigmoid, scale=1.5957691216057308), reads=[lb], writes=[lb])
                    P.op("dve", lambda e, kc=kc, ts=ts: e.tensor_tensor(out=XN3[:, kc, ts], in0=T5[0], in1=T5[1], op=ALU.mult), reads=[la, lb], writes=[L_XN[kc][tt]])
            for m in range(8):
                for tt in range(NT):
                    ts = slice(tt * 512, (tt + 1) * 512)
                    q = s5_ctr[0] % 2
                    s5_ctr[0] += 1
                    for k in range(8):
                        P.op("pe", lambda e, q=q, k=k, m=m, ts=ts: e.matmul(PS[q][:, :], lhsT=WGLU[:, k, m * 128:(m + 1) * 128], rhs=XN3[:, k, ts], start=(k == 0), stop=(k == 7)),
                             reads=[L_WGLU, L_XN[k][tt]], writes=[L_PS[q]])
                    P.op("act", lambda e, q=q: e.activation(out=T5[2 + q], in_=PS[q][:, :], func=AF.Sigmoid), reads=[L_PS[q]], writes=[L_T5[2 + q]])
                    P.op("dve", lambda e, q=q, m=m, ts=ts: e.tensor_tensor(out=T5[2 + q], in0=T5[2 + q], in1=XN3[:, m, ts], op=ALU.mult), reads=[L_T5[2 + q], L_XN[m][tt]], writes=[L_T5[2 + q]])
                    P.op("dve", lambda e, q=q, m=m, ts=ts: e.tensor_tensor(out=H3[:, m, ts], in0=H3[:, m, ts], in1=T5[2 + q], op=ALU.mult if False else ALU.add), reads=[L_T5[2 + q], L_H[m][tt]], writes=[L_H[m][tt]])


        NKT = S // 128
        NQG = S // 512
        NCMP = S // 16 - 1
        OFFB = 2064
        IDB, JM, BLK64 = NCB[:, 0:128], NCB[:, 128:256], NCB[:, 256:384]
        KSL_d = nc.dram_tensor("KSL_d", [4, 128, S], BF16).ap()
        KWIN_d = nc.dram_tensor("KWIN_d", [4, 128, S], BF16).ap()
        VS_d = nc.dram_tensor("VS_d", [128, NKT * 8 * 65], BF16).ap()
        KC_d = nc.dram_tensor("KC_d", [128, 512], BF16).ap()
        VC_d = nc.dram_tensor("VC_d", [128, 4 * 97], BF16).ap()
        BV_d = nc.dram_tensor("BV_d", [2 * 16 * 4096], BF16).ap()
        L_KSLd, L_KWINd = lts(4, "ksld"), lts(4, "kwind")
        L_VSd, L_KCd, L_VCd, L_BVd = LT("vsd"), LT("kcd"), LT("vcd"), LT("bvd")

        def nsa_setup():
            P.dma("pool", lambda e: e.dma_start(out=NCB[:], in_=nsac_d), L_NCB, writes=[L_NCB])
            P.dma("pool", lambda e: e.dma_start(out=EBLK[:], in_=eblk_d), L_EBLK, writes=[L_EBLK])
            P.dma("sp", lambda e: e.dma_start(out=SELTAB[:], in_=seltab_d), L_SELTAB, writes=[L_SELTAB])
            P.dma("sp", lambda e: e.dma_start(out=NGN[:], in_=ngn_d), L_NGN, writes=[L_NGN])
            RBA = ARENA[0:33, 0:16]
            OHA = [ARENA[0:33, 16 + v * 4096:16 + (v + 1) * 4096] for v in range(2)]
            BVS = ARENA[0:16, 8208:8208 + 4096]
            L_RBA, L_OHA, L_BVS = LT("rba"), LT("oha"), LT("bvs")
            P.op("dve", lambda e: e.memset(ARENA[32:33, 0:16], -3750.0), writes=[L_RBA])
            P.dma("pool", lambda e: e.dma_start(out=ARENA[0:32, 0:16], in_=relb_d), L_RBA, writes=[L_RBA])
            for v in range(2):
                P.dma("pool", lambda e, v=v: e.dma_start(out=OHA[v], in_=oha_d[v]), L_OHA, writes=[L_OHA])
            for v in range(2):
                for xc in range(8):
                    P.op("pe", lambda e, v=v, xc=xc: e.matmul(PS[0][0:16, :], lhsT=RBA, rhs=OHA[v][:, xc * 512:(xc + 1) * 512], start=True, stop=True),
                         reads=[L_RBA, L_OHA], writes=[L_PS[0]])
                    P.op("act", lambda e, xc=xc: e.activation(out=BVS[:, xc * 512:(xc + 1) * 512], in_=PS[0][0:16, :], func=AF.Copy, scale=8.0),
                         reads=[L_PS[0]], writes=[L_BVS])
                P.dma("sp", lambda e, v=v: e.dma_start(out=BV_d[v * 65536:(v + 1) * 65536].rearrange("(h x) -> h x", h=16), in_=BVS), L_BVS, reads=[L_BVS], writes=[L_BVd])
            ARENA_LTS.extend([L_RBA, L_OHA, L_BVS])

        def headnorm(psrc, lpsrc, gcol, out_ap, out_lts, sq_ap, l_sq, rst_ap, l_rst, N):
            P.op("act", lambda e: e.activation(out=sq_ap, in_=psrc, func=AF.Square), reads=[lpsrc], writes=[l_sq])
            P.op("pe", lambda e: e.matmul(PS[5][:, 0:N], lhsT=BLK64, rhs=sq_ap, start=True, stop=True), reads=[L_NCB, l_sq], writes=[L_PS[5]])
            P.op("act", lambda e: e.activation(out=rst_ap, in_=PS[5][:, 0:N], func=AF.Sqrt, bias=EPSC[:, 0:1], scale=1.0), reads=[L_PS[5], L_EPS], writes=[l_rst])
            P.op("dve", lambda e: e.reciprocal(rst_ap, rst_ap), reads=[l_rst], writes=[l_rst])
            P.op("dve", lambda e: e.scalar_tensor_tensor(out=out_ap, in0=psrc, scalar=NGN[:, gcol:gcol + 1], in1=rst_ap, op0=ALU.mult, op1=ALU.mult),
                 reads=[lpsrc, l_rst, L_NGN], writes=out_lts)

        kv_ctr = [0]
        L_KV = {n: LT("kv_" + n) for n in "w1k w1v w2 pos wkvv srct vst hid kcs vcs sqk rstk pw1 tg".split()}
        L_WKS = lts(2, "wks")
        L_KST = lts(2, "kst")
        ARENA_LTS.extend(list(L_KV.values()) + L_WKS + L_KST)

        def kv_phase():
            arena_barrier()
            W1 = [ARENA[:, i * 4096:(i + 1) * 4096].rearrange("p (l h) -> p l h", l=32) for i in range(2)]
            W2K = ARENA[:, 8192:8320]
            W2V = ARENA[:, 8320:8384]
            POS = [ARENA[:, 8384 + i * 32:8384 + (i + 1) * 32] for i in range(2)]
            WKVV = ARENA[:, 8448:12544].rearrange("p (k n) -> p k n", k=8)
            WKS = [ARENA[:, 12544 + i * 1024:12544 + (i + 1) * 1024].rearrange("p (k n) -> p k n", k=8) for i in range(2)]
            SRCT = ARENA[:, 14592:14592 + S]
            KST = [ARENA[:, 16640 + i * 2048:16640 + i * 2048 + S] for i in range(2)]
            VST = ARENA[:, 20736:20736 + NKT * 520].rearrange("p (t s d) -> p t s d", t=NKT, s=8)
            HID = ARENA[:, 29056:29184]
            KCS = ARENA[:, 29312:29824]
            VCS = ARENA[:, 29824:29824 + 388].rearrange("p (g d) -> p g d", g=4)
            SQK = ARENA[:, 30224:30736]
            RSTK = AF32[:, 15400:15912]
            PW1 = AF32[:, 15912:15914]
            TG = [AF32[:, 15920 + i * 128:15920 + (i + 1) * 128] for i in range(2)]
            wkv = wkv_d.rearrange("(k p) n -> p k n", p=128)
            rmsnorm(48)
            for i, (wd, ln) in enumerate(((w1k_d, "w1k"), (w1v_d, "w1v"))):
                for hf in range(2):
                    P.dma("pool", lambda e, i=i, wd=wd, hf=hf: e.dma_start(out=W1[i][64 * hf:64 * hf + 64, :, :], in_=wd.rearrange("l d h -> d l h")), L_KV[ln], writes=[L_KV[ln]])
            for hf in range(2):
                P.dma("pool", lambda e, hf=hf: e.dma_start(out=W2K[:, 64 * hf:64 * hf + 64], in_=w2k_d), L_KV["w2"], writes=[L_KV["w2"]])
            P.dma("pool", lambda e: e.dma_start(out=W2V, in_=w2v_d), L_KV["w2"], writes=[L_KV["w2"]])
            for i, pd in enumerate((posk_d, posv_d)):
                P.dma("pool", lambda e, i=i, pd=pd: e.dma_start(out=POS[i][0:64, :], in_=pd), L_KV["pos"], writes=[L_KV["pos"]])
            P.dma("pool", lambda e: e.dma_start(out=WKVV[:, :, 0:256], in_=wkv[:, :, 768:1024]), L_KV["wkvv"], writes=[L_KV["wkvv"]])
            P.dma("pool", lambda e: e.dma_start(out=WKVV[:, :, 256:512], in_=wkv[:, :, 1280:1536]), L_KV["wkvv"], writes=[L_KV["wkvv"]])
            for i, ln in enumerate(("w1k", "w1v")):
                for l in range(32):
                    P.op("pe", lambda e, i=i, l=l: e.matmul(PS[7][:, 0:1], lhsT=W1[i][0:64, l, :], rhs=POS[i][0:64, l:l + 1], start=(l == 0), stop=(l == 31)),
                         reads=[L_KV[ln], L_KV["pos"]], writes=[L_PS[7]])
                P.op("act", lambda e, i=i: e.activation(out=PW1[:, i:i + 1], in_=PS[7][:, 0:1], func=AF.Copy), reads=[L_PS[7]], writes=[L_KV["pw1"]])
            P.op("dve", lambda e: e.memset(VST[:, :, :, 64:65], 1.0), writes=[L_KV["vst"]])
            for kt in range(NKT):
                tt = kt // 4
                q = kv_ctr[0] % 2
                kv_ctr[0] += 1
                for k in range(8):
                    P.op("pe", lambda e, q=q, k=k, kt=kt: e.matmul(PS[q][:, :], lhsT=XN3[:, k, kt * 128:(kt + 1) * 128], rhs=WKVV[:, k, :], start=(k == 0), stop=(k == 7)),
                         reads=[L_XN[k][tt], L_KV["wkvv"]], writes=[L_PS[q]])
                P.op("act", lambda e, q=q, kt=kt: e.activation(out=VST[:, kt, :, 0:64], in_=PS[q][:, :].rearrange("p (s d) -> p s d", s=8), func=AF.Copy),
                     reads=[L_PS[q]], writes=[L_KV["vst"]])
            P.dma("sp", lambda e: e.dma_start(out=VS_d, in_=ARENA[:, 20736:20736 + NKT * 520]), L_KV["vst"], reads=[L_KV["vst"]], writes=[L_VSd])

            def load_wks(col0, dup):
                b = kv_ctr[0] % 2
                kv_ctr[0] += 1
                if dup:
                    for hf in range(2):
                        P.dma("pool", lambda e, b=b, hf=hf: e.dma_start(out=WKS[b][:, :, 64 * hf:64 * hf + 64], in_=wkv[:, :, col0:col0 + 64]), L_WKS[b], writes=[L_WKS[b]])
                else:
                    P.dma("pool", lambda e, b=b: e.dma_start(out=WKS[b][:, :, :], in_=wkv[:, :, col0:col0 + 128]), L_WKS[b], writes=[L_WKS[b]])
                return b

            for slot, dst, ldst, gcol in ((2, KSL_d, L_KSLd, 1), (4, KWIN_d, L_KWINd, 2)):
                for g in range(4):
                    b = load_wks(slot * 256 + g * 64, True)
                    kb = kv_ctr[0] % 2
                    for tt in range(NT):
                        ts = slice(tt * 512, (tt + 1) * 512)
                        q = 2 + (kv_ctr[0] % 2)
                        kv_ctr[0] += 1
                        for k in range(8):
                            P.op("pe", lambda e, q=q, k=k, b=b, ts=ts: e.matmul(PS[q][:, :], lhsT=WKS[b][:, k, :], rhs=XN3[:, k, ts], start=(k == 0), stop=(k == 7)),
                                 reads=[L_WKS[b], L_XN[k][tt]], writes=[L_PS[q]])
                        headnorm(PS[q][:, :], L_PS[q], gcol, KST[kb][:, ts], [L_KST[kb]], SQK, L_KV["sqk"], RSTK, L_KV["rstk"], 512)
                    P.dma("sp", lambda e, kb=kb, dst=dst, g=g: e.dma_start(out=dst[g], in_=KST[kb]), L_KST[kb], reads=[L_KST[kb]], writes=[ldst[g]])

            P.op("dve", lambda e: e.memset(KCS, 0.0), writes=[L_KV["kcs"]])
            P.op("dve", lambda e: e.memset(ARENA[:, 29824:29824 + 388], 0.0), writes=[L_KV["vcs"]])
            for g in range(4):
                P.dma("pool", lambda e, g=g: e.dma_start(out=VCS[:, g, 64:97], in_=ovm_d), L_KV["vcs"], writes=[L_KV["vcs"]])
            for slot in range(2):
                ln = ("w1k", "w1v")[slot]
                for gp in range(2):
                    b = load_wks(slot * 256 + gp * 128, False)
                    for tt in range(NT):
                        ts = slice(tt * 512, (tt + 1) * 512)
                        q = 2 + (kv_ctr[0] % 2)
                        kv_ctr[0] += 1
                        for k in range(8):
                            P.op("pe", lambda e, q=q, k=k, b=b, ts=ts: e.matmul(PS[q][:, :], lhsT=WKS[b][:, k, :], rhs=XN3[:, k, ts], start=(k == 0), stop=(k == 7)),
                                 reads=[L_WKS[b], L_XN[k][tt]], writes=[L_PS[q]])
                        P.op("act", lambda e, q=q, ts=ts: e.activation(out=SRCT[:, ts], in_=PS[q][:, :], func=AF.Copy), reads=[L_PS[q]], writes=[L_KV["srct"]])
                    for gi in range(2):
                        g = 2 * gp + gi
                        rows = slice(64 * gi, 64 * gi + 64)
                        for l in range(32):
                            P.op("pe", lambda e, slot=slot, rows=rows, l=l: e.matmul(PS[4][:, 0:NCMP], lhsT=W1[slot][rows, l, :], rhs=SRCT[rows, l:l + 16 * (NCMP - 1) + 1:16],
                                                                                  start=(l == 0), stop=(l == 31)),
                                 reads=[L_KV[ln], L_KV["srct"]], writes=[L_PS[4]])
                        a_, b_ = TG[0][:, 0:NCMP], TG[1][:, 0:NCMP]
                        lt = L_KV["tg"]
                        P.op("act", lambda e, slot=slot, a_=a_: e.activation(out=a_, in_=PS[4][:, 0:NCMP], func=AF.Identity, bias=PW1[:, slot:slot + 1], scale=1.0),
                             reads=[L_PS[4], L_KV["pw1"]], writes=[lt])
                        P.op("act", lambda e, a_=a_, b_=b_: e.activation(out=b_, in_=a_, func=AF.Square), reads=[lt], writes=[lt])
                        P.op("dve", lambda e, b_=b_: e.tensor_scalar(b_, b_, 0.044715, 1.0, op0=ALU.mult, op1=ALU.add), reads=[lt], writes=[lt])
                        P.op("dve", lambda e, a_=a_, b_=b_: e.tensor_tensor(out=b_, in0=b_, in1=a_, op=ALU.mult), reads=[lt], writes=[lt])
                        P.op("act", lambda e, b_=b_: e.activation(out=b_, in_=b_, func=AF.Sigmoid, scale=1.5957691216057308), reads=[lt], writes=[lt])
                        P.op("dve", lambda e, a_=a_, b_=b_: e.tensor_tensor(out=HID[:, 0:NCMP], in0=a_, in1=b_, op=ALU.mult), reads=[lt], writes=[L_KV["hid"]])
                        if slot == 0:
                            P.op("pe", lambda e: e.matmul(PS[7][:, 0:NCMP], lhsT=W2K, rhs=HID[:, 0:NCMP], start=True, stop=True), reads=[L_KV["w2"], L_KV["hid"]], writes=[L_PS[7]])
                            headnorm(PS[7][:, 0:NCMP], L_PS[7], 3, KCS[:, g * 128:g * 128 + NCMP], [L_KV["kcs"]], SQK[:, 0:NCMP], L_KV["sqk"], RSTK[:, 0:NCMP], L_KV["rstk"], NCMP)
                        else:
                            P.op("pe", lambda e: e.matmul(PS[7][0:NCMP, 0:64], lhsT=HID[:, 0:NCMP], rhs=W2V, start=True, stop=True), reads=[L_KV["w2"], L_KV["hid"]], writes=[L_PS[7]])
                            P.op("act", lambda e, g=g: e.activation(out=VCS[0:NCMP, g, 0:64], in_=PS[7][0:NCMP, 0:64], func=AF.Copy), reads=[L_PS[7]], writes=[L_KV["vcs"]])
            P.dma("sp", lambda e: e.dma_start(out=KC_d, in_=KCS), L_KV["kcs"], reads=[L_KV["kcs"]], writes=[L_KCd])
            P.dma("sp", lambda e: e.dma_start(out=VC_d, in_=ARENA[:, 29824:29824 + 388]), L_KV["vcs"], reads=[L_KV["vcs"]], writes=[L_VCd])

        L_N = {n: LT("n_" + n) for n in "ksl kwin vsl vwin kc vc tcmp tsel twin selt wg sqq snb oacc pslc gates rstq sc top8 rden coef oc".split()}
        L_QT = lts(8, "qt")
        L_PT = lts(2, "pt")
        L_WQ = lts(2, "wq")
        L_WO = lts(2, "wo")
        ARENA_LTS.extend(list(L_N.values()) + L_QT + L_PT + L_WQ + L_WO)
        n_ctr = [0, 0, 0]

        def nsa_mixer():
            arena_barrier()
            QT = ARENA[:, 0:8 * S].rearrange("p (m t) -> p m t", m=8)
            KSLg = ARENA[:, 16384:16384 + S]
            KWINg = ARENA[:, 18432:18432 + S]
            VSLg = ARENA[:, 20480:20480 + NKT * 65].rearrange("p (t d) -> p t d", t=NKT)
            VWINg = ARENA[:, 21520:21520 + NKT * 65].rearrange("p (t d) -> p t d", t=NKT)
            KCg = ARENA[:, 22560:23072]
            VCg = ARENA[:, 23072:23072 + 388]
            TCMP = ARENA[:, 23464:23464 + S]
            TSEL = ARENA[:, 25512:25512 + 1152]
            TWIN = ARENA[:, 26664:26664 + 1408]
            PT = [ARENA[:, 28072 + i * 512:28072 + (i + 1) * 512] for i in range(2)]
            SELT = ARENA[0:32, 29096:29096 + S]
            WQ = [ARENA[:, 31144 + i * 1024:31144 + (i + 1) * 1024].rearrange("p (k n) -> p k n", k=8) for i in range(2)]
            WG = ARENA[:, 33192:33576].rearrange("p (k n) -> p k n", k=8)
            WO = [ARENA[:, 33576 + i * 1024:33576 + (i + 1) * 1024].rearrange("p (k n) -> p k n", k=8) for i in range(2)]
            SQQ = ARENA[:, 35624:36136]
            SNB = ARENA[:, 36136:36136 + NKT * 32]
            OACC = AF32[:, 18400:18400 + NKT * 64].rearrange("p (t d) -> p t d", t=NKT)
            PSLC = AF32[:, 19424:19424 + NKT * 32].rearrange("p (t j) -> p t j", t=NKT)
            GATES = AF32[:, 19936:19936 + NKT * 48].rearrange("p (t c) -> p t c", t=NKT)
            RSTQ = AF32[:, 20704:21216]
            SC = AF32[:, 21216:21216 + NKT * 32].rearrange("p (t j) -> p t j", t=NKT)
            TOP8 = AF32[:, 21728:21728 + NKT * 8]
            RDEN = AF32[:, 21856:21860]
            COEF = AF32[:, 21860:21864]
            OC = XN[:].rearrange("p (t c) -> p t c", t=NKT)
            L_OC = L_N["oc"]
            wqg = wqg_d[0].rearrange("(k p) n -> p k n", p=128)
            wo = wo_d[0].rearrange("(k p) n -> p k n", p=128)

            rmsnorm((1 * 3 + 1) * 8)
            P.dma("pool", lambda e: e.dma_start(out=WG[:, :, :], in_=wqg[:, :, 1024:1072]), L_N["wg"], writes=[L_N["wg"]])

            def load_wq(m):
                b = n_ctr[0] % 2
                n_ctr[0] += 1
                P.dma("pool", lambda e, b=b, m=m: e.dma_start(out=WQ[b][:, :, :], in_=wqg[:, :, m * 128:(m + 1) * 128]), L_WQ[b], writes=[L_WQ[b]])
                return b
            pend = [load_wq(0)]
            for m in range(8):
                b = pend.pop(0)
                if m + 1 < 8:
                    pend.append(load_wq(m + 1))
                for tt in range(NT):
                    ts = slice(tt * 512, (tt + 1) * 512)
                    q = n_ctr[1] % 2
                    n_ctr[1] += 1
                    for k in range(8):
                        P.op("pe", lambda e, q=q, k=k, b=b, ts=ts: e.matmul(PS[q][:, :], lhsT=WQ[b][:, k, :], rhs=XN3[:, k, ts], start=(k == 0), stop=(k == 7)),
                             reads=[L_WQ[b], L_XN[k][tt]], writes=[L_PS[q]])
                    headnorm(PS[q][:, :], L_PS[q], 0, QT[:, m, ts], [L_QT[m]], SQQ, L_N["sqq"], RSTQ, L_N["rstq"], 512)
            for T in range(NKT):
                tt = T // 4
                for k in range(8):
                    P.op("pe", lambda e, k=k, T=T: e.matmul(PS[4][:, 0:48], lhsT=XN3[:, k, T * 128:(T + 1) * 128], rhs=WG[:, k, :], start=(k == 0), stop=(k == 7)),
                         reads=[L_XN[k][tt], L_N["wg"]], writes=[L_PS[4]])
                P.op("act", lambda e, T=T: e.activation(out=GATES[:, T, :], in_=PS[4][:, 0:48], func=AF.Sigmoid), reads=[L_PS[4]], writes=[L_N["gates"]])
            P.op("dve", lambda e: e.memset(BARD[:], 0.0), writes=[t for r in L_XN for t in r] + [L_OC])
            P.dma("sp", lambda e: e.dma_start(out=KCg, in_=KC_d), L_N["kc"], reads=[L_KCd], writes=[L_N["kc"]])
            P.dma("sp", lambda e: e.dma_start(out=VCg, in_=VC_d), L_N["vc"], reads=[L_VCd], writes=[L_N["vc"]])
            VSd4 = VS_d.rearrange("p (t s d) -> p t s d", t=NKT, s=8)

            def sbank():
                q = n_ctr[1] % 2
                n_ctr[1] += 1
                return q

            def obank():
                q = 2 + n_ctr[2] % 2
                n_ctr[2] += 1
                return q

            def finalize(po, lpo, W, qg, h, br, mode):
                pv = po[:, 0:4 * W].rearrange("p (t w) -> p t w", t=4)
                T0 = 4 * qg
                P.op("dve", lambda e: e.tensor_scalar(RDEN, pv[:, :, 64], 1e-30, None, op0=ALU.max), reads=[lpo], writes=[L_N["rden"]])
                P.op("dve", lambda e: e.reciprocal(RDEN, RDEN), reads=[L_N["rden"]], writes=[L_N["rden"]])
                P.op("dve", lambda e: e.tensor_tensor(out=COEF, in0=RDEN, in1=GATES[:, T0:T0 + 4, h * 3 + br], op=ALU.mult), reads=[L_N["rden"], L_N["gates"]], writes=[L_N["coef"]])
                for qt in range(4):
                    T = T0 + qt
                    if mode == "oc":
                        P.op("dve", lambda e, qt=qt, T=T: e.tensor_scalar(OC[:, T, h * 64:(h + 1) * 64], pv[:, qt, 0:64], COEF[:, qt:qt + 1], None, op0=ALU.mult),
                             reads=[lpo, L_N["coef"]], writes=[L_OC])
                    elif mode == "set":
                        P.op("dve", lambda e, qt=qt, T=T: e.tensor_scalar(OACC[:, T, :], pv[:, qt, 0:64], COEF[:, qt:qt + 1], None, op0=ALU.mult),
                             reads=[lpo, L_N["coef"]], writes=[L_N["oacc"]])
                    else:
                        P.op("dve", lambda e, qt=qt, T=T: e.scalar_tensor_tensor(out=OACC[:, T, :], in0=pv[:, qt, 0:64], scalar=COEF[:, qt:qt + 1], in1=OACC[:, T, :], op0=ALU.mult, op1=ALU.add),
                             reads=[lpo, L_N["coef"], L_N["oacc"]], writes=[L_N["oacc"]])
                return pv

            for g in range(4):
                P.dma("sp", lambda e, g=g: e.dma_start(out=KSLg, in_=KSL_d[g]), L_N["ksl"], reads=[L_KSLd[g]], writes=[L_N["ksl"]])
                P.dma("sp", lambda e, g=g: e.dma_start(out=KWINg, in_=KWIN_d[g]), L_N["kwin"], reads=[L_KWINd[g]], writes=[L_N["kwin"]])
                P.dma("sp", lambda e, g=g: e.dma_start(out=VSLg, in_=VSd4[:, :, g, :]), L_N["vsl"], reads=[L_VSd], writes=[L_N["vsl"]])
                P.dma("sp", lambda e, g=g: e.dma_start(out=VWINg, in_=VSd4[:, :, 4 + g, :]), L_N["vwin"], reads=[L_VSd], writes=[L_N["vwin"]])
                for r in range(4):
                    h = 4 * g + r
                    m, rows = h // 2, slice(64 * (h % 2), 64 * (h % 2) + 64)
                    P.dma("sp", lambda e, h=h: e.dma_start(out=TCMP, in_=bass.AP(BV_d.tensor, h * 4096 + OFFB - 2063, [[16, 128], [1, S]])), L_N["tcmp"], reads=[L_BVd], writes=[L_N["tcmp"]])
                    for qg in range(NQG):
                        qs = slice(qg * 512, (qg + 1) * 512)
                        sq_, oq = sbank(), obank()
                        P.op("pe", lambda e, sq_=sq_, rows=rows, m=m, qs=qs, g=g: e.matmul(PS[sq_][:, :], lhsT=KCg[rows, g * 128:(g + 1) * 128], rhs=QT[rows, m, qs], start=True, stop=False),
                             reads=[L_N["kc"], L_QT[m]], writes=[L_PS[sq_]])
                        P.op("pe", lambda e, sq_=sq_, qs=qs: e.matmul(PS[sq_][:, :], lhsT=JM, rhs=TCMP[:, qs], start=False, stop=True), reads=[L_NCB, L_N["tcmp"]], writes=[L_PS[sq_]])
                        P.op("act", lambda e, sq_=sq_: e.activation(out=PT[sq_], in_=PS[sq_][:, :], func=AF.Exp, scale=0.125), reads=[L_PS[sq_]], writes=[L_PT[sq_]])
                        for qt in range(4):
                            P.op("pe", lambda e, oq=oq, sq_=sq_, qt=qt, g=g: e.matmul(PS[oq][:, qt * 97:(qt + 1) * 97], lhsT=PT[sq_][:, qt * 128:(qt + 1) * 128], rhs=VCg[:, g * 97:(g + 1) * 97], start=True, stop=True),
                                 reads=[L_PT[sq_], L_N["vc"]], writes=[L_PS[oq]])
                        pv = finalize(PS[oq], L_PS[oq], 97, qg, h, 0, "oc")
                        for qt in range(4):
                            T = 4 * qg + qt
                            if r == 0:
                                P.op("dve", lambda e, pv=pv, qt=qt, T=T: e.tensor_scalar(PSLC[:, T, :], pv[:, qt, 65:97], RDEN[:, qt:qt + 1], None, op0=ALU.mult),
                                     reads=[L_PS[oq], L_N["rden"]], writes=[L_N["pslc"]])
                            else:
                                P.op("dve", lambda e, pv=pv, qt=qt, T=T: e.scalar_tensor_tensor(out=PSLC[:, T, :], in0=pv[:, qt, 65:97], scalar=RDEN[:, qt:qt + 1], in1=PSLC[:, T, :], op0=ALU.mult, op1=ALU.add),
                                     reads=[L_PS[oq], L_N["rden"], L_N["pslc"]], writes=[L_N["pslc"]])
                SCf = AF32[:, 21216:21216 + NKT * 32]
                PSLCf = AF32[:, 19424:19424 + NKT * 32]
                P.op("dve", lambda e: e.tensor_tensor(out=SCf, in0=PSLCf, in1=SELTAB[:, 0:NKT * 32], op=ALU.mult), reads=[L_N["pslc"], L_SELTAB], writes=[L_N["sc"]])
                P.op("dve", lambda e: e.tensor_tensor(out=SCf, in0=SCf, in1=SELTAB[:, 512:512 + NKT * 32], op=ALU.add), reads=[L_N["sc"], L_SELTAB], writes=[L_N["sc"]])
                for T in range(NKT):
                    P.op("dve", lambda e, T=T: e.max(TOP8[:, T * 8:(T + 1) * 8], SC[:, T, :]), reads=[L_N["sc"]], writes=[L_N["top8"]])
                for T in range(NKT):
                    P.op("dve", lambda e, T=T: e.tensor_scalar(SC[:, T, :], SC[:, T, :], TOP8[:, T * 8 + 7:T * 8 + 8], None, op0=ALU.is_ge), reads=[L_N["sc"], L_N["top8"]], writes=[L_N["sc"]])
                P.op("dve", lambda e: e.tensor_scalar(SNB, SCf, -1.0, 30000.0, op0=ALU.add, op1=ALU.mult), reads=[L_N["sc"]], writes=[L_N["snb"]])
                PSB = PS[6][:, :].bitcast(BF16)
                for T4 in range(NKT // 4):
                    for ti in range(4):
                        T = T4 * 4 + ti
                        P.op("pe", lambda e, T=T, ti=ti: e.transpose(PSB[0:32, ti * 128:(ti + 1) * 128], SNB[:, T * 32:(T + 1) * 32], IDB), reads=[L_N["snb"], L_NCB], writes=[L_PS[6]])
                    P.op("act", lambda e, T4=T4: e.activation(out=SELT[:, T4 * 512:(T4 + 1) * 512], in_=PSB[0:32, 0:512], func=AF.Copy), reads=[L_PS[6]], writes=[L_N["selt"]])
                for r in range(4):
                    h = 4 * g + r
                    m, rows = h // 2, slice(64 * (h % 2), 64 * (h % 2) + 64)
                    P.dma("sp", lambda e, h=h: e.dma_start(out=TSEL, in_=bass.AP(BV_d.tensor, h * 4096 + OFFB - 511, [[1, 128], [1, 1152]])), L_N["tsel"], reads=[L_BVd], writes=[L_N["tsel"]])
                    P.dma("sp", lambda e, h=h: e.dma_start(out=TWIN, in_=bass.AP(BV_d.tensor, (16 + h) * 4096 + OFFB - 511, [[1, 128], [1, 1408]])), L_N["twin"], reads=[L_BVd], writes=[L_N["twin"]])
                    for br, Kg, lK, Vg, lV, TB, lT in ((1, KSLg, L_N["ksl"], VSLg, L_N["vsl"], TSEL, L_N["tsel"]), (2, KWINg, L_N["kwin"], VWINg, L_N["vwin"], TWIN, L_N["twin"])):
                        for qg in range(NQG):
                            qs = slice(qg * 512, (qg + 1) * 512)
                            oq = obank()
                            kt_lo = 0 if br == 1 else max(0, 4 * qg - 4)
                            bank_used = [False]
                            for kt in range(kt_lo, 4 * qg + 4):
                                dl = 4 * qg - kt
                                col0 = 128 * ((min(dl, 2) if br == 1 else dl) + 3)
                                sq_ = sbank()
                                ks = slice(kt * 128, (kt + 1) * 128)
                                P.op("pe", lambda e, sq_=sq_, rows=rows, m=m, qs=qs, ks=ks, Kg=Kg: e.matmul(PS[sq_][:, :], lhsT=Kg[rows, ks], rhs=QT[rows, m, qs], start=True, stop=False),
                                     reads=[lK, L_QT[m]], writes=[L_PS[sq_]])
                                P.op("pe", lambda e, sq_=sq_, col0=col0, TB=TB, br=br: e.matmul(PS[sq_][:, :], lhsT=JM, rhs=TB[:, col0:col0 + 512], start=False, stop=(br == 2)),
                                     reads=[L_NCB, lT], writes=[L_PS[sq_]])
                                if br == 1:
                                    P.op("pe", lambda e, sq_=sq_, ks=ks, qs=qs: e.matmul(PS[sq_][:, :], lhsT=EBLK[0:32, ks], rhs=SELT[:, qs], start=False, stop=True),
                                         reads=[L_EBLK, L_N["selt"]], writes=[L_PS[sq_]])
                                P.op("act", lambda e, sq_=sq_: e.activation(out=PT[sq_], in_=PS[sq_][:, :], func=AF.Exp, scale=0.125), reads=[L_PS[sq_]], writes=[L_PT[sq_]])
                                for qt in range(4):
                                    T = 4 * qg + qt
                                    if T < kt or (br == 2 and T > kt + 4):
                                        continue
                                    first = not bank_used[0]
                                    bank_used[0] = True
                                    P.op("pe", lambda e, oq=oq, sq_=sq_, qt=qt, kt=kt, Vg=Vg, first=first, T=T: e.matmul(PS[oq][:, qt * 65:(qt + 1) * 65], lhsT=PT[sq_][:, qt * 128:(qt + 1) * 128], rhs=Vg[:, kt, :],
                                                                                                              start=first, stop=(kt == T), skip_group_check=True),
                                         reads=[L_PT[sq_], lV], writes=[L_PS[oq]])
                            finalize(PS[oq], L_PS[oq], 65, qg, h, br, "set" if br == 1 else "add")
                    for qg in range(NQG):
                        T0 = 4 * qg
                        P.op("dve", lambda e, T0=T0, h=h: e.tensor_tensor(out=OC[:, T0:T0 + 4, h * 64:(h + 1) * 64], in0=OACC[:, T0:T0 + 4, :], in1=OC[:, T0:T0 + 4, h * 64:(h + 1) * 64], op=ALU.add),
                             reads=[L_N["oacc"], L_OC], writes=[L_OC])
            OT = QT
            PSB = PS[6][:, :].bitcast(BF16)
            PSB2 = PS[7][:, :].bitcast(BF16)
            for T in range(NKT):
                for mg in range(2):
                    pb, lpb = (PSB, L_PS[6]) if mg == 0 else (PSB2, L_PS[7])
                    for mi in range(4):
                        mm_ = mg * 4 + mi
                        P.op("pe", lambda e, pb=pb, mi=mi, T=T, mm_=mm_: e.transpose(pb[:, mi * 128:(mi + 1) * 128], OC[:, T, mm_ * 128:(mm_ + 1) * 128], IDB), reads=[L_OC, L_NCB], writes=[lpb])
                    eng = "act" if mg == 0 else "dve"
                    if mg == 0:
                        P.op("act", lambda e, pb=pb, T=T, mg=mg: e.activation(out=OT[:, mg * 4:(mg + 1) * 4, T * 128:(T + 1) * 128], in_=pb[:, 0:512].rearrange("p (m t) -> p m t", m=4), func=AF.Copy),
                             reads=[lpb], writes=L_QT[mg * 4:(mg + 1) * 4])
                    else:
                        P.op("dve", lambda e, pb=pb, T=T, mg=mg: e.tensor_copy(OT[:, mg * 4:(mg + 1) * 4, T * 128:(T + 1) * 128], pb[:, 0:512].rearrange("p (m t) -> p m t", m=4)),
                             reads=[lpb], writes=L_QT[mg * 4:(mg + 1) * 4])

            def load_wo(mo):
                b = n_ctr[0] % 2
                n_ctr[0] += 1
                P.dma("pool", lambda e, b=b, mo=mo: e.dma_start(out=WO[b][:, :, :], in_=wo[:, :, mo * 128:(mo + 1) * 128]), L_WO[b], writes=[L_WO[b]])
                return b
            pend = [load_wo(0)]
            for mo in range(8):
                b = pend.pop(0)
                if mo + 1 < 8:
                    pend.append(load_wo(mo + 1))
                for tt in range(NT):
                    ts = slice(tt * 512, (tt + 1) * 512)
                    q = sbank()
                    for k in range(8):
                        P.op("pe", lambda e, q=q, k=k, b=b, ts=ts: e.matmul(PS[q][:, :], lhsT=WO[b][:, k, :], rhs=OT[:, k, ts], start=(k == 0), stop=(k == 7)),
                             reads=[L_WO[b], L_QT[k]], writes=[L_PS[q]])
                    P.op("dve", lambda e, q=q, mo=mo, ts=ts: e.tensor_tensor(out=H3[:, mo, ts], in0=H3[:, mo, ts], in1=PS[q][:, :], op=ALU.add),
                         reads=[L_PS[q], L_H[mo][tt]], writes=[L_H[mo][tt]])
            P.op("dve", lambda e: e.memset(BARD[:], 0.0), writes=[t for r in L_XN for t in r] + [L_OC])

        if "nsa" in parts:
            nsa_setup()
        for s in range(NSEQ):
            arena_barrier()
            for t128 in range(S // 128):
                b = t128 % 2
                tt = t128 // 4
                row0 = s * S + t128 * 128
                P.dma("sp", lambda e, b=b, row0=row0: e.dma_start(out=XIN[b], in_=x_d[row0:row0 + 128, :]), L_XIN[b], writes=[L_XIN[b]])
                for cg in range(2):
                    pq = 6 + cg
                    for ci in range(4):
                        c = cg * 4 + ci
                        P.op("pe", lambda e, b=b, c=c, ci=ci, pq=pq: e.transpose(PS[pq][:, ci * 128:(ci + 1) * 128], XIN[b][:, c * 128:(c + 1) * 128], IDF[:]),
                             reads=[L_XIN[b], L_IDF], writes=[L_PS[pq]])
                    P.op("act" if cg == 0 else "dve",
                         (lambda e, cg=cg, pq=pq, t128=t128: e.activation(out=H3[:, cg * 4:(cg + 1) * 4, t128 * 128:(t128 + 1) * 128], in_=PS[pq][:, :].rearrange("p (c t) -> p c t", c=4), func=AF.Copy)) if cg == 0 else
                         (lambda e, cg=cg, pq=pq, t128=t128: e.tensor_copy(H3[:, cg * 4:(cg + 1) * 4, t128 * 128:(t128 + 1) * 128], PS[pq][:, :].rearrange("p (c t) -> p c t", c=4))),
                         reads=[L_PS[pq]], writes=[L_H[cg * 4 + ci][tt] for ci in range(4)])

            for layer in range(2):
                rmsnorm((layer * 3 + 0) * 8)
                ffn(0, layer)
                if layer == 0 and "s5" in parts:
                    s5_mixer()
                if layer == 1 and "nsa" in parts:
                    nsa_mixer()
                rmsnorm((layer * 3 + 2) * 8)
                ffn(1, layer)
                if layer == 0 and "nsa" in parts:
                    kv_phase()

            arena_barrier()
            for t128 in range(S // 128):
                b = t128 % 2
                tt = t128 // 4
                row0 = s * S + t128 * 128
                for cg in range(2):
                    pq = 6 + cg
                    for ci in range(4):
                        c = cg * 4 + ci
                        P.op("pe", lambda e, c=c, ci=ci, pq=pq, t128=t128: e.transpose(PS[pq][:, ci * 128:(ci + 1) * 128], H3[:, c, t128 * 128:(t128 + 1) * 128], IDF[:]),
                             reads=[L_H[c][tt], L_IDF], writes=[L_PS[pq]])
                    P.op("act" if cg == 0 else "dve",
                         (lambda e, b=b, cg=cg, pq=pq: e.activation(out=XIN[b][:, cg * 512:(cg + 1) * 512], in_=PS[pq][:, :], func=AF.Copy)) if cg == 0 else
                         (lambda e, b=b, cg=cg, pq=pq: e.tensor_copy(XIN[b][:, cg * 512:(cg + 1) * 512], PS[pq][:, :])),
                         reads=[L_PS[pq]], writes=[L_XIN[b]])
                P.dma("sp", lambda e, b=b, row0=row0: e.dma_start(out=y_d[row0:row0 + 128, :], in_=XIN[b]), L_XIN[b], reads=[L_XIN[b]], writes=[L_Y])

        P.wait_all_dma("sp")
        P.emit()
    return nc


def prep_inputs(inp):
    g = np.stack([inp["ffn1_norm"], inp["mix_norm"], inp["ffn2_norm"]], axis=1)
    gains = np.ascontiguousarray(g.reshape(2, 3, 8, 128).transpose(3, 0, 1, 2).reshape(128, 48)).astype(np.float32)
    gains = np.ascontiguousarray(np.concatenate([gains, inp["kv_norm"].reshape(8, 128).T], axis=1)).astype(np.float32)
    m = dict(host_consts())
    m["gains"] = gains
    a_re, a_im, ldt = inp["s5_a_re"][0], inp["s5_a_im"][0], inp["s5_log_dt"][0]
    b_re, b_im, c_re, c_im = inp["s5_b_re"][0], inp["s5_b_im"][0], inp["s5_c_re"][0], inp["s5_c_im"][0]
    pl, gl, hp, kc, nn = np.meshgrid(np.arange(4), np.arange(2), np.arange(16), np.arange(8), np.arange(64), indexing="ij")
    gg = 8 * kc + 2 * pl + gl
    R = np.stack([a_re[gg, nn], a_im[gg, nn], ldt[gg], b_re[gg, nn, hp], b_im[gg, nn, hp]], axis=0)
    m["s5R"] = np.ascontiguousarray(R.reshape(5, 128, 512).transpose(1, 0, 2).reshape(128, 2560)).astype(np.float32)
    gl2, n2, pr2 = np.meshgrid(np.arange(2), np.arange(64), np.arange(32), indexing="ij")
    g2 = 2 * pr2 + gl2
    S3 = np.stack([a_re[g2, n2], a_im[g2, n2], ldt[g2]], axis=0).reshape(3, 128, 32).transpose(1, 0, 2).reshape(128, 96)
    gl3, n3, pr3, h3 = np.meshgrid(np.arange(2), np.arange(64), np.arange(32), np.arange(16), indexing="ij")
    g3 = 2 * pr3 + gl3
    C2 = np.stack([c_re[g3, h3, n3], c_im[g3, h3, n3]], axis=0).reshape(2, 128, 512).transpose(1, 0, 2).reshape(128, 1024)
    m["s5S"] = np.ascontiguousarray(np.concatenate([S3, C2], axis=1)).astype(np.float32)
    p = np.arange(128)
    maskR = np.stack([((p // 16) % 2 == 0), ((p // 16) % 2 == 1)], axis=1).astype(np.float32)
    dsk = inp["s5_d"][0].reshape(8, 128).T
    m["s5M"] = np.ascontiguousarray(np.concatenate([maskR, dsk], axis=1)).astype(np.float32)
    m["s5_w_glu"] = np.ascontiguousarray(inp["s5_w_glu"], dtype=np.float32)
    for k in ("w_kv", "w_qg", "w_o", "cmp_k_w1", "cmp_v_w1", "cmp_k_w2", "cmp_v_w2", "rel_bias"):
        m[k] = np.ascontiguousarray(inp[k], dtype=np.float32)
    m["posT_k"] = np.ascontiguousarray(inp["cmp_pos_k"].T).astype(np.float32)
    m["posT_v"] = np.ascontiguousarray(inp["cmp_pos_v"].T).astype(np.float32)
    dup = lambda v: np.concatenate([v, v])
    ngn = np.zeros((128, 8), np.float32)
    ngn[:, 0] = dup(inp["q_norm"][0])
    ngn[:, 1] = dup(inp["k_norm_slc"])
    ngn[:, 2] = dup(inp["k_norm_win"])
    ngn[:, 3] = dup(inp["k_norm_cmp"])
    m["ngn"] = ngn
    for k in ("ffn1_w_in", "ffn2_w_in", "ffn1_w_out", "ffn2_w_out"):
        m[k] = np.ascontiguousarray(inp[k], dtype=np.float32)
    return m


_NC_CACHE = {}


def kernel(**inp):
    x = np.ascontiguousarray(inp["x"], dtype=np.float32)
    B, S, Dm = x.shape
    n = 8
    per = B // n
    shared = prep_inputs(inp)
    key = (per, S)
    if key not in _NC_CACHE:
        _NC_CACHE[key] = build(NSEQ=per, S=S, parts=("ffn", "s5", "nsa"))
    nc = _NC_CACHE[key]
    in_maps = []
    for i in range(n):
        m = dict(shared)
        m["x"] = x[i * per:(i + 1) * per].reshape(per * S, Dm)
        in_maps.append(m)
    res = run_bass_kernel_spmd(nc, in_maps, core_ids=list(range(n)))
    out = np.concatenate([r["y"].reshape(per, S, Dm) for r in res.results], axis=0)
    return out.astype(np.float32)
```

```python
import contextlib
import math
import numpy as np
import concourse.bass as bass
import concourse.mybir as mybir
from concourse.bass_utils import run_bass_kernel_spmd

F32 = mybir.dt.float32
BF16 = mybir.dt.bfloat16
AF = mybir.ActivationFunctionType
ALU = mybir.AluOpType

ENGS = ("pe", "act", "dve", "pool", "sp")

D = 1024
DFF = 2816
NPAIR = DFF // 128
RMS_EPS = 1e-6


class LT:
    __slots__ = ("name", "w", "rs", "sem")

    def __init__(self, name=""):
        self.name = name
        self.w = None
        self.rs = {}
        self.sem = None


def lts(n, name=""):
    return [LT(name + str(i)) for i in range(n)]


class Prog:
    def __init__(self, nc):
        self.nc = nc
        self.ops = {e: [] for e in ENGS}
        self.dma_cnt = {}
        self.n_dma_sems = 0
        self.final_waits = {}

    def _deps(self, eng, reads, writes):
        deps = {}

        def add(ev):
            if ev is None:
                return
            k, v = ev
            if k == eng and eng == "pe":
                return
            if deps.get(k, -1) < v:
                deps[k] = v

        for t in reads:
            add(t.w)
        for t in writes:
            add(t.w)
            for k, v in t.rs.items():
                add((k, v))
        return deps

    def _commit(self, ev, reads, writes):
        k, v = ev
        for t in reads:
            if t.rs.get(k, -1) < v:
                t.rs[k] = v
        for t in writes:
            t.w = ev
            t.rs = {}

    def op(self, eng, fn, reads=(), writes=()):
        deps = self._deps(eng, reads, writes)
        idx = len(self.ops[eng])
        self.ops[eng].append({"fn": fn, "deps": deps, "sig": False, "dma": None})
        self._commit((eng, idx), reads, writes)

    def dma(self, eng, fn, home, reads=(), writes=()):
        deps = self._deps(eng, reads, writes)
        if home.sem is None:
            home.sem = {}
        if eng not in home.sem:
            home.sem[eng] = self.n_dma_sems
            self.n_dma_sems += 1
            self.dma_cnt[home.sem[eng]] = 0
        sid = home.sem[eng]
        self.dma_cnt[sid] += 16
        ev = (("d", sid), self.dma_cnt[sid])
        self.ops[eng].append({"fn": fn, "deps": deps, "sig": False, "dma": ev})
        self._commit(ev, reads, writes)
        return ev

    def wait_all_dma(self, eng="sp"):
        self.final_waits[eng] = dict(self.dma_cnt)

    def emit(self):
        nc = self.nc
        for e in ENGS:
            for o in self.ops[e]:
                for k, v in o["deps"].items():
                    if isinstance(k, str):
                        self.ops[k][v]["sig"] = True
        sigcount = {}
        for e in ENGS:
            c = 0
            arr = []
            for o in self.ops[e]:
                if o["sig"]:
                    c += 1
                arr.append(c)
            sigcount[e] = arr
        with contextlib.ExitStack() as st:
            esem = {e: st.enter_context(nc.semaphore("s_" + e)) for e in ENGS}
            dsem = [st.enter_context(nc.semaphore("d%d" % i)) for i in range(self.n_dma_sems)]
            block = st.enter_context(nc.Block())

            def run(e, eng):
                waited = {}
                for o in self.ops[e]:
                    for k, v in o["deps"].items():
                        if isinstance(k, str):
                            sem, val = esem[k], sigcount[k][v]
                        else:
                            sem, val = dsem[k[1]], v
                        if waited.get(k, -1) >= val:
                            continue
                        waited[k] = val
                        eng.wait_ge(sem, val)
                    inst = o["fn"](eng)
                    if o["dma"] is not None:
                        inst.then_inc(dsem[o["dma"][0][1]], 16)
                    elif o["sig"]:
                        inst.then_inc(esem[e], 1)
                if e in self.final_waits:
                    for s, v in self.final_waits[e].items():
                        eng.wait_ge(dsem[s], v)

            @block.tensor
            def _(eng):
                run("pe", eng)

            @block.scalar
            def _(eng):
                run("act", eng)

            @block.vector
            def _(eng):
                run("dve", eng)

            @block.gpsimd
            def _(eng):
                run("pool", eng)

            @block.sync
            def _(eng):
                run("sp", eng)


def _rel_bucket_np(dist):
    n = np.maximum(dist, 0)
    logv = np.log(np.maximum(n, 1).astype(np.float32) / np.float32(16)) / np.float32(math.log(8.0))
    large = np.minimum(16 + (logv * np.float32(16)).astype(np.int32), 31)
    return np.where(n < 16, n, large)


def host_consts():
    c = {}
    c["ident_f"] = np.eye(128, dtype=np.float32)
    c["ones_d"] = np.full((128, 128), 1.0 / D, dtype=np.float32)
    blk = np.zeros((128, 128), np.float32)
    blk[:64, :64] = 1.0 / 64
    blk[64:, 64:] = 1.0 / 64
    c["nsa_consts"] = np.concatenate([np.eye(128, dtype=np.float32), np.eye(128, dtype=np.float32)[::-1], blk], axis=1)
    x = np.arange(4096)
    dist = x - 2064
    bucket = _rel_bucket_np(dist)
    oha = np.zeros((2, 33, 4096), np.float32)
    for v in range(2):
        valid = (dist >= 0) & ((dist < 512) if v == 1 else True)
        oha[v, bucket[valid], x[valid]] = 1.0
        oha[v, 32, x[~valid]] = 1.0
    c["oha"] = oha
    eb = np.zeros((32, 2048), np.float32)
    k = np.arange(2048)
    eb[k // 64, k] = 1.0
    c["eblk"] = eb
    p_, T_, j_ = np.meshgrid(np.arange(128), np.arange(16), np.arange(32), indexing="ij")
    t_ = 128 * T_ + p_
    cur = t_ // 64
    forced = (j_ == 0) | (j_ == cur) | (j_ == cur - 1)
    causal = j_ * 64 <= t_
    caus01 = (causal & ~forced).astype(np.float32)
    addm = np.where(forced, 1e9, np.where(causal, 0.0, -1e9)).astype(np.float32)
    c["seltab"] = np.concatenate([caus01.reshape(128, 512), addm.reshape(128, 512)], axis=1)
    cs = np.arange(128)[:, None] * 16
    js = np.arange(32)[None, :] * 64
    ov = np.clip(np.minimum(cs + 32, js + 64) - np.maximum(cs, js), 0, None).astype(np.float32) / 32.0
    ovm = np.concatenate([np.ones((128, 1), np.float32), ov], axis=1)
    ovm[127, :] = 0.0
    c["ovm"] = ovm
    return c


def build(NSEQ=4, S=2048, parts=("ffn",)):
    NT = S // 512
    nc = bass.Bass("TRN2", target_bir_lowering=False)

    def din(name, shape, dt=F32):
        return nc.dram_tensor(name, list(shape), dt, kind="ExternalInput").ap()

    x_d = din("x", [NSEQ * S, D])
    y_d = nc.dram_tensor("y", [NSEQ * S, D], F32, kind="ExternalOutput").ap()
    w_in_d = [din("ffn1_w_in", [2, D, 2 * DFF]), din("ffn2_w_in", [2, D, 2 * DFF])]
    w_out_d = [din("ffn1_w_out", [2, DFF, D]), din("ffn2_w_out", [2, DFF, D])]
    gains_d = din("gains", [128, 7 * 8])
    ident_d = din("ident_f", [128, 128])
    s5R_d = din("s5R", [128, 5 * 512])
    s5S_d = din("s5S", [128, 3 * 32 + 2 * 512])
    s5M_d = din("s5M", [128, 2 + 8])
    wglu_d = din("s5_w_glu", [1, D, D])
    wkv_d = din("w_kv", [D, 1536])
    wqg_d = din("w_qg", [1, D, 1072])
    wo_d = din("w_o", [1, D, D])
    w1k_d = din("cmp_k_w1", [32, 64, 128])
    w1v_d = din("cmp_v_w1", [32, 64, 128])
    w2k_d = din("cmp_k_w2", [128, 64])
    w2v_d = din("cmp_v_w2", [128, 64])
    posk_d = din("posT_k", [64, 32])
    posv_d = din("posT_v", [64, 32])
    relb_d = din("rel_bias", [32, 16])
    oha_d = din("oha", [2, 33, 4096])
    nsac_d = din("nsa_consts", [128, 384])
    eblk_d = din("eblk", [32, 2048])
    seltab_d = din("seltab", [128, 1024])
    ngn_d = din("ngn", [128, 8])
    ovm_d = din("ovm", [128, 33])
    ones_d = din("ones_d", [128, 128])

    st = contextlib.ExitStack()
    with st:
        def sb(name, shape, dt):
            return st.enter_context(nc.sbuf_tensor(name, list(shape), dt))

        H = sb("H", [128, 8 * S], F32)
        XN = sb("XN", [128, 8 * S], BF16)
        ARENA = sb("ARENA", [128, 45056], BF16)
        IDF = sb("IDF", [128, 128], F32)
        ONES = sb("ONES", [128, 128], BF16)
        GAINS = sb("GAINS", [128, 56], F32)
        SQ = sb("SQ", [128, 2 * 512], BF16)
        NCB = sb("NCB", [128, 3 * 128], BF16)
        EBLK = sb("EBLK", [32, 2048], BF16)
        SELTAB = sb("SELTAB", [128, 1024], F32)
        NGN = sb("NGN", [128, 8], F32)
        RSTD = sb("RSTD", [128, 512], F32)
        EPSC = sb("EPSC", [128, 1], F32)
        AF32 = ARENA[:].bitcast(F32)
        XIN = [AF32[:, 20480 + i * 1024:20480 + (i + 1) * 1024] for i in range(2)]
        SILU = [ARENA[:, 31488 + i * 512:31488 + (i + 1) * 512] for i in range(2)]
        PS = [st.enter_context(nc.psum_tensor("PS%d" % i, [128, 512], F32)) for i in range(8)]

        H3 = H[:].rearrange("p (c t) -> p c t", c=8)
        XN3 = XN[:].rearrange("p (c t) -> p c t", c=8)

        P = Prog(nc)
        L_H = [[LT("h%d_%d" % (c, t)) for t in range(NT)] for c in range(8)]
        L_XN = [[LT("xn%d_%d" % (c, t)) for t in range(NT)] for c in range(8)]
        L_PS = lts(8, "ps")
        L_IDF, L_ONES, L_GAINS, L_RSTD, L_NCB, L_EBLK, L_SELTAB, L_NGN = [LT(n) for n in "idf ones gains rstd ncb eblk seltab ngn".split()]
        L_SQ = lts(2, "sq")
        L_XIN = lts(2, "xin")
        L_SILU = lts(2, "silu")
        L_Y = LT("ydram")

        P.dma("sp", lambda e: e.dma_start(out=IDF[:], in_=ident_d), L_IDF, writes=[L_IDF])
        P.dma("pool", lambda e: e.dma_start(out=ONES[:], in_=ones_d), L_ONES, writes=[L_ONES])
        P.dma("sp", lambda e: e.dma_start(out=GAINS[:], in_=gains_d), L_GAINS, writes=[L_GAINS])

        L_EPS = LT("eps")
        P.op("dve", lambda e: e.memset(EPSC[:], RMS_EPS), writes=[L_EPS])

        G_OFF = 0
        G3 = ARENA[:, G_OFF:G_OFF + 11 * S].rearrange("p (c t) -> p c t", c=11)
        WIN_OFF = 11 * 2048
        WIN = [ARENA[:, WIN_OFF + i * 2048: WIN_OFF + (i + 1) * 2048].rearrange("p (k n) -> p k n", k=8) for i in range(3)]
        WOUT_OFF = WIN_OFF + 3 * 2048
        WOUT = [ARENA[:, WOUT_OFF + i * 1408: WOUT_OFF + (i + 1) * 1408].rearrange("p (k n) -> p k n", k=11) for i in range(2)]
        L_G = [[LT("g%d_%d" % (c, t)) for t in range(NT)] for c in range(11)]
        L_WIN = lts(3, "win")
        L_WOUT = lts(2, "wout")

        ARENA_LTS = []
        BARD = sb("BARD", [128, 1], F32)

        def arena_barrier():
            P.op("dve", lambda e: e.memset(BARD[:], 0.0), writes=ARENA_LTS)

        def rmsnorm(gcol):
            for tt in range(NT):
                ts = slice(tt * 512, (tt + 1) * 512)
                for c in range(8):
                    b = c % 2
                    P.op("act", lambda e, c=c, ts=ts, b=b: e.activation(out=SQ[:, b * 512:(b + 1) * 512], in_=H3[:, c, ts], func=AF.Square),
                         reads=[L_H[c][tt]], writes=[L_SQ[b]])
                    P.op("pe", lambda e, c=c, b=b: e.matmul(PS[6][:, :], lhsT=ONES[:], rhs=SQ[:, b * 512:(b + 1) * 512], start=(c == 0), stop=(c == 7)),
                         reads=[L_ONES, L_SQ[b]], writes=[L_PS[6]])
                P.op("act", lambda e: e.activation(out=RSTD[:], in_=PS[6][:, :], func=AF.Sqrt, bias=EPSC[:, 0:1], scale=1.0),
                     reads=[L_PS[6], L_EPS], writes=[L_RSTD])
                P.op("dve", lambda e: e.reciprocal(RSTD[:], RSTD[:]), reads=[L_RSTD], writes=[L_RSTD])
                for c in range(8):
                    P.op("dve", lambda e, c=c, ts=ts: e.scalar_tensor_tensor(out=XN3[:, c, ts], in0=H3[:, c, ts], scalar=GAINS[:, gcol + c:gcol + c + 1],
                                                                            in1=RSTD[:], op0=ALU.mult, op1=ALU.mult),
                         reads=[L_H[c][tt], L_RSTD, L_GAINS], writes=[L_XN[c][tt]])

        ffn_ctr = [0, 0, 0]

        ARENA_LTS.extend([t for r in L_G for t in r] + L_WIN + L_WOUT + L_XIN + L_SILU)

        def ffn(which, layer):
            arena_barrier()
            w_in = w_in_d[which][layer].rearrange("(k p) n -> p k n", p=128)
            w_out = w_out_d[which][layer].rearrange("(k p) n -> p k n", p=128)

            def load_win(i):
                b = ffn_ctr[0] % 3
                ffn_ctr[0] += 1
                P.dma("pool", lambda e, b=b, i=i: e.dma_start(out=WIN[b][:, :, 0:128], in_=w_in[:, :, i * 128:(i + 1) * 128]), L_WIN[b], writes=[L_WIN[b]])
                P.dma("pool", lambda e, b=b, i=i: e.dma_start(out=WIN[b][:, :, 128:256], in_=w_in[:, :, DFF + i * 128:DFF + (i + 1) * 128]), L_WIN[b], writes=[L_WIN[b]])
                return b

            def load_wout(hh, m):
                b = ffn_ctr[1] % 2
                ffn_ctr[1] += 1
                P.dma("pool", lambda e, b=b: e.dma_start(out=WOUT[b][:, :, :], in_=w_out[:, hh * 11:(hh + 1) * 11, m * 128:(m + 1) * 128]), L_WOUT[b], writes=[L_WOUT[b]])
                return b

            for hh in range(2):
                pend = [load_win(hh * 11 + 0), load_win(hh * 11 + 1)]
                for il in range(11):
                    b = pend.pop(0)
                    if il + 2 < 11:
                        pend.append(load_win(hh * 11 + il + 2))
                    for tt in range(NT):
                        ts = slice(tt * 512, (tt + 1) * 512)
                        q = ffn_ctr[2] % 2
                        ffn_ctr[2] += 1
                        pa, pb = PS[2 * q], PS[2 * q + 1]
                        for half, pt, lp in ((0, pa, L_PS[2 * q]), (1, pb, L_PS[2 * q + 1])):
                            for k in range(8):
                                P.op("pe", lambda e, k=k, half=half, pt=pt, b=b, ts=ts: e.matmul(pt[:, :], lhsT=WIN[b][:, k, half * 128:(half + 1) * 128], rhs=XN3[:, k, ts],
                                                                                                 start=(k == 0), stop=(k == 7)),
                                     reads=[L_WIN[b], L_XN[k][tt]], writes=[lp])
                        P.op("act", lambda e, q=q, pa=pa: e.activation(out=SILU[q], in_=pa[:, :], func=AF.Silu), reads=[L_PS[2 * q]], writes=[L_SILU[q]])
                        P.op("dve", lambda e, q=q, pb=pb, il=il, ts=ts: e.tensor_tensor(out=G3[:, il, ts], in0=SILU[q], in1=pb[:, :], op=ALU.mult),
                             reads=[L_SILU[q], L_PS[2 * q + 1]], writes=[L_G[il][tt]])
                pend = [load_wout(hh, 0)]
                for m in range(8):
                    b = pend.pop(0)
                    if m + 1 < 8:
                        pend.append(load_wout(hh, m + 1))
                    for tt in range(NT):
                        ts = slice(tt * 512, (tt + 1) * 512)
                        q = ffn_ctr[2] % 2
                        ffn_ctr[2] += 1
                        po, lpo = PS[4 + q], L_PS[4 + q]
                        for k in range(11):
                            P.op("pe", lambda e, k=k, po=po, b=b, ts=ts: e.matmul(po[:, :], lhsT=WOUT[b][:, k, :], rhs=G3[:, k, ts], start=(k == 0), stop=(k == 10)),
                                 reads=[L_WOUT[b], L_G[k][tt]], writes=[lpo])
                        P.op("dve", lambda e, po=po, m=m, ts=ts: e.scalar_tensor_tensor(out=H3[:, m, ts], in0=po[:, :], scalar=0.5, in1=H3[:, m, ts], op0=ALU.mult, op1=ALU.add),
                             reads=[lpo, L_H[m][tt]], writes=[L_H[m][tt]])


        NLEV = int(round(math.log2(S)))
        XPt = [AF32[:, i * 2048:i * 2048 + S] for i in range(4)]
        L_XP = lts(4, "xp")
        XB = [ARENA[:, 16384 + i * 2048:16384 + i * 2048 + S] for i in range(2)]
        L_XB = lts(2, "xb")
        WGLU = ARENA[:, 20480:28672].rearrange("p (k n) -> p k n", k=8)
        L_WGLU = LT("wglu")
        TB0 = 28672
        BBt = [ARENA[:, TB0 + i * 1024:TB0 + (i + 1) * 1024] for i in range(2)]
        CTt = [ARENA[:, TB0 + 2048 + i * 1024:TB0 + 2048 + (i + 1) * 1024] for i in range(2)]
        FB0 = (TB0 + 4096) // 2
        ALt = [AF32[:, FB0 + i * 352:FB0 + (i + 1) * 352] for i in range(3)]
        S5M = AF32[:, FB0 + 1056:FB0 + 1066]
        S5S = AF32[:, FB0 + 1066:FB0 + 1066 + 1120]
        T5 = [AF32[:, FB0 + 2200 + i * 512:FB0 + 2200 + (i + 1) * 512] for i in range(4)]
        L_T5 = lts(4, "t5")
        L_PRE = LT("s5pre")
        LCH = 8
        NCH = S // LCH
        LV1 = 3
        LV2 = int(round(math.log2(NCH)))
        PJt = [AF32[:, 20632 + i * 256:20632 + (i + 1) * 256] for i in range(3)]
        XS = [AF32[:, 21400 + i * 256:21400 + i * 256 + NCH] for i in range(4)]
        L_XS = lts(4, "xs")

        def s5_scalars(F, are, aim, ldt, tm):
            def A(out, in_, func, **kw):
                P.op("act", lambda e: e.activation(out=out, in_=in_, func=func, **kw), reads=[L_PRE], writes=[L_PRE])

            def TT(out, a, b, op):
                P.op("dve", lambda e: e.tensor_tensor(out=out, in0=a, in1=b, op=op), reads=[L_PRE], writes=[L_PRE])

            def TS(out, a, s1, s2, op0, op1=None):
                if op1 is None:
                    P.op("dve", lambda e: e.tensor_scalar(out, a, s1, None, op0=op0), reads=[L_PRE], writes=[L_PRE])
                else:
                    P.op("dve", lambda e: e.tensor_scalar(out, a, s1, s2, op0=op0, op1=op1), reads=[L_PRE], writes=[L_PRE])
            dt, th, mag, sn, cs, t1, den, t2 = tm[4], tm[5], tm[6], tm[7], tm[0], tm[1], tm[2], tm[3]
            A(dt, ldt, AF.Exp)
            TT(th, aim, dt, ALU.mult)
            TT(mag, are, dt, ALU.mult)
            A(mag, mag, AF.Exp)
            I32 = mybir.dt.int32

            def wrap(out, shift):
                tf, ti = tm[1], tm[2].bitcast(I32)
                TS(out, th, shift, None, ALU.add)
                TS(tf, out, 1.0 / (2 * math.pi), None, ALU.mult)
                P.op("dve", lambda e: e.tensor_copy(ti, tf), reads=[L_PRE], writes=[L_PRE])
                P.op("dve", lambda e: e.tensor_copy(tf, ti), reads=[L_PRE], writes=[L_PRE])
                P.op("dve", lambda e: e.scalar_tensor_tensor(out=out, in0=tf, scalar=-2 * math.pi, in1=out, op0=ALU.mult, op1=ALU.add), reads=[L_PRE], writes=[L_PRE])
                TS(tf, out, math.pi, -2 * math.pi, ALU.is_gt, ALU.mult)
                TT(out, out, tf, ALU.add)
                TS(tf, out, -math.pi, 2 * math.pi, ALU.is_lt, ALU.mult)
                TT(out, out, tf, ALU.add)
                TS(out, out, math.pi, -math.pi, ALU.min, ALU.max)
            wrap(sn, 0.0)
            A(sn, sn, AF.Sin)
            wrap(cs, 0.5 * math.pi)
            A(cs, cs, AF.Sin)
            abr, abi = tm[0], tm[7]
            TT(abr, cs, mag, ALU.mult)
            TT(abi, sn, mag, ALU.mult)
            TT(den, are, are, ALU.mult)
            TT(t2, aim, aim, ALU.mult)
            TT(den, den, t2, ALU.add)
            P.op("dve", lambda e: e.reciprocal(den, den), reads=[L_PRE], writes=[L_PRE])
            TS(t1, abr, -1.0, None, ALU.add)
            zr, zi = tm[4], tm[5]
            TT(zr, t1, are, ALU.mult)
            TT(t2, abi, aim, ALU.mult)
            TT(zr, zr, t2, ALU.add)
            TT(zr, zr, den, ALU.mult)
            TT(zi, abi, are, ALU.mult)
            TT(t2, t1, aim, ALU.mult)
            TT(zi, zi, t2, ALU.subtract)
            TT(zi, zi, den, ALU.mult)
            return abr, abi, zr, zi

        def s5_precompute():
            def TT(out, a, b, op):
                P.op("dve", lambda e: e.tensor_tensor(out=out, in0=a, in1=b, op=op), reads=[L_PRE], writes=[L_PRE])
            RP = AF32[:, 0:2560]
            P.dma("sp", lambda e: e.dma_start(out=RP, in_=s5R_d), L_PRE, writes=[L_PRE] + L_XP)
            P.dma("sp", lambda e: e.dma_start(out=S5M, in_=s5M_d), L_PRE, writes=[L_PRE])
            P.dma("sp", lambda e: e.dma_start(out=S5S, in_=s5S_d), L_PRE, writes=[L_PRE])
            are, aim, ldt, bre, bim = [RP[:, i * 512:(i + 1) * 512] for i in range(5)]
            tm = [AF32[:, 2560 + i * 512:2560 + (i + 1) * 512] for i in range(8)]
            abr, abi, zr, zi = s5_scalars(512, are, aim, ldt, tm)
            bbr, bbi, t2 = tm[1], tm[2], tm[3]
            TT(bbr, zr, bre, ALU.mult)
            TT(t2, zi, bim, ALU.mult)
            TT(bbr, bbr, t2, ALU.subtract)
            TT(bbi, zr, bim, ALU.mult)
            TT(t2, zi, bre, ALU.mult)
            TT(bbi, bbi, t2, ALU.add)
            for ri, src in ((0, bbr), (1, bbi)):
                dst = BBt[ri].rearrange("p (k g n) -> p k g n", k=8, g=2)
                for gl in range(2):
                    P.op("dve", lambda e, dst=dst, gl=gl, src=src: e.tensor_scalar(dst[:, :, gl, :], src.rearrange("p (k n) -> p k n", k=8), S5M[:, gl:gl + 1], None, op0=ALU.mult),
                         reads=[L_PRE], writes=[L_PRE])
            sare, saim, sldt = [S5S[:, i * 32:(i + 1) * 32] for i in range(3)]
            scre = S5S[:, 96:96 + 512]
            scim = S5S[:, 96 + 512:96 + 1024]
            tm2 = [AF32[:, 2560 + 4096 + i * 32:2560 + 4096 + (i + 1) * 32] for i in range(10)]
            abr, abi, zr, zi = s5_scalars(32, sare, saim, sldt, tm2[:8])
            AL = [t.rearrange("p (a k) -> p a k", k=11) for t in ALt]
            P.op("dve", lambda e: e.tensor_copy(AL[0][:, :, 0], abr), reads=[L_PRE], writes=[L_PRE])
            P.op("dve", lambda e: e.tensor_copy(AL[1][:, :, 0], abi), reads=[L_PRE], writes=[L_PRE])
            for k in range(1, 11):
                pr, pi = AL[0][:, :, k - 1], AL[1][:, :, k - 1]
                TT(tm2[8], pr, pr, ALU.mult)
                TT(tm2[9], pi, pi, ALU.mult)
                TT(AL[0][:, :, k], tm2[8], tm2[9], ALU.subtract)
                TT(tm2[8], pr, pi, ALU.mult)
                P.op("dve", lambda e, k=k: e.tensor_scalar(AL[1][:, :, k], tm2[8], 2.0, None, op0=ALU.mult), reads=[L_PRE], writes=[L_PRE])
            P.op("dve", lambda e: e.tensor_scalar(ALt[2], ALt[1], -1.0, None, op0=ALU.mult), reads=[L_PRE], writes=[L_PRE])
            PJ = [t.rearrange("p (a j) -> p a j", j=LCH) for t in PJt]
            P.op("dve", lambda e: e.tensor_copy(PJ[0][:, :, 0], abr), reads=[L_PRE], writes=[L_PRE])
            P.op("dve", lambda e: e.tensor_copy(PJ[1][:, :, 0], abi), reads=[L_PRE], writes=[L_PRE])
            for j in range(1, LCH):
                pr, pi = PJ[0][:, :, j - 1], PJ[1][:, :, j - 1]
                TT(tm2[8], pr, abr, ALU.mult)
                TT(tm2[9], pi, abi, ALU.mult)
                TT(PJ[0][:, :, j], tm2[8], tm2[9], ALU.subtract)
                TT(tm2[8], pr, abi, ALU.mult)
                TT(tm2[9], pi, abr, ALU.mult)
                TT(PJ[1][:, :, j], tm2[8], tm2[9], ALU.add)
            P.op("dve", lambda e: e.tensor_scalar(PJt[2], PJt[1], -1.0, None, op0=ALU.mult), reads=[L_PRE], writes=[L_PRE])
            for ri, src, sc in ((0, scre, 1.0), (1, scim, -1.0)):
                dst = CTt[ri].rearrange("p (a g h) -> p a g h", a=32, g=2)
                P.op("dve", lambda e, ri=ri: e.memset(CTt[ri], 0.0), reads=[L_PRE], writes=[L_PRE])
                for gl in range(2):
                    ps_ = slice(64 * gl, 64 * gl + 64)
                    P.op("dve", lambda e, dst=dst, gl=gl, src=src, sc=sc, ps_=ps_: e.tensor_scalar(dst[ps_, :, gl, :], src.rearrange("p (a h) -> p a h", a=32)[ps_], sc, None, op0=ALU.mult),
                         reads=[L_PRE], writes=[L_PRE])

        s5_ctr = [0]

        ARENA_LTS.extend(L_XP + L_XB + [L_WGLU, L_PRE] + L_T5 + L_XS)

        def s5_mixer():
            arena_barrier()
            s5_precompute()
            P.dma("pool", lambda e: e.dma_start(out=WGLU[:, :, :], in_=wglu_d[0].rearrange("(k p) n -> p k n", p=128)), L_WGLU, writes=[L_WGLU])
            rmsnorm((0 * 3 + 1) * 8)
            for kc in range(8):
                for pl in range(4):
                    pair = kc * 4 + pl
                    rows = slice(32 * pl, 32 * pl + 32)
                    for tt in range(NT):
                        ts = slice(tt * 512, (tt + 1) * 512)
                        q = s5_ctr[0] % 2
                        s5_ctr[0] += 1
                        for ri in range(2):
                            pt, lp = PS[2 * q + ri], L_PS[2 * q + ri]
                            P.op("pe", lambda e, pt=pt, ri=ri, rows=rows, kc=kc, ts=ts, pl=pl: e.matmul(pt[:, :], lhsT=BBt[ri][rows, kc * 128:(kc + 1) * 128], rhs=XN3[rows, kc, ts],
                                                                                                   start=True, stop=True, tile_position=(32 * pl, 0)),
                                 reads=[L_PRE, L_XN[kc][tt]], writes=[lp])
                            P.op("act", lambda e, pt=pt, ri=ri, ts=ts: e.activation(out=XPt[ri][:, ts], in_=pt[:, :], func=AF.Copy), reads=[lp], writes=[L_XP[ri]])
                    def V(t):
                        return t.rearrange("p (c j) -> p c j", j=LCH)

                    def cstt(out, in0, sc, in1, rd, wr):
                        P.op("dve", lambda e: e.scalar_tensor_tensor(out=out, in0=in0, scalar=sc, in1=in1, op0=ALU.mult, op1=ALU.add), reads=rd + [L_PRE], writes=wr)

                    def al(i, k):
                        return ALt[i][:, pair * 11 + k:pair * 11 + k + 1]
                    cur = 0
                    for k in range(LV1):
                        d = 1 << k
                        sr, si, dr, di = V(XPt[cur]), V(XPt[cur + 1]), V(XPt[2 - cur]), V(XPt[3 - cur])
                        lsr, lsi, ldr, ldi = L_XP[cur], L_XP[cur + 1], L_XP[2 - cur], L_XP[3 - cur]
                        cstt(dr[:, :, d:], sr[:, :, :LCH - d], al(0, k), sr[:, :, d:], [lsr], [ldr])
                        cstt(dr[:, :, d:], si[:, :, :LCH - d], al(2, k), dr[:, :, d:], [lsi], [ldr])
                        cstt(di[:, :, d:], sr[:, :, :LCH - d], al(1, k), si[:, :, d:], [lsr, lsi], [ldi])
                        cstt(di[:, :, d:], si[:, :, :LCH - d], al(0, k), di[:, :, d:], [lsi], [ldi])
                        P.op("act", lambda e, sr=sr, dr=dr, d=d: e.activation(out=dr[:, :, :d], in_=sr[:, :, :d], func=AF.Copy), reads=[lsr], writes=[ldr])
                        P.op("act", lambda e, si=si, di=di, d=d: e.activation(out=di[:, :, :d], in_=si[:, :, :d], func=AF.Copy), reads=[lsi], writes=[ldi])
                        cur = 2 - cur
                    Cr, Ci = V(XPt[cur]), V(XPt[cur + 1])
                    lcr, lci = L_XP[cur], L_XP[cur + 1]
                    P.op("act", lambda e, Cr=Cr: e.activation(out=XS[0], in_=Cr[:, :, LCH - 1], func=AF.Copy), reads=[lcr], writes=[L_XS[0]])
                    P.op("act", lambda e, Ci=Ci: e.activation(out=XS[1], in_=Ci[:, :, LCH - 1], func=AF.Copy), reads=[lci], writes=[L_XS[1]])
                    xc = 0
                    for k2 in range(LV2):
                        d = 1 << k2
                        kk = LV1 + k2
                        sr, si, dr, di = XS[xc], XS[xc + 1], XS[2 - xc], XS[3 - xc]
                        lsr, lsi, ldr, ldi = L_XS[xc], L_XS[xc + 1], L_XS[2 - xc], L_XS[3 - xc]
                        cstt(dr[:, d:], sr[:, :NCH - d], al(0, kk), sr[:, d:], [lsr], [ldr])
                        cstt(dr[:, d:], si[:, :NCH - d], al(2, kk), dr[:, d:], [lsi], [ldr])
                        cstt(di[:, d:], sr[:, :NCH - d], al(1, kk), si[:, d:], [lsr, lsi], [ldi])
                        cstt(di[:, d:], si[:, :NCH - d], al(0, kk), di[:, d:], [lsi], [ldi])
                        P.op("act", lambda e, sr=sr, dr=dr, d=d: e.activation(out=dr[:, :d], in_=sr[:, :d], func=AF.Copy), reads=[lsr], writes=[ldr])
                        P.op("act", lambda e, si=si, di=di, d=d: e.activation(out=di[:, :d], in_=si[:, :d], func=AF.Copy), reads=[lsi], writes=[ldi])
                        xc = 2 - xc
                    Xr, Xi = XS[xc], XS[xc + 1]
                    lxr, lxi = L_XS[xc], L_XS[xc + 1]
                    for j in range(LCH):
                        pj = [PJt[i][:, pair * LCH + j:pair * LCH + j + 1] for i in range(3)]
                        cstt(Cr[:, 1:, j], Xr[:, :NCH - 1], pj[0], Cr[:, 1:, j], [lxr], [lcr])
                        cstt(Cr[:, 1:, j], Xi[:, :NCH - 1], pj[2], Cr[:, 1:, j], [lxi], [lcr])
                        cstt(Ci[:, 1:, j], Xr[:, :NCH - 1], pj[1], Ci[:, 1:, j], [lxr], [lci])
                        cstt(Ci[:, 1:, j], Xi[:, :NCH - 1], pj[0], Ci[:, 1:, j], [lxi], [lci])
                    for ri in range(2):
                        P.op("act", lambda e, ri=ri, cur=cur: e.activation(out=XB[ri], in_=XPt[cur + ri], func=AF.Copy), reads=[L_XP[cur + ri]], writes=[L_XB[ri]])
                    for tt in range(NT):
                        ts = slice(tt * 512, (tt + 1) * 512)
                        for ri in range(2):
                            P.op("pe", lambda e, tt=tt, ri=ri, pair=pair, pl=pl, ts=ts: e.matmul(PS[4 + tt][32 * pl:32 * pl + 32, :], lhsT=CTt[ri][:, pair * 32:(pair + 1) * 32], rhs=XB[ri][:, ts],
                                                                                            start=(ri == 0), stop=(ri == 1), tile_position=(0, 32 * pl)),
                                 reads=[L_PRE, L_XB[ri]], writes=[L_PS[4 + tt]])
                for tt in range(NT):
                    ts = slice(tt * 512, (tt + 1) * 512)
                    a, b = T5[0], T5[1]
                    la, lb = L_T5[0], L_T5[1]
                    P.op("dve", lambda e, kc=kc, ts=ts, tt=tt: e.scalar_tensor_tensor(out=T5[0], in0=XN3[:, kc, ts], scalar=S5M[:, 2 + kc:3 + kc], in1=PS[4 + tt][:, :], op0=ALU.mult, op1=ALU.add),
                         reads=[L_XN[kc][tt], L_PS[4 + tt], L_PRE], writes=[la])
                    P.op("act", lambda e: e.activation(out=T5[1], in_=T5[0], func=AF.Square), reads=[la], writes=[lb])
                    P.op("dve", lambda e: e.tensor_scalar(T5[1], T5[1], 0.044715, 1.0, op0=ALU.mult, op1=ALU.add), reads=[lb], writes=[lb])
                    P.op("dve", lambda e: e.tensor_tensor(out=T5[1], in0=T5[1], in1=T5[0], op=ALU.mult), reads=[la, lb], writes=[lb])
                    P.op("act", lambda e: e.activation(out=T5[1], in_=T5[1], func=AF.Sigmoid, scale=1.5957691216057308), reads=[lb], writes=[lb])
                    P.op("dve", lambda e, kc=kc, ts=ts: e.tensor_tensor(out=XN3[:, kc, ts], in0=T5[0], in1=T5[1], op=ALU.mult), reads=[la, lb], writes=[L_XN[kc][tt]])
            for m in range(8):
                for tt in range(NT):
                    ts = slice(tt * 512, (tt + 1) * 512)
                    q = s5_ctr[0] % 2
                    s5_ctr[0] += 1
                    for k in range(8):
                        P.op("pe", lambda e, q=q, k=k, m=m, ts=ts: e.matmul(PS[q][:, :], lhsT=WGLU[:, k, m * 128:(m + 1) * 128], rhs=XN3[:, k, ts], start=(k == 0), stop=(k == 7)),
                             reads=[L_WGLU, L_XN[k][tt]], writes=[L_PS[q]])
                    P.op("act", lambda e, q=q: e.activation(out=T5[2 + q], in_=PS[q][:, :], func=AF.Sigmoid), reads=[L_PS[q]], writes=[L_T5[2 + q]])
                    P.op("dve", lambda e, q=q, m=m, ts=ts: e.tensor_tensor(out=T5[2 + q], in0=T5[2 + q], in1=XN3[:, m, ts], op=ALU.mult), reads=[L_T5[2 + q], L_XN[m][tt]], writes=[L_T5[2 + q]])
                    P.op("dve", lambda e, q=q, m=m, ts=ts: e.tensor_tensor(out=H3[:, m, ts], in0=H3[:, m, ts], in1=T5[2 + q], op=ALU.mult if False else ALU.add), reads=[L_T5[2 + q], L_H[m][tt]], writes=[L_H[m][tt]])


        NKT = S // 128
        NQG = S // 512
        NCMP = S // 16 - 1
        OFFB = 2064
        IDB, JM, BLK64 = NCB[:, 0:128], NCB[:, 128:256], NCB[:, 256:384]
        KSL_d = nc.dram_tensor("KSL_d", [4, 128, S], BF16).ap()
        KWIN_d = nc.dram_tensor("KWIN_d", [4, 128, S], BF16).ap()
        VS_d = nc.dram_tensor("VS_d", [128, NKT * 8 * 65], BF16).ap()
        KC_d = nc.dram_tensor("KC_d", [128, 512], BF16).ap()
        VC_d = nc.dram_tensor("VC_d", [128, 4 * 97], BF16).ap()
        BV_d = nc.dram_tensor("BV_d", [2 * 16 * 4096], BF16).ap()
        L_KSLd, L_KWINd = lts(4, "ksld"), lts(4, "kwind")
        L_VSd, L_KCd, L_VCd, L_BVd = LT("vsd"), LT("kcd"), LT("vcd"), LT("bvd")

        def nsa_setup():
            P.dma("pool", lambda e: e.dma_start(out=NCB[:], in_=nsac_d), L_NCB, writes=[L_NCB])
            P.dma("pool", lambda e: e.dma_start(out=EBLK[:], in_=eblk_d), L_EBLK, writes=[L_EBLK])
            P.dma("sp", lambda e: e.dma_start(out=SELTAB[:], in_=seltab_d), L_SELTAB, writes=[L_SELTAB])
            P.dma("sp", lambda e: e.dma_start(out=NGN[:], in_=ngn_d), L_NGN, writes=[L_NGN])
            RBA = ARENA[0:33, 0:16]
            OHA = [ARENA[0:33, 16 + v * 4096:16 + (v + 1) * 4096] for v in range(2)]
            BVS = ARENA[0:16, 8208:8208 + 4096]
            L_RBA, L_OHA, L_BVS = LT("rba"), LT("oha"), LT("bvs")
            P.op("dve", lambda e: e.memset(ARENA[32:33, 0:16], -3750.0), writes=[L_RBA])
            P.dma("pool", lambda e: e.dma_start(out=ARENA[0:32, 0:16], in_=relb_d), L_RBA, writes=[L_RBA])
            for v in range(2):
                P.dma("pool", lambda e, v=v: e.dma_start(out=OHA[v], in_=oha_d[v]), L_OHA, writes=[L_OHA])
            for v in range(2):
                for xc in range(8):
                    P.op("pe", lambda e, v=v, xc=xc: e.matmul(PS[0][0:16, :], lhsT=RBA, rhs=OHA[v][:, xc * 512:(xc + 1) * 512], start=True, stop=True),
                         reads=[L_RBA, L_OHA], writes=[L_PS[0]])
                    P.op("act", lambda e, xc=xc: e.activation(out=BVS[:, xc * 512:(xc + 1) * 512], in_=PS[0][0:16, :], func=AF.Copy, scale=8.0),
                         reads=[L_PS[0]], writes=[L_BVS])
                P.dma("sp", lambda e, v=v: e.dma_start(out=BV_d[v * 65536:(v + 1) * 65536].rearrange("(h x) -> h x", h=16), in_=BVS), L_BVS, reads=[L_BVS], writes=[L_BVd])
            ARENA_LTS.extend([L_RBA, L_OHA, L_BVS])

        def headnorm(psrc, lpsrc, gcol, out_ap, out_lts, sq_ap, l_sq, rst_ap, l_rst, N):
            P.op("act", lambda e: e.activation(out=sq_ap, in_=psrc, func=AF.Square), reads=[lpsrc], writes=[l_sq])
            P.op("pe", lambda e: e.matmul(PS[5][:, 0:N], lhsT=BLK64, rhs=sq_ap, start=True, stop=True), reads=[L_NCB, l_sq], writes=[L_PS[5]])
            P.op("act", lambda e: e.activation(out=rst_ap, in_=PS[5][:, 0:N], func=AF.Sqrt, bias=EPSC[:, 0:1], scale=1.0), reads=[L_PS[5], L_EPS], writes=[l_rst])
            P.op("dve", lambda e: e.reciprocal(rst_ap, rst_ap), reads=[l_rst], writes=[l_rst])
            P.op("dve", lambda e: e.scalar_tensor_tensor(out=out_ap, in0=psrc, scalar=NGN[:, gcol:gcol + 1], in1=rst_ap, op0=ALU.mult, op1=ALU.mult),
                 reads=[lpsrc, l_rst, L_NGN], writes=out_lts)

        kv_ctr = [0]
        L_KV = {n: LT("kv_" + n) for n in "w1k w1v w2 pos wkvv srct vst hid kcs vcs sqk rstk pw1 tg".split()}
        L_WKS = lts(2, "wks")
        L_KST = lts(2, "kst")
        ARENA_LTS.extend(list(L_KV.values()) + L_WKS + L_KST)

        def kv_phase():
            arena_barrier()
            W1 = [ARENA[:, i * 4096:(i + 1) * 4096].rearrange("p (l h) -> p l h", l=32) for i in range(2)]
            W2K = ARENA[:, 8192:8320]
            W2V = ARENA[:, 8320:8384]
            POS = [ARENA[:, 8384 + i * 32:8384 + (i + 1) * 32] for i in range(2)]
            WKVV = ARENA[:, 8448:12544].rearrange("p (k n) -> p k n", k=8)
            WKS = [ARENA[:, 12544 + i * 1024:12544 + (i + 1) * 1024].rearrange("p (k n) -> p k n", k=8) for i in range(2)]
            SRCT = ARENA[:, 14592:14592 + S]
            KST = [ARENA[:, 16640 + i * 2048:16640 + i * 2048 + S] for i in range(2)]
            VST = ARENA[:, 20736:20736 + NKT * 520].rearrange("p (t s d) -> p t s d", t=NKT, s=8)
            HID = ARENA[:, 29056:29184]
            KCS = ARENA[:, 29312:29824]
            VCS = ARENA[:, 29824:29824 + 388].rearrange("p (g d) -> p g d", g=4)
            SQK = ARENA[:, 30224:30736]
            RSTK = AF32[:, 15400:15912]
            PW1 = AF32[:, 15912:15914]
            TG = [AF32[:, 15920 + i * 128:15920 + (i + 1) * 128] for i in range(2)]
            wkv = wkv_d.rearrange("(k p) n -> p k n", p=128)
            rmsnorm(48)
            for i, (wd, ln) in enumerate(((w1k_d, "w1k"), (w1v_d, "w1v"))):
                for hf in range(2):
                    P.dma("pool", lambda e, i=i, wd=wd, hf=hf: e.dma_start(out=W1[i][64 * hf:64 * hf + 64, :, :], in_=wd.rearrange("l d h -> d l h")), L_KV[ln], writes=[L_KV[ln]])
            for hf in range(2):
                P.dma("pool", lambda e, hf=hf: e.dma_start(out=W2K[:, 64 * hf:64 * hf + 64], in_=w2k_d), L_KV["w2"], writes=[L_KV["w2"]])
            P.dma("pool", lambda e: e.dma_start(out=W2V, in_=w2v_d), L_KV["w2"], writes=[L_KV["w2"]])
            for i, pd in enumerate((posk_d, posv_d)):
                P.dma("pool", lambda e, i=i, pd=pd: e.dma_start(out=POS[i][0:64, :], in_=pd), L_KV["pos"], writes=[L_KV["pos"]])
            P.dma("pool", lambda e: e.dma_start(out=WKVV[:, :, 0:256], in_=wkv[:, :, 768:1024]), L_KV["wkvv"], writes=[L_KV["wkvv"]])
            P.dma("pool", lambda e: e.dma_start(out=WKVV[:, :, 256:512], in_=wkv[:, :, 1280:1536]), L_KV["wkvv"], writes=[L_KV["wkvv"]])
            for i, ln in enumerate(("w1k", "w1v")):
                for l in range(32):
                    P.op("pe", lambda e, i=i, l=l: e.matmul(PS[7][:, 0:1], lhsT=W1[i][0:64, l, :], rhs=POS[i][0:64, l:l + 1], start=(l == 0), stop=(l == 31)),
                         reads=[L_KV[ln], L_KV["pos"]], writes=[L_PS[7]])
                P.op("act", lambda e, i=i: e.activation(out=PW1[:, i:i + 1], in_=PS[7][:, 0:1], func=AF.Copy), reads=[L_PS[7]], writes=[L_KV["pw1"]])
            P.op("dve", lambda e: e.memset(VST[:, :, :, 64:65], 1.0), writes=[L_KV["vst"]])
            for kt in range(NKT):
                tt = kt // 4
                q = kv_ctr[0] % 2
                kv_ctr[0] += 1
                for k in range(8):
                    P.op("pe", lambda e, q=q, k=k, kt=kt: e.matmul(PS[q][:, :], lhsT=XN3[:, k, kt * 128:(kt + 1) * 128], rhs=WKVV[:, k, :], start=(k == 0), stop=(k == 7)),
                         reads=[L_XN[k][tt], L_KV["wkvv"]], writes=[L_PS[q]])
                P.op("act", lambda e, q=q, kt=kt: e.activation(out=VST[:, kt, :, 0:64], in_=PS[q][:, :].rearrange("p (s d) -> p s d", s=8), func=AF.Copy),
                     reads=[L_PS[q]], writes=[L_KV["vst"]])
            P.dma("sp", lambda e: e.dma_start(out=VS_d, in_=ARENA[:, 20736:20736 + NKT * 520]), L_KV["vst"], reads=[L_KV["vst"]], writes=[L_VSd])

            def load_wks(col0, dup):
                b = kv_ctr[0] % 2
                kv_ctr[0] += 1
                if dup:
                    for hf in range(2):
                        P.dma("pool", lambda e, b=b, hf=hf: e.dma_start(out=WKS[b][:, :, 64 * hf:64 * hf + 64], in_=wkv[:, :, col0:col0 + 64]), L_WKS[b], writes=[L_WKS[b]])
                else:
                    P.dma("pool", lambda e, b=b: e.dma_start(out=WKS[b][:, :, :], in_=wkv[:, :, col0:col0 + 128]), L_WKS[b], writes=[L_WKS[b]])
                return b

            for slot, dst, ldst, gcol in ((2, KSL_d, L_KSLd, 1), (4, KWIN_d, L_KWINd, 2)):
                for g in range(4):
                    b = load_wks(slot * 256 + g * 64, True)
                    kb = kv_ctr[0] % 2
                    for tt in range(NT):
                        ts = slice(tt * 512, (tt + 1) * 512)
                        q = 2 + (kv_ctr[0] % 2)
                        kv_ctr[0] += 1
                        for k in range(8):
                            P.op("pe", lambda e, q=q, k=k, b=b, ts=ts: e.matmul(PS[q][:, :], lhsT=WKS[b][:, k, :], rhs=XN3[:, k, ts], start=(k == 0), stop=(k == 7)),
                                 reads=[L_WKS[b], L_XN[k][tt]], writes=[L_PS[q]])
                        headnorm(PS[q][:, :], L_PS[q], gcol, KST[kb][:, ts], [L_KST[kb]], SQK, L_KV["sqk"], RSTK, L_KV["rstk"], 512)
                    P.dma("sp", lambda e, kb=kb, dst=dst, g=g: e.dma_start(out=dst[g], in_=KST[kb]), L_KST[kb], reads=[L_KST[kb]], writes=[ldst[g]])

            P.op("dve", lambda e: e.memset(KCS, 0.0), writes=[L_KV["kcs"]])
            P.op("dve", lambda e: e.memset(ARENA[:, 29824:29824 + 388], 0.0), writes=[L_KV["vcs"]])
            for g in range(4):
                P.dma("pool", lambda e, g=g: e.dma_start(out=VCS[:, g, 64:97], in_=ovm_d), L_KV["vcs"], writes=[L_KV["vcs"]])
            for slot in range(2):
                ln = ("w1k", "w1v")[slot]
                for gp in range(2):
                    b = load_wks(slot * 256 + gp * 128, False)
                    for tt in range(NT):
                        ts = slice(tt * 512, (tt + 1) * 512)
                        q = 2 + (kv_ctr[0] % 2)
                        kv_ctr[0] += 1
                        for k in range(8):
                            P.op("pe", lambda e, q=q, k=k, b=b, ts=ts: e.matmul(PS[q][:, :], lhsT=WKS[b][:, k, :], rhs=XN3[:, k, ts], start=(k == 0), stop=(k == 7)),
                                 reads=[L_WKS[b], L_XN[k][tt]], writes=[L_PS[q]])
                        P.op("act", lambda e, q=q, ts=ts: e.activation(out=SRCT[:, ts], in_=PS[q][:, :], func=AF.Copy), reads=[L_PS[q]], writes=[L_KV["srct"]])
                    for gi in range(2):
                        g = 2 * gp + gi
                        rows = slice(64 * gi, 64 * gi + 64)
                        for l in range(32):
                            P.op("pe", lambda e, slot=slot, rows=rows, l=l: e.matmul(PS[4][:, 0:NCMP], lhsT=W1[slot][rows, l, :], rhs=SRCT[rows, l:l + 16 * (NCMP - 1) + 1:16],
                                                                                  start=(l == 0), stop=(l == 31)),
                                 reads=[L_KV[ln], L_KV["srct"]], writes=[L_PS[4]])
                        a_, b_ = TG[0][:, 0:NCMP], TG[1][:, 0:NCMP]
                        lt = L_KV["tg"]
                        P.op("act", lambda e, slot=slot, a_=a_: e.activation(out=a_, in_=PS[4][:, 0:NCMP], func=AF.Identity, bias=PW1[:, slot:slot + 1], scale=1.0),
                             reads=[L_PS[4], L_KV["pw1"]], writes=[lt])
                        P.op("act", lambda e, a_=a_, b_=b_: e.activation(out=b_, in_=a_, func=AF.Square), reads=[lt], writes=[lt])
                        P.op("dve", lambda e, b_=b_: e.tensor_scalar(b_, b_, 0.044715, 1.0, op0=ALU.mult, op1=ALU.add), reads=[lt], writes=[lt])
                        P.op("dve", lambda e, a_=a_, b_=b_: e.tensor_tensor(out=b_, in0=b_, in1=a_, op=ALU.mult), reads=[lt], writes=[lt])
                        P.op("act", lambda e, b_=b_: e.activation(out=b_, in_=b_, func=AF.Sigmoid, scale=1.5957691216057308), reads=[lt], writes=[lt])
                        P.op("dve", lambda e, a_=a_, b_=b_: e.tensor_tensor(out=HID[:, 0:NCMP], in0=a_, in1=b_, op=ALU.mult), reads=[lt], writes=[L_KV["hid"]])
                        if slot == 0:
                            P.op("pe", lambda e: e.matmul(PS[7][:, 0:NCMP], lhsT=W2K, rhs=HID[:, 0:NCMP], start=True, stop=True), reads=[L_KV["w2"], L_KV["hid"]], writes=[L_PS[7]])
                            headnorm(PS[7][:, 0:NCMP], L_PS[7], 3, KCS[:, g * 128:g * 128 + NCMP], [L_KV["kcs"]], SQK[:, 0:NCMP], L_KV["sqk"], RSTK[:, 0:NCMP], L_KV["rstk"], NCMP)
                        else:
                            P.op("pe", lambda e: e.matmul(PS[7][0:NCMP, 0:64], lhsT=HID[:, 0:NCMP], rhs=W2V, start=True, stop=True), reads=[L_KV["w2"], L_KV["hid"]], writes=[L_PS[7]])
                            P.op("act", lambda e, g=g: e.activation(out=VCS[0:NCMP, g, 0:64], in_=PS[7][0:NCMP, 0:64], func=AF.Copy), reads=[L_PS[7]], writes=[L_KV["vcs"]])
            P.dma("sp", lambda e: e.dma_start(out=KC_d, in_=KCS), L_KV["kcs"], reads=[L_KV["kcs"]], writes=[L_KCd])
            P.dma("sp", lambda e: e.dma_start(out=VC_d, in_=ARENA[:, 29824:29824 + 388]), L_KV["vcs"], reads=[L_KV["vcs"]], writes=[L_VCd])

        L_N = {n: LT("n_" + n) for n in "ksl kwin vsl vwin kc vc tcmp tsel twin selt wg sqq snb oacc pslc gates rstq sc top8 rden coef oc".split()}
        L_QT = lts(8, "qt")
        L_PT = lts(2, "pt")
        L_WQ = lts(2, "wq")
        L_WO = lts(2, "wo")
        ARENA_LTS.extend(list(L_N.values()) + L_QT + L_PT + L_WQ + L_WO)
        n_ctr = [0, 0, 0]

        def nsa_mixer():
            arena_barrier()
            QT = ARENA[:, 0:8 * S].rearrange("p (m t) -> p m t", m=8)
            KSLg = ARENA[:, 16384:16384 + S]
            KWINg = ARENA[:, 18432:18432 + S]
            VSLg = ARENA[:, 20480:20480 + NKT * 65].rearrange("p (t d) -> p t d", t=NKT)
            VWINg = ARENA[:, 21520:21520 + NKT * 65].rearrange("p (t d) -> p t d", t=NKT)
            KCg = ARENA[:, 22560:23072]
            VCg = ARENA[:, 23072:23072 + 388]
            TCMP = ARENA[:, 23464:23464 + S]
            TSEL = ARENA[:, 25512:25512 + 1152]
            TWIN = ARENA[:, 26664:26664 + 1408]
            PT = [ARENA[:, 28072 + i * 512:28072 + (i + 1) * 512] for i in range(2)]
            SELT = ARENA[0:32, 29096:29096 + S]
            WQ = [ARENA[:, 31144 + i * 1024:31144 + (i + 1) * 1024].rearrange("p (k n) -> p k n", k=8) for i in range(2)]
            WG = ARENA[:, 33192:33576].rearrange("p (k n) -> p k n", k=8)
            WO = [ARENA[:, 33576 + i * 1024:33576 + (i + 1) * 1024].rearrange("p (k n) -> p k n", k=8) for i in range(2)]
            SQQ = ARENA[:, 35624:36136]
            SNB = ARENA[:, 36136:36136 + NKT * 32]
            OACC = AF32[:, 18400:18400 + NKT * 64].rearrange("p (t d) -> p t d", t=NKT)
            PSLC = AF32[:, 19424:19424 + NKT * 32].rearrange("p (t j) -> p t j", t=NKT)
            GATES = AF32[:, 19936:19936 + NKT * 48].rearrange("p (t c) -> p t c", t=NKT)
            RSTQ = AF32[:, 20704:21216]
            SC = AF32[:, 21216:21216 + NKT * 32].rearrange("p (t j) -> p t j", t=NKT)
            TOP8 = AF32[:, 21728:21728 + NKT * 8]
            RDEN = AF32[:, 21856:21860]
            COEF = AF32[:, 21860:21864]
            OC = XN[:].rearrange("p (t c) -> p t c", t=NKT)
            L_OC = L_N["oc"]
            wqg = wqg_d[0].rearrange("(k p) n -> p k n", p=128)
            wo = wo_d[0].rearrange("(k p) n -> p k n", p=128)

            rmsnorm((1 * 3 + 1) * 8)
            P.dma("pool", lambda e: e.dma_start(out=WG[:, :, :], in_=wqg[:, :, 1024:1072]), L_N["wg"], writes=[L_N["wg"]])

            def load_wq(m):
                b = n_ctr[0] % 2
                n_ctr[0] += 1
                P.dma("pool", lambda e, b=b, m=m: e.dma_start(out=WQ[b][:, :, :], in_=wqg[:, :, m * 128:(m + 1) * 128]), L_WQ[b], writes=[L_WQ[b]])
                return b
            pend = [load_wq(0)]
            for m in range(8):
                b = pend.pop(0)
                if m + 1 < 8:
                    pend.append(load_wq(m + 1))
                for tt in range(NT):
                    ts = slice(tt * 512, (tt + 1) * 512)
                    q = n_ctr[1] % 2
                    n_ctr[1] += 1
                    for k in range(8):
                        P.op("pe", lambda e, q=q, k=k, b=b, ts=ts: e.matmul(PS[q][:, :], lhsT=WQ[b][:, k, :], rhs=XN3[:, k, ts], start=(k == 0), stop=(k == 7)),
                             reads=[L_WQ[b], L_XN[k][tt]], writes=[L_PS[q]])
                    headnorm(PS[q][:, :], L_PS[q], 0, QT[:, m, ts], [L_QT[m]], SQQ, L_N["sqq"], RSTQ, L_N["rstq"], 512)
            for T in range(NKT):
                tt = T // 4
                for k in range(8):
                    P.op("pe", lambda e, k=k, T=T: e.matmul(PS[4][:, 0:48], lhsT=XN3[:, k, T * 128:(T + 1) * 128], rhs=WG[:, k, :], start=(k == 0), stop=(k == 7)),
                         reads=[L_XN[k][tt], L_N["wg"]], writes=[L_PS[4]])
                P.op("act", lambda e, T=T: e.activation(out=GATES[:, T, :], in_=PS[4][:, 0:48], func=AF.Sigmoid), reads=[L_PS[4]], writes=[L_N["gates"]])
            P.op("dve", lambda e: e.memset(BARD[:], 0.0), writes=[t for r in L_XN for t in r] + [L_OC])
            P.dma("sp", lambda e: e.dma_start(out=KCg, in_=KC_d), L_N["kc"], reads=[L_KCd], writes=[L_N["kc"]])
            P.dma("sp", lambda e: e.dma_start(out=VCg, in_=VC_d), L_N["vc"], reads=[L_VCd], writes=[L_N["vc"]])
            VSd4 = VS_d.rearrange("p (t s d) -> p t s d", t=NKT, s=8)

            def sbank():
                q = n_ctr[1] % 2
                n_ctr[1] += 1
                return q

            def obank():
                q = 2 + n_ctr[2] % 2
                n_ctr[2] += 1
                return q

            def finalize(po, lpo, W, qg, h, br, mode):
                pv = po[:, 0:4 * W].rearrange("p (t w) -> p t w", t=4)
                T0 = 4 * qg
                P.op("dve", lambda e: e.tensor_scalar(RDEN, pv[:, :, 64], 1e-30, None, op0=ALU.max), reads=[lpo], writes=[L_N["rden"]])
                P.op("dve", lambda e: e.reciprocal(RDEN, RDEN), reads=[L_N["rden"]], writes=[L_N["rden"]])
                P.op("dve", lambda e: e.tensor_tensor(out=COEF, in0=RDEN, in1=GATES[:, T0:T0 + 4, h * 3 + br], op=ALU.mult), reads=[L_N["rden"], L_N["gates"]], writes=[L_N["coef"]])
                for qt in range(4):
                    T = T0 + qt
                    if mode == "oc":
                        P.op("dve", lambda e, qt=qt, T=T: e.tensor_scalar(OC[:, T, h * 64:(h + 1) * 64], pv[:, qt, 0:64], COEF[:, qt:qt + 1], None, op0=ALU.mult),
                             reads=[lpo, L_N["coef"]], writes=[L_OC])
                    elif mode == "set":
                        P.op("dve", lambda e, qt=qt, T=T: e.tensor_scalar(OACC[:, T, :], pv[:, qt, 0:64], COEF[:, qt:qt + 1], None, op0=ALU.mult),
                             reads=[lpo, L_N["coef"]], writes=[L_N["oacc"]])
                    else:
                        P.op("dve", lambda e, qt=qt, T=T: e.scalar_tensor_tensor(out=OACC[:, T, :], in0=pv[:, qt, 0:64], scalar=COEF[:, qt:qt + 1], in1=OACC[:, T, :], op0=ALU.mult, op1=ALU.add),
                             reads=[lpo, L_N["coef"], L_N["oacc"]], writes=[L_N["oacc"]])
                return pv

            for g in range(4):
                P.dma("sp", lambda e, g=g: e.dma_start(out=KSLg, in_=KSL_d[g]), L_N["ksl"], reads=[L_KSLd[g]], writes=[L_N["ksl"]])
                P.dma("sp", lambda e, g=g: e.dma_start(out=KWINg, in_=KWIN_d[g]), L_N["kwin"], reads=[L_KWINd[g]], writes=[L_N["kwin"]])
                P.dma("sp", lambda e, g=g: e.dma_start(out=VSLg, in_=VSd4[:, :, g, :]), L_N["vsl"], reads=[L_VSd], writes=[L_N["vsl"]])
                P.dma("sp", lambda e, g=g: e.dma_start(out=VWINg, in_=VSd4[:, :, 4 + g, :]), L_N["vwin"], reads=[L_VSd], writes=[L_N["vwin"]])
                for r in range(4):
                    h = 4 * g + r
                    m, rows = h // 2, slice(64 * (h % 2), 64 * (h % 2) + 64)
                    P.dma("sp", lambda e, h=h: e.dma_start(out=TCMP, in_=bass.AP(BV_d.tensor, h * 4096 + OFFB - 2063, [[16, 128], [1, S]])), L_N["tcmp"], reads=[L_BVd], writes=[L_N["tcmp"]])
                    csteps = []
                    for qg in range(NQG):
                        qs = slice(qg * 512, (qg + 1) * 512)
                        sq_, oq = sbank(), obank()

                        def score(sq_=sq_, qs=qs, rows=rows, m=m, g=g):
                            P.op("pe", lambda e: e.matmul(PS[sq_][:, :], lhsT=KCg[rows, g * 128:(g + 1) * 128], rhs=QT[rows, m, qs], start=True, stop=False),
                                 reads=[L_N["kc"], L_QT[m]], writes=[L_PS[sq_]])
                            P.op("pe", lambda e: e.matmul(PS[sq_][:, :], lhsT=JM, rhs=TCMP[:, qs], start=False, stop=True), reads=[L_NCB, L_N["tcmp"]], writes=[L_PS[sq_]])
                            P.op("act", lambda e: e.activation(out=PT[sq_], in_=PS[sq_][:, :], func=AF.Exp, scale=0.125), reads=[L_PS[sq_]], writes=[L_PT[sq_]])

                        def pv_(sq_=sq_, oq=oq, qg=qg, g=g, h=h, r=r):
                            for qt in range(4):
                                P.op("pe", lambda e, qt=qt: e.matmul(PS[oq][:, qt * 97:(qt + 1) * 97], lhsT=PT[sq_][:, qt * 128:(qt + 1) * 128], rhs=VCg[:, g * 97:(g + 1) * 97], start=True, stop=True),
                                     reads=[L_PT[sq_], L_N["vc"]], writes=[L_PS[oq]])
                            pv = finalize(PS[oq], L_PS[oq], 97, qg, h, 0, "oc")
                            for qt in range(4):
                                T = 4 * qg + qt
                                if r == 0:
                                    P.op("dve", lambda e, qt=qt, T=T: e.tensor_scalar(PSLC[:, T, :], pv[:, qt, 65:97], RDEN[:, qt:qt + 1], None, op0=ALU.mult),
                                         reads=[L_PS[oq], L_N["rden"]], writes=[L_N["pslc"]])
                                else:
                                    P.op("dve", lambda e, qt=qt, T=T: e.scalar_tensor_tensor(out=PSLC[:, T, :], in0=pv[:, qt, 65:97], scalar=RDEN[:, qt:qt + 1], in1=PSLC[:, T, :], op0=ALU.mult, op1=ALU.add),
                                         reads=[L_PS[oq], L_N["rden"], L_N["pslc"]], writes=[L_N["pslc"]])
                        csteps.append((score, pv_))
                    for i, (sc_f, pv_f) in enumerate(csteps):
                        if i == 0:
                            sc_f()
                        if i + 1 < len(csteps):
                            csteps[i + 1][0]()
                        pv_f()
                SCf = AF32[:, 21216:21216 + NKT * 32]
                PSLCf = AF32[:, 19424:19424 + NKT * 32]
                P.op("dve", lambda e: e.tensor_tensor(out=SCf, in0=PSLCf, in1=SELTAB[:, 0:NKT * 32], op=ALU.mult), reads=[L_N["pslc"], L_SELTAB], writes=[L_N["sc"]])
                P.op("dve", lambda e: e.tensor_tensor(out=SCf, in0=SCf, in1=SELTAB[:, 512:512 + NKT * 32], op=ALU.add), reads=[L_N["sc"], L_SELTAB], writes=[L_N["sc"]])
                for T in range(NKT):
                    P.op("dve", lambda e, T=T: e.max(TOP8[:, T * 8:(T + 1) * 8], SC[:, T, :]), reads=[L_N["sc"]], writes=[L_N["top8"]])
                for T in range(NKT):
                    P.op("dve", lambda e, T=T: e.tensor_scalar(SC[:, T, :], SC[:, T, :], TOP8[:, T * 8 + 7:T * 8 + 8], None, op0=ALU.is_ge), reads=[L_N["sc"], L_N["top8"]], writes=[L_N["sc"]])
                P.op("dve", lambda e: e.tensor_scalar(SNB, SCf, -1.0, 30000.0, op0=ALU.add, op1=ALU.mult), reads=[L_N["sc"]], writes=[L_N["snb"]])
                PSB = PS[6][:, :].bitcast(BF16)
                for T4 in range(NKT // 4):
                    for ti in range(4):
                        T = T4 * 4 + ti
                        P.op("pe", lambda e, T=T, ti=ti: e.transpose(PSB[0:32, ti * 128:(ti + 1) * 128], SNB[:, T * 32:(T + 1) * 32], IDB), reads=[L_N["snb"], L_NCB], writes=[L_PS[6]])
                    P.op("act", lambda e, T4=T4: e.activation(out=SELT[:, T4 * 512:(T4 + 1) * 512], in_=PSB[0:32, 0:512], func=AF.Copy), reads=[L_PS[6]], writes=[L_N["selt"]])
                for r in range(4):
                    h = 4 * g + r
                    m, rows = h // 2, slice(64 * (h % 2), 64 * (h % 2) + 64)
                    P.dma("sp", lambda e, h=h: e.dma_start(out=TSEL, in_=bass.AP(BV_d.tensor, h * 4096 + OFFB - 511, [[1, 128], [1, 1152]])), L_N["tsel"], reads=[L_BVd], writes=[L_N["tsel"]])
                    P.dma("sp", lambda e, h=h: e.dma_start(out=TWIN, in_=bass.AP(BV_d.tensor, (16 + h) * 4096 + OFFB - 511, [[1, 128], [1, 1408]])), L_N["twin"], reads=[L_BVd], writes=[L_N["twin"]])
                    steps = []
                    for br, Kg, lK, Vg, lV, TB, lT in ((1, KSLg, L_N["ksl"], VSLg, L_N["vsl"], TSEL, L_N["tsel"]), (2, KWINg, L_N["kwin"], VWINg, L_N["vwin"], TWIN, L_N["twin"])):
                        for qg in range(NQG):
                            qs = slice(qg * 512, (qg + 1) * 512)
                            oq = obank()
                            kt_lo = 0 if br == 1 else max(0, 4 * qg - 4)
                            bank_used = [False]
                            for kt in range(kt_lo, 4 * qg + 4):
                                dl = 4 * qg - kt
                                col0 = 128 * ((min(dl, 2) if br == 1 else dl) + 3)
                                sq_ = sbank()
                                ks = slice(kt * 128, (kt + 1) * 128)

                                def score(sq_=sq_, rows=rows, m=m, qs=qs, ks=ks, Kg=Kg, lK=lK, col0=col0, TB=TB, lT=lT, br=br):
                                    P.op("pe", lambda e: e.matmul(PS[sq_][:, :], lhsT=Kg[rows, ks], rhs=QT[rows, m, qs], start=True, stop=False),
                                         reads=[lK, L_QT[m]], writes=[L_PS[sq_]])
                                    P.op("pe", lambda e: e.matmul(PS[sq_][:, :], lhsT=JM, rhs=TB[:, col0:col0 + 512], start=False, stop=(br == 2)),
                                         reads=[L_NCB, lT], writes=[L_PS[sq_]])
                                    if br == 1:
                                        P.op("pe", lambda e: e.matmul(PS[sq_][:, :], lhsT=EBLK[0:32, ks], rhs=SELT[:, qs], start=False, stop=True),
                                             reads=[L_EBLK, L_N["selt"]], writes=[L_PS[sq_]])
                                    P.op("act", lambda e: e.activation(out=PT[sq_], in_=PS[sq_][:, :], func=AF.Exp, scale=0.125), reads=[L_PS[sq_]], writes=[L_PT[sq_]])

                                pvl = []
                                for qt in range(4):
                                    T = 4 * qg + qt
                                    if T < kt or (br == 2 and T > kt + 4):
                                        continue
                                    first = not bank_used[0]
                                    bank_used[0] = True
                                    pvl.append((qt, first, kt == T))
                                last = (kt == 4 * qg + 3)

                                def pv_(sq_=sq_, oq=oq, kt=kt, Vg=Vg, lV=lV, pvl=pvl, last=last, qg=qg, h=h, br=br):
                                    for qt, first, stop_ in pvl:
                                        P.op("pe", lambda e, qt=qt, first=first, stop_=stop_: e.matmul(PS[oq][:, qt * 65:(qt + 1) * 65], lhsT=PT[sq_][:, qt * 128:(qt + 1) * 128], rhs=Vg[:, kt, :],
                                                                                               start=first, stop=stop_, skip_group_check=True),
                                             reads=[L_PT[sq_], lV], writes=[L_PS[oq]])
                                    if last:
                                        finalize(PS[oq], L_PS[oq], 65, qg, h, br, "set" if br == 1 else "add")
                                steps.append((score, pv_))
                    for i, (sc_f, pv_f) in enumerate(steps):
                        if i == 0:
                            sc_f()
                        if i + 1 < len(steps):
                            steps[i + 1][0]()
                        pv_f()
                    for qg in range(NQG):
                        T0 = 4 * qg
                        P.op("dve", lambda e, T0=T0, h=h: e.tensor_tensor(out=OC[:, T0:T0 + 4, h * 64:(h + 1) * 64], in0=OACC[:, T0:T0 + 4, :], in1=OC[:, T0:T0 + 4, h * 64:(h + 1) * 64], op=ALU.add),
                             reads=[L_N["oacc"], L_OC], writes=[L_OC])
            OT = QT
            PSB = PS[6][:, :].bitcast(BF16)
            PSB2 = PS[7][:, :].bitcast(BF16)
            for T in range(NKT):
                for mg in range(2):
                    pb, lpb = (PSB, L_PS[6]) if mg == 0 else (PSB2, L_PS[7])
                    for mi in range(4):
                        mm_ = mg * 4 + mi
                        P.op("pe", lambda e, pb=pb, mi=mi, T=T, mm_=mm_: e.transpose(pb[:, mi * 128:(mi + 1) * 128], OC[:, T, mm_ * 128:(mm_ + 1) * 128], IDB), reads=[L_OC, L_NCB], writes=[lpb])
                    eng = "act" if mg == 0 else "dve"
                    if mg == 0:
                        P.op("act", lambda e, pb=pb, T=T, mg=mg: e.activation(out=OT[:, mg * 4:(mg + 1) * 4, T * 128:(T + 1) * 128], in_=pb[:, 0:512].rearrange("p (m t) -> p m t", m=4), func=AF.Copy),
                             reads=[lpb], writes=L_QT[mg * 4:(mg + 1) * 4])
                    else:
                        P.op("dve", lambda e, pb=pb, T=T, mg=mg: e.tensor_copy(OT[:, mg * 4:(mg + 1) * 4, T * 128:(T + 1) * 128], pb[:, 0:512].rearrange("p (m t) -> p m t", m=4)),
                             reads=[lpb], writes=L_QT[mg * 4:(mg + 1) * 4])

            def load_wo(mo):
                b = n_ctr[0] % 2
                n_ctr[0] += 1
                P.dma("pool", lambda e, b=b, mo=mo: e.dma_start(out=WO[b][:, :, :], in_=wo[:, :, mo * 128:(mo + 1) * 128]), L_WO[b], writes=[L_WO[b]])
                return b
            pend = [load_wo(0)]
            for mo in range(8):
                b = pend.pop(0)
                if mo + 1 < 8:
                    pend.append(load_wo(mo + 1))
                for tt in range(NT):
                    ts = slice(tt * 512, (tt + 1) * 512)
                    q = sbank()
                    for k in range(8):
                        P.op("pe", lambda e, q=q, k=k, b=b, ts=ts: e.matmul(PS[q][:, :], lhsT=WO[b][:, k, :], rhs=OT[:, k, ts], start=(k == 0), stop=(k == 7)),
                             reads=[L_WO[b], L_QT[k]], writes=[L_PS[q]])
                    P.op("dve", lambda e, q=q, mo=mo, ts=ts: e.tensor_tensor(out=H3[:, mo, ts], in0=H3[:, mo, ts], in1=PS[q][:, :], op=ALU.add),
                         reads=[L_PS[q], L_H[mo][tt]], writes=[L_H[mo][tt]])
            P.op("dve", lambda e: e.memset(BARD[:], 0.0), writes=[t for r in L_XN for t in r] + [L_OC])

        if "nsa" in parts:
            nsa_setup()
        for s in range(NSEQ):
            arena_barrier()
            for t128 in range(S // 128):
                b = t128 % 2
                tt = t128 // 4
                row0 = s * S + t128 * 128
                P.dma("sp", lambda e, b=b, row0=row0: e.dma_start(out=XIN[b], in_=x_d[row0:row0 + 128, :]), L_XIN[b], writes=[L_XIN[b]])
                for cg in range(2):
                    pq = 6 + cg
                    for ci in range(4):
                        c = cg * 4 + ci
                        P.op("pe", lambda e, b=b, c=c, ci=ci, pq=pq: e.transpose(PS[pq][:, ci * 128:(ci + 1) * 128], XIN[b][:, c * 128:(c + 1) * 128], IDF[:]),
                             reads=[L_XIN[b], L_IDF], writes=[L_PS[pq]])
                    P.op("act" if cg == 0 else "dve",
                         (lambda e, cg=cg, pq=pq, t128=t128: e.activation(out=H3[:, cg * 4:(cg + 1) * 4, t128 * 128:(t128 + 1) * 128], in_=PS[pq][:, :].rearrange("p (c t) -> p c t", c=4), func=AF.Copy)) if cg == 0 else
                         (lambda e, cg=cg, pq=pq, t128=t128: e.tensor_copy(H3[:, cg * 4:(cg + 1) * 4, t128 * 128:(t128 + 1) * 128], PS[pq][:, :].rearrange("p (c t) -> p c t", c=4))),
                         reads=[L_PS[pq]], writes=[L_H[cg * 4 + ci][tt] for ci in range(4)])

            for layer in range(2):
                rmsnorm((layer * 3 + 0) * 8)
                ffn(0, layer)
                if layer == 0 and "s5" in parts:
                    s5_mixer()
                if layer == 1 and "nsa" in parts:
                    nsa_mixer()
                rmsnorm((layer * 3 + 2) * 8)
                ffn(1, layer)
                if layer == 0 and "nsa" in parts:
                    kv_phase()

            arena_barrier()
            for t128 in range(S // 128):
                b = t128 % 2
                tt = t128 // 4
                row0 = s * S + t128 * 128
                for cg in range(2):
                    pq = 6 + cg
                    for ci in range(4):
                        c = cg * 4 + ci
                        P.op("pe", lambda e, c=c, ci=ci, pq=pq, t128=t128: e.transpose(PS[pq][:, ci * 128:(ci + 1) * 128], H3[:, c, t128 * 128:(t128 + 1) * 128], IDF[:]),
                             reads=[L_H[c][tt], L_IDF], writes=[L_PS[pq]])
                    P.op("act" if cg == 0 else "dve",
                         (lambda e, b=b, cg=cg, pq=pq: e.activation(out=XIN[b][:, cg * 512:(cg + 1) * 512], in_=PS[pq][:, :], func=AF.Copy)) if cg == 0 else
                         (lambda e, b=b, cg=cg, pq=pq: e.tensor_copy(XIN[b][:, cg * 512:(cg + 1) * 512], PS[pq][:, :])),
                         reads=[L_PS[pq]], writes=[L_XIN[b]])
                P.dma("sp", lambda e, b=b, row0=row0: e.dma_start(out=y_d[row0:row0 + 128, :], in_=XIN[b]), L_XIN[b], reads=[L_XIN[b]], writes=[L_Y])

        P.wait_all_dma("sp")
        P.emit()
    return nc


def prep_inputs(inp):
    g = np.stack([inp["ffn1_norm"], inp["mix_norm"], inp["ffn2_norm"]], axis=1)
    gains = np.ascontiguousarray(g.reshape(2, 3, 8, 128).transpose(3, 0, 1, 2).reshape(128, 48)).astype(np.float32)
    gains = np.ascontiguousarray(np.concatenate([gains, inp["kv_norm"].reshape(8, 128).T], axis=1)).astype(np.float32)
    m = dict(host_consts())
    m["gains"] = gains
    a_re, a_im, ldt = inp["s5_a_re"][0], inp["s5_a_im"][0], inp["s5_log_dt"][0]
    b_re, b_im, c_re, c_im = inp["s5_b_re"][0], inp["s5_b_im"][0], inp["s5_c_re"][0], inp["s5_c_im"][0]
    pl, gl, hp, kc, nn = np.meshgrid(np.arange(4), np.arange(2), np.arange(16), np.arange(8), np.arange(64), indexing="ij")
    gg = 8 * kc + 2 * pl + gl
    R = np.stack([a_re[gg, nn], a_im[gg, nn], ldt[gg], b_re[gg, nn, hp], b_im[gg, nn, hp]], axis=0)
    m["s5R"] = np.ascontiguousarray(R.reshape(5, 128, 512).transpose(1, 0, 2).reshape(128, 2560)).astype(np.float32)
    gl2, n2, pr2 = np.meshgrid(np.arange(2), np.arange(64), np.arange(32), indexing="ij")
    g2 = 2 * pr2 + gl2
    S3 = np.stack([a_re[g2, n2], a_im[g2, n2], ldt[g2]], axis=0).reshape(3, 128, 32).transpose(1, 0, 2).reshape(128, 96)
    gl3, n3, pr3, h3 = np.meshgrid(np.arange(2), np.arange(64), np.arange(32), np.arange(16), indexing="ij")
    g3 = 2 * pr3 + gl3
    C2 = np.stack([c_re[g3, h3, n3], c_im[g3, h3, n3]], axis=0).reshape(2, 128, 512).transpose(1, 0, 2).reshape(128, 1024)
    m["s5S"] = np.ascontiguousarray(np.concatenate([S3, C2], axis=1)).astype(np.float32)
    p = np.arange(128)
    maskR = np.stack([((p // 16) % 2 == 0), ((p // 16) % 2 == 1)], axis=1).astype(np.float32)
    dsk = inp["s5_d"][0].reshape(8, 128).T
    m["s5M"] = np.ascontiguousarray(np.concatenate([maskR, dsk], axis=1)).astype(np.float32)
    m["s5_w_glu"] = np.ascontiguousarray(inp["s5_w_glu"], dtype=np.float32)
    for k in ("w_kv", "w_qg", "w_o", "cmp_k_w1", "cmp_v_w1", "cmp_k_w2", "cmp_v_w2", "rel_bias"):
        m[k] = np.ascontiguousarray(inp[k], dtype=np.float32)
    m["posT_k"] = np.ascontiguousarray(inp["cmp_pos_k"].T).astype(np.float32)
    m["posT_v"] = np.ascontiguousarray(inp["cmp_pos_v"].T).astype(np.float32)
    dup = lambda v: np.concatenate([v, v])
    ngn = np.zeros((128, 8), np.float32)
    ngn[:, 0] = dup(inp["q_norm"][0])
    ngn[:, 1] = dup(inp["k_norm_slc"])
    ngn[:, 2] = dup(inp["k_norm_win"])
    ngn[:, 3] = dup(inp["k_norm_cmp"])
    m["ngn"] = ngn
    for k in ("ffn1_w_in", "ffn2_w_in", "ffn1_w_out", "ffn2_w_out"):
        m[k] = np.ascontiguousarray(inp[k], dtype=np.float32)
    return m


_NC_CACHE = {}


def kernel(**inp):
    x = np.ascontiguousarray(inp["x"], dtype=np.float32)
    B, S, Dm = x.shape
    n = 8
    per = B // n
    shared = prep_inputs(inp)
    key = (per, S)
    if key not in _NC_CACHE:
        _NC_CACHE[key] = build(NSEQ=per, S=S, parts=("ffn", "s5", "nsa"))
    nc = _NC_CACHE[key]
    in_maps = []
    for i in range(n):
        m = dict(shared)
        m["x"] = x[i * per:(i + 1) * per].reshape(per * S, Dm)
        in_maps.append(m)
    res = run_bass_kernel_spmd(nc, in_maps, core_ids=list(range(n)))
    out = np.concatenate([r["y"].reshape(per, S, Dm) for r in res.results], axis=0)
    return out.astype(np.float32)
```

```python
import contextlib
import math
import numpy as np
import concourse.bass as bass
import concourse.mybir as mybir
from concourse.bass_utils import run_bass_kernel_spmd

F32 = mybir.dt.float32
BF16 = mybir.dt.bfloat16
AF = mybir.ActivationFunctionType
ALU = mybir.AluOpType

ENGS = ("pe", "act", "dve", "pool", "sp")

D = 1024
DFF = 2816
NPAIR = DFF // 128
RMS_EPS = 1e-6


class LT:
    __slots__ = ("name", "w", "rs", "sem")

    def __init__(self, name=""):
        self.name = name
        self.w = None
        self.rs = {}
        self.sem = None


def lts(n, name=""):
    return [LT(name + str(i)) for i in range(n)]


class Prog:
    def __init__(self, nc):
        self.nc = nc
        self.ops = {e: [] for e in ENGS}
        self.dma_cnt = {}
        self.n_dma_sems = 0
        self.final_waits = {}

    def _deps(self, eng, reads, writes):
        deps = {}

        def add(ev):
            if ev is None:
                return
            k, v = ev
            if k == eng and eng == "pe":
                return
            if deps.get(k, -1) < v:
                deps[k] = v

        for t in reads:
            add(t.w)
        for t in writes:
            add(t.w)
            for k, v in t.rs.items():
                add((k, v))
        return deps

    def _commit(self, ev, reads, writes):
        k, v = ev
        for t in reads:
            if t.rs.get(k, -1) < v:
                t.rs[k] = v
        for t in writes:
            t.w = ev
            t.rs = {}

    def op(self, eng, fn, reads=(), writes=()):
        deps = self._deps(eng, reads, writes)
        idx = len(self.ops[eng])
        self.ops[eng].append({"fn": fn, "deps": deps, "sig": False, "dma": None})
        self._commit((eng, idx), reads, writes)

    def dma(self, eng, fn, home, reads=(), writes=()):
        deps = self._deps(eng, reads, writes)
        if home.sem is None:
            home.sem = {}
        if eng not in home.sem:
            home.sem[eng] = self.n_dma_sems
            self.n_dma_sems += 1
            self.dma_cnt[home.sem[eng]] = 0
        sid = home.sem[eng]
        self.dma_cnt[sid] += 16
        ev = (("d", sid), self.dma_cnt[sid])
        self.ops[eng].append({"fn": fn, "deps": deps, "sig": False, "dma": ev})
        self._commit(ev, reads, writes)
        return ev

    def wait_all_dma(self, eng="sp"):
        self.final_waits[eng] = dict(self.dma_cnt)

    def emit(self):
        nc = self.nc
        for e in ENGS:
            for o in self.ops[e]:
                for k, v in o["deps"].items():
                    if isinstance(k, str):
                        self.ops[k][v]["sig"] = True
        sigcount = {}
        for e in ENGS:
            c = 0
            arr = []
            for o in self.ops[e]:
                if o["sig"]:
                    c += 1
                arr.append(c)
            sigcount[e] = arr
        with contextlib.ExitStack() as st:
            esem = {e: st.enter_context(nc.semaphore("s_" + e)) for e in ENGS}
            dsem = [st.enter_context(nc.semaphore("d%d" % i)) for i in range(self.n_dma_sems)]
            block = st.enter_context(nc.Block())

            def run(e, eng):
                waited = {}
                for o in self.ops[e]:
                    for k, v in o["deps"].items():
                        if isinstance(k, str):
                            sem, val = esem[k], sigcount[k][v]
                        else:
                            sem, val = dsem[k[1]], v
                        if waited.get(k, -1) >= val:
                            continue
                        waited[k] = val
                        eng.wait_ge(sem, val)
                    inst = o["fn"](eng)
                    if o["dma"] is not None:
                        inst.then_inc(dsem[o["dma"][0][1]], 16)
                    elif o["sig"]:
                        inst.then_inc(esem[e], 1)
                if e in self.final_waits:
                    for s, v in self.final_waits[e].items():
                        eng.wait_ge(dsem[s], v)

            @block.tensor
            def _(eng):
                run("pe", eng)

            @block.scalar
            def _(eng):
                run("act", eng)

            @block.vector
            def _(eng):
                run("dve", eng)

            @block.gpsimd
            def _(eng):
                run("pool", eng)

            @block.sync
            def _(eng):
                run("sp", eng)


def _rel_bucket_np(dist):
    n = np.maximum(dist, 0)
    logv = np.log(np.maximum(n, 1).astype(np.float32) / np.float32(16)) / np.float32(math.log(8.0))
    large = np.minimum(16 + (logv * np.float32(16)).astype(np.int32), 31)
    return np.where(n < 16, n, large)


def host_consts():
    c = {}
    c["ident_f"] = np.eye(128, dtype=np.float32)
    c["ones_d"] = np.full((128, 128), 1.0 / D, dtype=np.float32)
    blk = np.zeros((128, 128), np.float32)
    blk[:64, :64] = 1.0 / 64
    blk[64:, 64:] = 1.0 / 64
    c["nsa_consts"] = np.concatenate([np.eye(128, dtype=np.float32), np.eye(128, dtype=np.float32)[::-1], blk], axis=1)
    x = np.arange(4096)
    dist = x - 2064
    bucket = _rel_bucket_np(dist)
    oha = np.zeros((2, 33, 4096), np.float32)
    for v in range(2):
        valid = (dist >= 0) & ((dist < 512) if v == 1 else True)
        oha[v, bucket[valid], x[valid]] = 1.0
        oha[v, 32, x[~valid]] = 1.0
    c["oha"] = oha
    eb = np.zeros((32, 2048), np.float32)
    k = np.arange(2048)
    eb[k // 64, k] = 1.0
    c["eblk"] = eb
    p_, T_, j_ = np.meshgrid(np.arange(128), np.arange(16), np.arange(32), indexing="ij")
    t_ = 128 * T_ + p_
    cur = t_ // 64
    forced = (j_ == 0) | (j_ == cur) | (j_ == cur - 1)
    causal = j_ * 64 <= t_
    caus01 = (causal & ~forced).astype(np.float32)
    addm = np.where(forced, 1e9, np.where(causal, 0.0, -1e9)).astype(np.float32)
    c["seltab"] = np.concatenate([caus01.reshape(128, 512), addm.reshape(128, 512)], axis=1)
    cs = np.arange(128)[:, None] * 16
    js = np.arange(32)[None, :] * 64
    ov = np.clip(np.minimum(cs + 32, js + 64) - np.maximum(cs, js), 0, None).astype(np.float32) / 32.0
    ovm = np.concatenate([np.ones((128, 1), np.float32), ov], axis=1)
    ovm[127, :] = 0.0
    c["ovm"] = ovm
    return c


def build(NSEQ=4, S=2048, parts=("ffn",)):
    NT = S // 512
    nc = bass.Bass("TRN2", target_bir_lowering=False)

    def din(name, shape, dt=F32):
        return nc.dram_tensor(name, list(shape), dt, kind="ExternalInput").ap()

    x_d = din("x", [NSEQ * S, D])
    y_d = nc.dram_tensor("y", [NSEQ * S, D], F32, kind="ExternalOutput").ap()
    w_in_d = [din("ffn1_w_in", [2, D, 2 * DFF]), din("ffn2_w_in", [2, D, 2 * DFF])]
    w_out_d = [din("ffn1_w_out", [2, DFF, D]), din("ffn2_w_out", [2, DFF, D])]
    gains_d = din("gains", [128, 7 * 8])
    ident_d = din("ident_f", [128, 128])
    s5R_d = din("s5R", [128, 5 * 512])
    s5S_d = din("s5S", [128, 3 * 32 + 2 * 512])
    s5M_d = din("s5M", [128, 2 + 8])
    wglu_d = din("s5_w_glu", [1, D, D])
    wkv_d = din("w_kv", [D, 1536])
    wqg_d = din("w_qg", [1, D, 1072])
    wo_d = din("w_o", [1, D, D])
    w1k_d = din("cmp_k_w1", [32, 64, 128])
    w1v_d = din("cmp_v_w1", [32, 64, 128])
    w2k_d = din("cmp_k_w2", [128, 64])
    w2v_d = din("cmp_v_w2", [128, 64])
    posk_d = din("posT_k", [64, 32])
    posv_d = din("posT_v", [64, 32])
    relb_d = din("rel_bias", [32, 16])
    oha_d = din("oha", [2, 33, 4096])
    nsac_d = din("nsa_consts", [128, 384])
    eblk_d = din("eblk", [32, 2048])
    seltab_d = din("seltab", [128, 1024])
    ngn_d = din("ngn", [128, 8])
    ovm_d = din("ovm", [128, 33])
    ones_d = din("ones_d", [128, 128])

    st = contextlib.ExitStack()
    with st:
        def sb(name, shape, dt):
            return st.enter_context(nc.sbuf_tensor(name, list(shape), dt))

        H = sb("H", [128, 8 * S], F32)
        XN = sb("XN", [128, 8 * S], BF16)
        ARENA = sb("ARENA", [128, 45056], BF16)
        IDF = sb("IDF", [128, 128], F32)
        ONES = sb("ONES", [128, 128], BF16)
        GAINS = sb("GAINS", [128, 56], F32)
        SQ = sb("SQ", [128, 2 * 512], BF16)
        NCB = sb("NCB", [128, 3 * 128], BF16)
        EBLK = sb("EBLK", [32, 2048], BF16)
        SELTAB = sb("SELTAB", [128, 1024], F32)
        NGN = sb("NGN", [128, 8], F32)
        RSTD = sb("RSTD", [128, 512], F32)
        EPSC = sb("EPSC", [128, 1], F32)
        AF32 = ARENA[:].bitcast(F32)
        XIN = [AF32[:, 20480 + i * 1024:20480 + (i + 1) * 1024] for i in range(2)]
        SILU = [ARENA[:, 31488 + i * 512:31488 + (i + 1) * 512] for i in range(2)]
        PS = [st.enter_context(nc.psum_tensor("PS%d" % i, [128, 512], F32)) for i in range(8)]

        H3 = H[:].rearrange("p (c t) -> p c t", c=8)
        XN3 = XN[:].rearrange("p (c t) -> p c t", c=8)

        P = Prog(nc)
        L_H = [[LT("h%d_%d" % (c, t)) for t in range(NT)] for c in range(8)]
        L_XN = [[LT("xn%d_%d" % (c, t)) for t in range(NT)] for c in range(8)]
        L_PS = lts(8, "ps")
        L_IDF, L_ONES, L_GAINS, L_RSTD, L_NCB, L_EBLK, L_SELTAB, L_NGN = [LT(n) for n in "idf ones gains rstd ncb eblk seltab ngn".split()]
        L_SQ = lts(2, "sq")
        L_XIN = lts(2, "xin")
        L_SILU = lts(2, "silu")
        L_Y = LT("ydram")

        P.dma("sp", lambda e: e.dma_start(out=IDF[:], in_=ident_d), L_IDF, writes=[L_IDF])
        P.dma("pool", lambda e: e.dma_start(out=ONES[:], in_=ones_d), L_ONES, writes=[L_ONES])
        P.dma("sp", lambda e: e.dma_start(out=GAINS[:], in_=gains_d), L_GAINS, writes=[L_GAINS])

        L_EPS = LT("eps")
        P.op("dve", lambda e: e.memset(EPSC[:], RMS_EPS), writes=[L_EPS])

        G_OFF = 0
        G3 = ARENA[:, G_OFF:G_OFF + 11 * S].rearrange("p (c t) -> p c t", c=11)
        WIN_OFF = 11 * 2048
        WIN = [ARENA[:, WIN_OFF + i * 2048: WIN_OFF + (i + 1) * 2048].rearrange("p (k n) -> p k n", k=8) for i in range(3)]
        WOUT_OFF = WIN_OFF + 3 * 2048
        WOUT = [ARENA[:, WOUT_OFF + i * 1408: WOUT_OFF + (i + 1) * 1408].rearrange("p (k n) -> p k n", k=11) for i in range(2)]
        L_G = [[LT("g%d_%d" % (c, t)) for t in range(NT)] for c in range(11)]
        L_WIN = lts(3, "win")
        L_WOUT = lts(2, "wout")

        ARENA_LTS = []
        BARD = sb("BARD", [128, 1], F32)

        def arena_barrier():
            P.op("dve", lambda e: e.memset(BARD[:], 0.0), writes=ARENA_LTS)

        def rmsnorm(gcol):
            for tt in range(NT):
                ts = slice(tt * 512, (tt + 1) * 512)
                for c in range(8):
                    b = c % 2
                    P.op("act", lambda e, c=c, ts=ts, b=b: e.activation(out=SQ[:, b * 512:(b + 1) * 512], in_=H3[:, c, ts], func=AF.Square),
                         reads=[L_H[c][tt]], writes=[L_SQ[b]])
                    P.op("pe", lambda e, c=c, b=b: e.matmul(PS[6][:, :], lhsT=ONES[:], rhs=SQ[:, b * 512:(b + 1) * 512], start=(c == 0), stop=(c == 7)),
                         reads=[L_ONES, L_SQ[b]], writes=[L_PS[6]])
                P.op("act", lambda e: e.activation(out=RSTD[:], in_=PS[6][:, :], func=AF.Sqrt, bias=EPSC[:, 0:1], scale=1.0),
                     reads=[L_PS[6], L_EPS], writes=[L_RSTD])
                P.op("dve", lambda e: e.reciprocal(RSTD[:], RSTD[:]), reads=[L_RSTD], writes=[L_RSTD])
                for c in range(8):
                    P.op("dve", lambda e, c=c, ts=ts: e.scalar_tensor_tensor(out=XN3[:, c, ts], in0=H3[:, c, ts], scalar=GAINS[:, gcol + c:gcol + c + 1],
                                                                            in1=RSTD[:], op0=ALU.mult, op1=ALU.mult),
                         reads=[L_H[c][tt], L_RSTD, L_GAINS], writes=[L_XN[c][tt]])

        ffn_ctr = [0, 0, 0]

        ARENA_LTS.extend([t for r in L_G for t in r] + L_WIN + L_WOUT + L_XIN + L_SILU)

        def ffn(which, layer):
            arena_barrier()
            w_in = w_in_d[which][layer].rearrange("(k p) n -> p k n", p=128)
            w_out = w_out_d[which][layer].rearrange("(k p) n -> p k n", p=128)

            def load_win(i):
                b = ffn_ctr[0] % 3
                ffn_ctr[0] += 1
                P.dma("pool", lambda e, b=b, i=i: e.dma_start(out=WIN[b][:, :, 0:128], in_=w_in[:, :, i * 128:(i + 1) * 128]), L_WIN[b], writes=[L_WIN[b]])
                P.dma("pool", lambda e, b=b, i=i: e.dma_start(out=WIN[b][:, :, 128:256], in_=w_in[:, :, DFF + i * 128:DFF + (i + 1) * 128]), L_WIN[b], writes=[L_WIN[b]])
                return b

            def load_wout(hh, m):
                b = ffn_ctr[1] % 2
                ffn_ctr[1] += 1
                P.dma("pool", lambda e, b=b: e.dma_start(out=WOUT[b][:, :, :], in_=w_out[:, hh * 11:(hh + 1) * 11, m * 128:(m + 1) * 128]), L_WOUT[b], writes=[L_WOUT[b]])
                return b

            for hh in range(2):
                pend = [load_win(hh * 11 + 0), load_win(hh * 11 + 1)]
                for il in range(11):
                    b = pend.pop(0)
                    if il + 2 < 11:
                        pend.append(load_win(hh * 11 + il + 2))
                    for tt in range(NT):
                        ts = slice(tt * 512, (tt + 1) * 512)
                        q = ffn_ctr[2] % 2
                        ffn_ctr[2] += 1
                        pa, pb = PS[2 * q], PS[2 * q + 1]
                        for half, pt, lp in ((0, pa, L_PS[2 * q]), (1, pb, L_PS[2 * q + 1])):
                            for k in range(8):
                                P.op("pe", lambda e, k=k, half=half, pt=pt, b=b, ts=ts: e.matmul(pt[:, :], lhsT=WIN[b][:, k, half * 128:(half + 1) * 128], rhs=XN3[:, k, ts],
                                                                                                 start=(k == 0), stop=(k == 7)),
                                     reads=[L_WIN[b], L_XN[k][tt]], writes=[lp])
                        P.op("act", lambda e, q=q, pa=pa: e.activation(out=SILU[q], in_=pa[:, :], func=AF.Silu), reads=[L_PS[2 * q]], writes=[L_SILU[q]])
                        P.op("dve", lambda e, q=q, pb=pb, il=il, ts=ts: e.tensor_tensor(out=G3[:, il, ts], in0=SILU[q], in1=pb[:, :], op=ALU.mult),
                             reads=[L_SILU[q], L_PS[2 * q + 1]], writes=[L_G[il][tt]])
                pend = [load_wout(hh, 0)]
                for m in range(8):
                    b = pend.pop(0)
                    if m + 1 < 8:
                        pend.append(load_wout(hh, m + 1))
                    for tt in range(NT):
                        ts = slice(tt * 512, (tt + 1) * 512)
                        q = ffn_ctr[2] % 2
                        ffn_ctr[2] += 1
                        po, lpo = PS[4 + q], L_PS[4 + q]
                        for k in range(11):
                            P.op("pe", lambda e, k=k, po=po, b=b, ts=ts: e.matmul(po[:, :], lhsT=WOUT[b][:, k, :], rhs=G3[:, k, ts], start=(k == 0), stop=(k == 10)),
                                 reads=[L_WOUT[b], L_G[k][tt]], writes=[lpo])
                        P.op("dve", lambda e, po=po, m=m, ts=ts: e.scalar_tensor_tensor(out=H3[:, m, ts], in0=po[:, :], scalar=0.5, in1=H3[:, m, ts], op0=ALU.mult, op1=ALU.add),
                             reads=[lpo, L_H[m][tt]], writes=[L_H[m][tt]])


        NLEV = int(round(math.log2(S)))
        XPt = [AF32[:, i * 2048:i * 2048 + S] for i in range(4)]
        L_XP = lts(4, "xp")
        XB = [ARENA[:, 16384 + i * 2048:16384 + i * 2048 + S] for i in range(2)]
        L_XB = lts(2, "xb")
        WGLU = ARENA[:, 20480:28672].rearrange("p (k n) -> p k n", k=8)
        L_WGLU = LT("wglu")
        TB0 = 28672
        BBt = [ARENA[:, TB0 + i * 1024:TB0 + (i + 1) * 1024] for i in range(2)]
        CTt = [ARENA[:, TB0 + 2048 + i * 1024:TB0 + 2048 + (i + 1) * 1024] for i in range(2)]
        FB0 = (TB0 + 4096) // 2
        ALt = [AF32[:, FB0 + i * 352:FB0 + (i + 1) * 352] for i in range(3)]
        S5M = AF32[:, FB0 + 1056:FB0 + 1066]
        S5S = AF32[:, FB0 + 1066:FB0 + 1066 + 1120]
        T5 = [AF32[:, FB0 + 2200 + i * 512:FB0 + 2200 + (i + 1) * 512] for i in range(4)]
        L_T5 = lts(4, "t5")
        L_PRE = LT("s5pre")
        LCH = 8
        NCH = S // LCH
        LV1 = 3
        LV2 = int(round(math.log2(NCH)))
        PJt = [AF32[:, 20632 + i * 256:20632 + (i + 1) * 256] for i in range(3)]
        XS = [AF32[:, 21400 + i * 256:21400 + i * 256 + NCH] for i in range(4)]
        L_XS = lts(4, "xs")

        def s5_scalars(F, are, aim, ldt, tm):
            def A(out, in_, func, **kw):
                P.op("act", lambda e: e.activation(out=out, in_=in_, func=func, **kw), reads=[L_PRE], writes=[L_PRE])

            def TT(out, a, b, op):
                P.op("dve", lambda e: e.tensor_tensor(out=out, in0=a, in1=b, op=op), reads=[L_PRE], writes=[L_PRE])

            def TS(out, a, s1, s2, op0, op1=None):
                if op1 is None:
                    P.op("dve", lambda e: e.tensor_scalar(out, a, s1, None, op0=op0), reads=[L_PRE], writes=[L_PRE])
                else:
                    P.op("dve", lambda e: e.tensor_scalar(out, a, s1, s2, op0=op0, op1=op1), reads=[L_PRE], writes=[L_PRE])
            dt, th, mag, sn, cs, t1, den, t2 = tm[4], tm[5], tm[6], tm[7], tm[0], tm[1], tm[2], tm[3]
            A(dt, ldt, AF.Exp)
            TT(th, aim, dt, ALU.mult)
            TT(mag, are, dt, ALU.mult)
            A(mag, mag, AF.Exp)
            I32 = mybir.dt.int32

            def wrap(out, shift):
                tf, ti = tm[1], tm[2].bitcast(I32)
                TS(out, th, shift, None, ALU.add)
                TS(tf, out, 1.0 / (2 * math.pi), None, ALU.mult)
                P.op("dve", lambda e: e.tensor_copy(ti, tf), reads=[L_PRE], writes=[L_PRE])
                P.op("dve", lambda e: e.tensor_copy(tf, ti), reads=[L_PRE], writes=[L_PRE])
                P.op("dve", lambda e: e.scalar_tensor_tensor(out=out, in0=tf, scalar=-2 * math.pi, in1=out, op0=ALU.mult, op1=ALU.add), reads=[L_PRE], writes=[L_PRE])
                TS(tf, out, math.pi, -2 * math.pi, ALU.is_gt, ALU.mult)
                TT(out, out, tf, ALU.add)
                TS(tf, out, -math.pi, 2 * math.pi, ALU.is_lt, ALU.mult)
                TT(out, out, tf, ALU.add)
                TS(out, out, math.pi, -math.pi, ALU.min, ALU.max)
            wrap(sn, 0.0)
            A(sn, sn, AF.Sin)
            wrap(cs, 0.5 * math.pi)
            A(cs, cs, AF.Sin)
            abr, abi = tm[0], tm[7]
            TT(abr, cs, mag, ALU.mult)
            TT(abi, sn, mag, ALU.mult)
            TT(den, are, are, ALU.mult)
            TT(t2, aim, aim, ALU.mult)
            TT(den, den, t2, ALU.add)
            P.op("dve", lambda e: e.reciprocal(den, den), reads=[L_PRE], writes=[L_PRE])
            TS(t1, abr, -1.0, None, ALU.add)
            zr, zi = tm[4], tm[5]
            TT(zr, t1, are, ALU.mult)
            TT(t2, abi, aim, ALU.mult)
            TT(zr, zr, t2, ALU.add)
            TT(zr, zr, den, ALU.mult)
            TT(zi, abi, are, ALU.mult)
            TT(t2, t1, aim, ALU.mult)
            TT(zi, zi, t2, ALU.subtract)
            TT(zi, zi, den, ALU.mult)
            return abr, abi, zr, zi

        def s5_precompute():
            def TT(out, a, b, op):
                P.op("dve", lambda e: e.tensor_tensor(out=out, in0=a, in1=b, op=op), reads=[L_PRE], writes=[L_PRE])
            RP = AF32[:, 0:2560]
            P.dma("sp", lambda e: e.dma_start(out=RP, in_=s5R_d), L_PRE, writes=[L_PRE] + L_XP)
            P.dma("sp", lambda e: e.dma_start(out=S5M, in_=s5M_d), L_PRE, writes=[L_PRE])
            P.dma("sp", lambda e: e.dma_start(out=S5S, in_=s5S_d), L_PRE, writes=[L_PRE])
            are, aim, ldt, bre, bim = [RP[:, i * 512:(i + 1) * 512] for i in range(5)]
            tm = [AF32[:, 2560 + i * 512:2560 + (i + 1) * 512] for i in range(8)]
            abr, abi, zr, zi = s5_scalars(512, are, aim, ldt, tm)
            bbr, bbi, t2 = tm[1], tm[2], tm[3]
            TT(bbr, zr, bre, ALU.mult)
            TT(t2, zi, bim, ALU.mult)
            TT(bbr, bbr, t2, ALU.subtract)
            TT(bbi, zr, bim, ALU.mult)
            TT(t2, zi, bre, ALU.mult)
            TT(bbi, bbi, t2, ALU.add)
            for ri, src in ((0, bbr), (1, bbi)):
                dst = BBt[ri].rearrange("p (k g n) -> p k g n", k=8, g=2)
                for gl in range(2):
                    P.op("dve", lambda e, dst=dst, gl=gl, src=src: e.tensor_scalar(dst[:, :, gl, :], src.rearrange("p (k n) -> p k n", k=8), S5M[:, gl:gl + 1], None, op0=ALU.mult),
                         reads=[L_PRE], writes=[L_PRE])
            sare, saim, sldt = [S5S[:, i * 32:(i + 1) * 32] for i in range(3)]
            scre = S5S[:, 96:96 + 512]
            scim = S5S[:, 96 + 512:96 + 1024]
            tm2 = [AF32[:, 2560 + 4096 + i * 32:2560 + 4096 + (i + 1) * 32] for i in range(10)]
            abr, abi, zr, zi = s5_scalars(32, sare, saim, sldt, tm2[:8])
            AL = [t.rearrange("p (a k) -> p a k", k=11) for t in ALt]
            P.op("dve", lambda e: e.tensor_copy(AL[0][:, :, 0], abr), reads=[L_PRE], writes=[L_PRE])
            P.op("dve", lambda e: e.tensor_copy(AL[1][:, :, 0], abi), reads=[L_PRE], writes=[L_PRE])
            for k in range(1, 11):
                pr, pi = AL[0][:, :, k - 1], AL[1][:, :, k - 1]
                TT(tm2[8], pr, pr, ALU.mult)
                TT(tm2[9], pi, pi, ALU.mult)
                TT(AL[0][:, :, k], tm2[8], tm2[9], ALU.subtract)
                TT(tm2[8], pr, pi, ALU.mult)
                P.op("dve", lambda e, k=k: e.tensor_scalar(AL[1][:, :, k], tm2[8], 2.0, None, op0=ALU.mult), reads=[L_PRE], writes=[L_PRE])
            P.op("dve", lambda e: e.tensor_scalar(ALt[2], ALt[1], -1.0, None, op0=ALU.mult), reads=[L_PRE], writes=[L_PRE])
            PJ = [t.rearrange("p (a j) -> p a j", j=LCH) for t in PJt]
            P.op("dve", lambda e: e.tensor_copy(PJ[0][:, :, 0], abr), reads=[L_PRE], writes=[L_PRE])
            P.op("dve", lambda e: e.tensor_copy(PJ[1][:, :, 0], abi), reads=[L_PRE], writes=[L_PRE])
            for j in range(1, LCH):
                pr, pi = PJ[0][:, :, j - 1], PJ[1][:, :, j - 1]
                TT(tm2[8], pr, abr, ALU.mult)
                TT(tm2[9], pi, abi, ALU.mult)
                TT(PJ[0][:, :, j], tm2[8], tm2[9], ALU.subtract)
                TT(tm2[8], pr, abi, ALU.mult)
                TT(tm2[9], pi, abr, ALU.mult)
                TT(PJ[1][:, :, j], tm2[8], tm2[9], ALU.add)
            P.op("dve", lambda e: e.tensor_scalar(PJt[2], PJt[1], -1.0, None, op0=ALU.mult), reads=[L_PRE], writes=[L_PRE])
            for ri, src, sc in ((0, scre, 1.0), (1, scim, -1.0)):
                dst = CTt[ri].rearrange("p (a g h) -> p a g h", a=32, g=2)
                P.op("dve", lambda e, ri=ri: e.memset(CTt[ri], 0.0), reads=[L_PRE], writes=[L_PRE])
                for gl in range(2):
                    ps_ = slice(64 * gl, 64 * gl + 64)
                    P.op("dve", lambda e, dst=dst, gl=gl, src=src, sc=sc, ps_=ps_: e.tensor_scalar(dst[ps_, :, gl, :], src.rearrange("p (a h) -> p a h", a=32)[ps_], sc, None, op0=ALU.mult),
                         reads=[L_PRE], writes=[L_PRE])

        s5_ctr = [0]

        ARENA_LTS.extend(L_XP + L_XB + [L_WGLU, L_PRE] + L_T5 + L_XS)

        def s5_mixer():
            arena_barrier()
            s5_precompute()
            P.dma("pool", lambda e: e.dma_start(out=WGLU[:, :, :], in_=wglu_d[0].rearrange("(k p) n -> p k n", p=128)), L_WGLU, writes=[L_WGLU])
            rmsnorm((0 * 3 + 1) * 8)

            def inmm(kc, pl):
                rows = slice(32 * pl, 32 * pl + 32)
                for tt in range(NT):
                    ts = slice(tt * 512, (tt + 1) * 512)
                    q = s5_ctr[0] % 2
                    s5_ctr[0] += 1
                    for ri in range(2):
                        pt, lp = PS[2 * q + ri], L_PS[2 * q + ri]
                        P.op("pe", lambda e, pt=pt, ri=ri, rows=rows, kc=kc, ts=ts, pl=pl: e.matmul(pt[:, :], lhsT=BBt[ri][rows, kc * 128:(kc + 1) * 128], rhs=XN3[rows, kc, ts],
                                                                                               start=True, stop=True, tile_position=(32 * pl, 0)),
                             reads=[L_PRE, L_XN[kc][tt]], writes=[lp])
                        P.op("act", lambda e, pt=pt, ri=ri, ts=ts: e.activation(out=XPt[ri][:, ts], in_=pt[:, :], func=AF.Copy), reads=[lp], writes=[L_XP[ri]])

            inmm(0, 0)
            for kc in range(8):
                for pl in range(4):
                    pair = kc * 4 + pl
                    rows = slice(32 * pl, 32 * pl + 32)
                    def V(t):
                        return t.rearrange("p (c j) -> p c j", j=LCH)

                    def cstt(out, in0, sc, in1, rd, wr):
                        P.op("dve", lambda e: e.scalar_tensor_tensor(out=out, in0=in0, scalar=sc, in1=in1, op0=ALU.mult, op1=ALU.add), reads=rd + [L_PRE], writes=wr)

                    def al(i, k):
                        return ALt[i][:, pair * 11 + k:pair * 11 + k + 1]
                    cur = 0
                    for k in range(LV1):
                        d = 1 << k
                        sr, si, dr, di = V(XPt[cur]), V(XPt[cur + 1]), V(XPt[2 - cur]), V(XPt[3 - cur])
                        lsr, lsi, ldr, ldi = L_XP[cur], L_XP[cur + 1], L_XP[2 - cur], L_XP[3 - cur]
                        cstt(dr[:, :, d:], sr[:, :, :LCH - d], al(0, k), sr[:, :, d:], [lsr], [ldr])
                        cstt(dr[:, :, d:], si[:, :, :LCH - d], al(2, k), dr[:, :, d:], [lsi], [ldr])
                        cstt(di[:, :, d:], sr[:, :, :LCH - d], al(1, k), si[:, :, d:], [lsr, lsi], [ldi])
                        cstt(di[:, :, d:], si[:, :, :LCH - d], al(0, k), di[:, :, d:], [lsi], [ldi])
                        P.op("act", lambda e, sr=sr, dr=dr, d=d: e.activation(out=dr[:, :, :d], in_=sr[:, :, :d], func=AF.Copy), reads=[lsr], writes=[ldr])
                        P.op("act", lambda e, si=si, di=di, d=d: e.activation(out=di[:, :, :d], in_=si[:, :, :d], func=AF.Copy), reads=[lsi], writes=[ldi])
                        cur = 2 - cur
                    Cr, Ci = V(XPt[cur]), V(XPt[cur + 1])
                    lcr, lci = L_XP[cur], L_XP[cur + 1]
                    P.op("act", lambda e, Cr=Cr: e.activation(out=XS[0], in_=Cr[:, :, LCH - 1], func=AF.Copy), reads=[lcr], writes=[L_XS[0]])
                    P.op("act", lambda e, Ci=Ci: e.activation(out=XS[1], in_=Ci[:, :, LCH - 1], func=AF.Copy), reads=[lci], writes=[L_XS[1]])
                    xc = 0
                    for k2 in range(LV2):
                        d = 1 << k2
                        kk = LV1 + k2
                        sr, si, dr, di = XS[xc], XS[xc + 1], XS[2 - xc], XS[3 - xc]
                        lsr, lsi, ldr, ldi = L_XS[xc], L_XS[xc + 1], L_XS[2 - xc], L_XS[3 - xc]
                        cstt(dr[:, d:], sr[:, :NCH - d], al(0, kk), sr[:, d:], [lsr], [ldr])
                        cstt(dr[:, d:], si[:, :NCH - d], al(2, kk), dr[:, d:], [lsi], [ldr])
                        cstt(di[:, d:], sr[:, :NCH - d], al(1, kk), si[:, d:], [lsr, lsi], [ldi])
                        cstt(di[:, d:], si[:, :NCH - d], al(0, kk), di[:, d:], [lsi], [ldi])
                        P.op("act", lambda e, sr=sr, dr=dr, d=d: e.activation(out=dr[:, :d], in_=sr[:, :d], func=AF.Copy), reads=[lsr], writes=[ldr])
                        P.op("act", lambda e, si=si, di=di, d=d: e.activation(out=di[:, :d], in_=si[:, :d], func=AF.Copy), reads=[lsi], writes=[ldi])
                        xc = 2 - xc
                    Xr, Xi = XS[xc], XS[xc + 1]
                    lxr, lxi = L_XS[xc], L_XS[xc + 1]
                    for j in range(LCH):
                        pj = [PJt[i][:, pair * LCH + j:pair * LCH + j + 1] for i in range(3)]
                        cstt(Cr[:, 1:, j], Xr[:, :NCH - 1], pj[0], Cr[:, 1:, j], [lxr], [lcr])
                        cstt(Cr[:, 1:, j], Xi[:, :NCH - 1], pj[2], Cr[:, 1:, j], [lxi], [lcr])
                        cstt(Ci[:, 1:, j], Xr[:, :NCH - 1], pj[1], Ci[:, 1:, j], [lxr], [lci])
                        cstt(Ci[:, 1:, j], Xi[:, :NCH - 1], pj[0], Ci[:, 1:, j], [lxi], [lci])
                    for ri in range(2):
                        P.op("act", lambda e, ri=ri, cur=cur: e.activation(out=XB[ri], in_=XPt[cur + ri], func=AF.Copy), reads=[L_XP[cur + ri]], writes=[L_XB[ri]])
                    if pair + 1 < 32:
                        inmm((pair + 1) // 4, (pair + 1) % 4)
                    for tt in range(NT):
                        ts = slice(tt * 512, (tt + 1) * 512)
                        for ri in range(2):
                            P.op("pe", lambda e, tt=tt, ri=ri, pair=pair, pl=pl, ts=ts: e.matmul(PS[4 + tt][32 * pl:32 * pl + 32, :], lhsT=CTt[ri][:, pair * 32:(pair + 1) * 32], rhs=XB[ri][:, ts],
                                                                                            start=(ri == 0), stop=(ri == 1), tile_position=(0, 32 * pl)),
                                 reads=[L_PRE, L_XB[ri]], writes=[L_PS[4 + tt]])
                for tt in range(NT):
                    ts = slice(tt * 512, (tt + 1) * 512)
                    a, b = T5[0], T5[1]
                    la, lb = L_T5[0], L_T5[1]
                    P.op("dve", lambda e, kc=kc, ts=ts, tt=tt: e.scalar_tensor_tensor(out=T5[0], in0=XN3[:, kc, ts], scalar=S5M[:, 2 + kc:3 + kc], in1=PS[4 + tt][:, :], op0=ALU.mult, op1=ALU.add),
                         reads=[L_XN[kc][tt], L_PS[4 + tt], L_PRE], writes=[la])
                    P.op("act", lambda e: e.activation(out=T5[1], in_=T5[0], func=AF.Square), reads=[la], writes=[lb])
                    P.op("dve", lambda e: e.tensor_scalar(T5[1], T5[1], 0.044715, 1.0, op0=ALU.mult, op1=ALU.add), reads=[lb], writes=[lb])
                    P.op("dve", lambda e: e.tensor_tensor(out=T5[1], in0=T5[1], in1=T5[0], op=ALU.mult), reads=[la, lb], writes=[lb])
                    P.op("act", lambda e: e.activation(out=T5[1], in_=T5[1], func=AF.Sigmoid, scale=1.5957691216057308), reads=[lb], writes=[lb])
                    P.op("dve", lambda e, kc=kc, ts=ts: e.tensor_tensor(out=XN3[:, kc, ts], in0=T5[0], in1=T5[1], op=ALU.mult), reads=[la, lb], writes=[L_XN[kc][tt]])
            for m in range(8):
                for tt in range(NT):
                    ts = slice(tt * 512, (tt + 1) * 512)
                    q = s5_ctr[0] % 2
                    s5_ctr[0] += 1
                    for k in range(8):
                        P.op("pe", lambda e, q=q, k=k, m=m, ts=ts: e.matmul(PS[q][:, :], lhsT=WGLU[:, k, m * 128:(m + 1) * 128], rhs=XN3[:, k, ts], start=(k == 0), stop=(k == 7)),
                             reads=[L_WGLU, L_XN[k][tt]], writes=[L_PS[q]])
                    P.op("act", lambda e, q=q: e.activation(out=T5[2 + q], in_=PS[q][:, :], func=AF.Sigmoid), reads=[L_PS[q]], writes=[L_T5[2 + q]])
                    P.op("dve", lambda e, q=q, m=m, ts=ts: e.tensor_tensor(out=T5[2 + q], in0=T5[2 + q], in1=XN3[:, m, ts], op=ALU.mult), reads=[L_T5[2 + q], L_XN[m][tt]], writes=[L_T5[2 + q]])
                    P.op("dve", lambda e, q=q, m=m, ts=ts: e.tensor_tensor(out=H3[:, m, ts], in0=H3[:, m, ts], in1=T5[2 + q], op=ALU.mult if False else ALU.add), reads=[L_T5[2 + q], L_H[m][tt]], writes=[L_H[m][tt]])


        NKT = S // 128
        NQG = S // 512
        NCMP = S // 16 - 1
        OFFB = 2064
        IDB, JM, BLK64 = NCB[:, 0:128], NCB[:, 128:256], NCB[:, 256:384]
        KSL_d = nc.dram_tensor("KSL_d", [4, 128, S], BF16).ap()
        KWIN_d = nc.dram_tensor("KWIN_d", [4, 128, S], BF16).ap()
        VS_d = nc.dram_tensor("VS_d", [128, NKT * 8 * 65], BF16).ap()
        KC_d = nc.dram_tensor("KC_d", [128, 512], BF16).ap()
        VC_d = nc.dram_tensor("VC_d", [128, 4 * 97], BF16).ap()
        BV_d = nc.dram_tensor("BV_d", [2 * 16 * 4096], BF16).ap()
        L_KSLd, L_KWINd = lts(4, "ksld"), lts(4, "kwind")
        L_VSd, L_KCd, L_VCd, L_BVd = LT("vsd"), LT("kcd"), LT("vcd"), LT("bvd")

        def nsa_setup():
            P.dma("pool", lambda e: e.dma_start(out=NCB[:], in_=nsac_d), L_NCB, writes=[L_NCB])
            P.dma("pool", lambda e: e.dma_start(out=EBLK[:], in_=eblk_d), L_EBLK, writes=[L_EBLK])
            P.dma("sp", lambda e: e.dma_start(out=SELTAB[:], in_=seltab_d), L_SELTAB, writes=[L_SELTAB])
            P.dma("sp", lambda e: e.dma_start(out=NGN[:], in_=ngn_d), L_NGN, writes=[L_NGN])
            RBA = ARENA[0:33, 0:16]
            OHA = [ARENA[0:33, 16 + v * 4096:16 + (v + 1) * 4096] for v in range(2)]
            BVS = ARENA[0:16, 8208:8208 + 4096]
            L_RBA, L_OHA, L_BVS = LT("rba"), LT("oha"), LT("bvs")
            P.op("dve", lambda e: e.memset(ARENA[32:33, 0:16], -3750.0), writes=[L_RBA])
            P.dma("pool", lambda e: e.dma_start(out=ARENA[0:32, 0:16], in_=relb_d), L_RBA, writes=[L_RBA])
            for v in range(2):
                P.dma("pool", lambda e, v=v: e.dma_start(out=OHA[v], in_=oha_d[v]), L_OHA, writes=[L_OHA])
            for v in range(2):
                for xc in range(8):
                    P.op("pe", lambda e, v=v, xc=xc: e.matmul(PS[0][0:16, :], lhsT=RBA, rhs=OHA[v][:, xc * 512:(xc + 1) * 512], start=True, stop=True),
                         reads=[L_RBA, L_OHA], writes=[L_PS[0]])
                    P.op("act", lambda e, xc=xc: e.activation(out=BVS[:, xc * 512:(xc + 1) * 512], in_=PS[0][0:16, :], func=AF.Copy, scale=8.0),
                         reads=[L_PS[0]], writes=[L_BVS])
                P.dma("sp", lambda e, v=v: e.dma_start(out=BV_d[v * 65536:(v + 1) * 65536].rearrange("(h x) -> h x", h=16), in_=BVS), L_BVS, reads=[L_BVS], writes=[L_BVd])
            ARENA_LTS.extend([L_RBA, L_OHA, L_BVS])

        def headnorm(psrc, lpsrc, gcol, out_ap, out_lts, sq_ap, l_sq, rst_ap, l_rst, N):
            P.op("act", lambda e: e.activation(out=sq_ap, in_=psrc, func=AF.Square), reads=[lpsrc], writes=[l_sq])
            P.op("pe", lambda e: e.matmul(PS[5][:, 0:N], lhsT=BLK64, rhs=sq_ap, start=True, stop=True), reads=[L_NCB, l_sq], writes=[L_PS[5]])
            P.op("act", lambda e: e.activation(out=rst_ap, in_=PS[5][:, 0:N], func=AF.Sqrt, bias=EPSC[:, 0:1], scale=1.0), reads=[L_PS[5], L_EPS], writes=[l_rst])
            P.op("dve", lambda e: e.reciprocal(rst_ap, rst_ap), reads=[l_rst], writes=[l_rst])
            P.op("dve", lambda e: e.scalar_tensor_tensor(out=out_ap, in0=psrc, scalar=NGN[:, gcol:gcol + 1], in1=rst_ap, op0=ALU.mult, op1=ALU.mult),
                 reads=[lpsrc, l_rst, L_NGN], writes=out_lts)

        kv_ctr = [0]
        L_KV = {n: LT("kv_" + n) for n in "w1k w1v w2 pos wkvv srct vst hid kcs vcs sqk rstk pw1 tg".split()}
        L_WKS = lts(2, "wks")
        L_KST = lts(2, "kst")
        ARENA_LTS.extend(list(L_KV.values()) + L_WKS + L_KST)

        def kv_phase():
            arena_barrier()
            W1 = [ARENA[:, i * 4096:(i + 1) * 4096].rearrange("p (l h) -> p l h", l=32) for i in range(2)]
            W2K = ARENA[:, 8192:8320]
            W2V = ARENA[:, 8320:8384]
            POS = [ARENA[:, 8384 + i * 32:8384 + (i + 1) * 32] for i in range(2)]
            WKVV = ARENA[:, 8448:12544].rearrange("p (k n) -> p k n", k=8)
            WKS = [ARENA[:, 12544 + i * 1024:12544 + (i + 1) * 1024].rearrange("p (k n) -> p k n", k=8) for i in range(2)]
            SRCT = ARENA[:, 14592:14592 + S]
            KST = [ARENA[:, 16640 + i * 2048:16640 + i * 2048 + S] for i in range(2)]
            VST = ARENA[:, 20736:20736 + NKT * 520].rearrange("p (t s d) -> p t s d", t=NKT, s=8)
            HID = ARENA[:, 29056:29184]
            KCS = ARENA[:, 29312:29824]
            VCS = ARENA[:, 29824:29824 + 388].rearrange("p (g d) -> p g d", g=4)
            SQK = ARENA[:, 30224:30736]
            RSTK = AF32[:, 15400:15912]
            PW1 = AF32[:, 15912:15914]
            TG = [AF32[:, 15920 + i * 128:15920 + (i + 1) * 128] for i in range(2)]
            wkv = wkv_d.rearrange("(k p) n -> p k n", p=128)
            rmsnorm(48)
            for i, (wd, ln) in enumerate(((w1k_d, "w1k"), (w1v_d, "w1v"))):
                for hf in range(2):
                    P.dma("pool", lambda e, i=i, wd=wd, hf=hf: e.dma_start(out=W1[i][64 * hf:64 * hf + 64, :, :], in_=wd.rearrange("l d h -> d l h")), L_KV[ln], writes=[L_KV[ln]])
            for hf in range(2):
                P.dma("pool", lambda e, hf=hf: e.dma_start(out=W2K[:, 64 * hf:64 * hf + 64], in_=w2k_d), L_KV["w2"], writes=[L_KV["w2"]])
            P.dma("pool", lambda e: e.dma_start(out=W2V, in_=w2v_d), L_KV["w2"], writes=[L_KV["w2"]])
            for i, pd in enumerate((posk_d, posv_d)):
                P.dma("pool", lambda e, i=i, pd=pd: e.dma_start(out=POS[i][0:64, :], in_=pd), L_KV["pos"], writes=[L_KV["pos"]])
            P.dma("pool", lambda e: e.dma_start(out=WKVV[:, :, 0:256], in_=wkv[:, :, 768:1024]), L_KV["wkvv"], writes=[L_KV["wkvv"]])
            P.dma("pool", lambda e: e.dma_start(out=WKVV[:, :, 256:512], in_=wkv[:, :, 1280:1536]), L_KV["wkvv"], writes=[L_KV["wkvv"]])
            for i, ln in enumerate(("w1k", "w1v")):
                for l in range(32):
                    P.op("pe", lambda e, i=i, l=l: e.matmul(PS[7][:, 0:1], lhsT=W1[i][0:64, l, :], rhs=POS[i][0:64, l:l + 1], start=(l == 0), stop=(l == 31)),
                         reads=[L_KV[ln], L_KV["pos"]], writes=[L_PS[7]])
                P.op("act", lambda e, i=i: e.activation(out=PW1[:, i:i + 1], in_=PS[7][:, 0:1], func=AF.Copy), reads=[L_PS[7]], writes=[L_KV["pw1"]])
            P.op("dve", lambda e: e.memset(VST[:, :, :, 64:65], 1.0), writes=[L_KV["vst"]])
            for kt in range(NKT):
                tt = kt // 4
                q = kv_ctr[0] % 2
                kv_ctr[0] += 1
                for k in range(8):
                    P.op("pe", lambda e, q=q, k=k, kt=kt: e.matmul(PS[q][:, :], lhsT=XN3[:, k, kt * 128:(kt + 1) * 128], rhs=WKVV[:, k, :], start=(k == 0), stop=(k == 7)),
                         reads=[L_XN[k][tt], L_KV["wkvv"]], writes=[L_PS[q]])
                P.op("act", lambda e, q=q, kt=kt: e.activation(out=VST[:, kt, :, 0:64], in_=PS[q][:, :].rearrange("p (s d) -> p s d", s=8), func=AF.Copy),
                     reads=[L_PS[q]], writes=[L_KV["vst"]])
            P.dma("sp", lambda e: e.dma_start(out=VS_d, in_=ARENA[:, 20736:20736 + NKT * 520]), L_KV["vst"], reads=[L_KV["vst"]], writes=[L_VSd])

            def load_wks(col0, dup):
                b = kv_ctr[0] % 2
                kv_ctr[0] += 1
                if dup:
                    for hf in range(2):
                        P.dma("pool", lambda e, b=b, hf=hf: e.dma_start(out=WKS[b][:, :, 64 * hf:64 * hf + 64], in_=wkv[:, :, col0:col0 + 64]), L_WKS[b], writes=[L_WKS[b]])
                else:
                    P.dma("pool", lambda e, b=b: e.dma_start(out=WKS[b][:, :, :], in_=wkv[:, :, col0:col0 + 128]), L_WKS[b], writes=[L_WKS[b]])
                return b

            for slot, dst, ldst, gcol in ((2, KSL_d, L_KSLd, 1), (4, KWIN_d, L_KWINd, 2)):
                for g in range(4):
                    b = load_wks(slot * 256 + g * 64, True)
                    kb = kv_ctr[0] % 2
                    for tt in range(NT):
                        ts = slice(tt * 512, (tt + 1) * 512)
                        q = 2 + (kv_ctr[0] % 2)
                        kv_ctr[0] += 1
                        for k in range(8):
                            P.op("pe", lambda e, q=q, k=k, b=b, ts=ts: e.matmul(PS[q][:, :], lhsT=WKS[b][:, k, :], rhs=XN3[:, k, ts], start=(k == 0), stop=(k == 7)),
                                 reads=[L_WKS[b], L_XN[k][tt]], writes=[L_PS[q]])
                        headnorm(PS[q][:, :], L_PS[q], gcol, KST[kb][:, ts], [L_KST[kb]], SQK, L_KV["sqk"], RSTK, L_KV["rstk"], 512)
                    P.dma("sp", lambda e, kb=kb, dst=dst, g=g: e.dma_start(out=dst[g], in_=KST[kb]), L_KST[kb], reads=[L_KST[kb]], writes=[ldst[g]])

            P.op("dve", lambda e: e.memset(KCS, 0.0), writes=[L_KV["kcs"]])
            P.op("dve", lambda e: e.memset(ARENA[:, 29824:29824 + 388], 0.0), writes=[L_KV["vcs"]])
            for g in range(4):
                P.dma("pool", lambda e, g=g: e.dma_start(out=VCS[:, g, 64:97], in_=ovm_d), L_KV["vcs"], writes=[L_KV["vcs"]])
            for slot in range(2):
                ln = ("w1k", "w1v")[slot]
                for gp in range(2):
                    b = load_wks(slot * 256 + gp * 128, False)
                    for tt in range(NT):
                        ts = slice(tt * 512, (tt + 1) * 512)
                        q = 2 + (kv_ctr[0] % 2)
                        kv_ctr[0] += 1
                        for k in range(8):
                            P.op("pe", lambda e, q=q, k=k, b=b, ts=ts: e.matmul(PS[q][:, :], lhsT=WKS[b][:, k, :], rhs=XN3[:, k, ts], start=(k == 0), stop=(k == 7)),
                                 reads=[L_WKS[b], L_XN[k][tt]], writes=[L_PS[q]])
                        P.op("act", lambda e, q=q, ts=ts: e.activation(out=SRCT[:, ts], in_=PS[q][:, :], func=AF.Copy), reads=[L_PS[q]], writes=[L_KV["srct"]])
                    for gi in range(2):
                        g = 2 * gp + gi
                        rows = slice(64 * gi, 64 * gi + 64)
                        for l in range(32):
                            P.op("pe", lambda e, slot=slot, rows=rows, l=l: e.matmul(PS[4][:, 0:NCMP], lhsT=W1[slot][rows, l, :], rhs=SRCT[rows, l:l + 16 * (NCMP - 1) + 1:16],
                                                                                  start=(l == 0), stop=(l == 31)),
                                 reads=[L_KV[ln], L_KV["srct"]], writes=[L_PS[4]])
                        a_, b_ = TG[0][:, 0:NCMP], TG[1][:, 0:NCMP]
                        lt = L_KV["tg"]
                        P.op("act", lambda e, slot=slot, a_=a_: e.activation(out=a_, in_=PS[4][:, 0:NCMP], func=AF.Identity, bias=PW1[:, slot:slot + 1], scale=1.0),
                             reads=[L_PS[4], L_KV["pw1"]], writes=[lt])
                        P.op("act", lambda e, a_=a_, b_=b_: e.activation(out=b_, in_=a_, func=AF.Square), reads=[lt], writes=[lt])
                        P.op("dve", lambda e, b_=b_: e.tensor_scalar(b_, b_, 0.044715, 1.0, op0=ALU.mult, op1=ALU.add), reads=[lt], writes=[lt])
                        P.op("dve", lambda e, a_=a_, b_=b_: e.tensor_tensor(out=b_, in0=b_, in1=a_, op=ALU.mult), reads=[lt], writes=[lt])
                        P.op("act", lambda e, b_=b_: e.activation(out=b_, in_=b_, func=AF.Sigmoid, scale=1.5957691216057308), reads=[lt], writes=[lt])
                        P.op("dve", lambda e, a_=a_, b_=b_: e.tensor_tensor(out=HID[:, 0:NCMP], in0=a_, in1=b_, op=ALU.mult), reads=[lt], writes=[L_KV["hid"]])
                        if slot == 0:
                            P.op("pe", lambda e: e.matmul(PS[7][:, 0:NCMP], lhsT=W2K, rhs=HID[:, 0:NCMP], start=True, stop=True), reads=[L_KV["w2"], L_KV["hid"]], writes=[L_PS[7]])
                            headnorm(PS[7][:, 0:NCMP], L_PS[7], 3, KCS[:, g * 128:g * 128 + NCMP], [L_KV["kcs"]], SQK[:, 0:NCMP], L_KV["sqk"], RSTK[:, 0:NCMP], L_KV["rstk"], NCMP)
                        else:
                            P.op("pe", lambda e: e.matmul(PS[7][0:NCMP, 0:64], lhsT=HID[:, 0:NCMP], rhs=W2V, start=True, stop=True), reads=[L_KV["w2"], L_KV["hid"]], writes=[L_PS[7]])
                            P.op("act", lambda e, g=g: e.activation(out=VCS[0:NCMP, g, 0:64], in_=PS[7][0:NCMP, 0:64], func=AF.Copy), reads=[L_PS[7]], writes=[L_KV["vcs"]])
            P.dma("sp", lambda e: e.dma_start(out=KC_d, in_=KCS), L_KV["kcs"], reads=[L_KV["kcs"]], writes=[L_KCd])
            P.dma("sp", lambda e: e.dma_start(out=VC_d, in_=ARENA[:, 29824:29824 + 388]), L_KV["vcs"], reads=[L_KV["vcs"]], writes=[L_VCd])

        L_N = {n: LT("n_" + n) for n in "ksl kwin vsl vwin kc vc tcmp tsel twin selt wg sqq snb oacc pslc gates rstq sc top8 rden coef oc".split()}
        L_QT = lts(8, "qt")
        L_PT = lts(2, "pt")
        L_WQ = lts(2, "wq")
        L_WO = lts(2, "wo")
        ARENA_LTS.extend(list(L_N.values()) + L_QT + L_PT + L_WQ + L_WO)
        n_ctr = [0, 0, 0]

        def nsa_mixer():
            arena_barrier()
            QT = ARENA[:, 0:8 * S].rearrange("p (m t) -> p m t", m=8)
            KSLg = ARENA[:, 16384:16384 + S]
            KWINg = ARENA[:, 18432:18432 + S]
            VSLg = ARENA[:, 20480:20480 + NKT * 65].rearrange("p (t d) -> p t d", t=NKT)
            VWINg = ARENA[:, 21520:21520 + NKT * 65].rearrange("p (t d) -> p t d", t=NKT)
            KCg = ARENA[:, 22560:23072]
            VCg = ARENA[:, 23072:23072 + 388]
            TCMP = ARENA[:, 23464:23464 + S]
            TSEL = ARENA[:, 25512:25512 + 1152]
            TWIN = ARENA[:, 26664:26664 + 1408]
            PT = [ARENA[:, 28072 + i * 512:28072 + (i + 1) * 512] for i in range(2)]
            SELT = ARENA[0:32, 29096:29096 + S]
            WQ = [ARENA[:, 31144 + i * 1024:31144 + (i + 1) * 1024].rearrange("p (k n) -> p k n", k=8) for i in range(2)]
            WG = ARENA[:, 33192:33576].rearrange("p (k n) -> p k n", k=8)
            WO = [ARENA[:, 33576 + i * 1024:33576 + (i + 1) * 1024].rearrange("p (k n) -> p k n", k=8) for i in range(2)]
            SQQ = ARENA[:, 35624:36136]
            SNB = ARENA[:, 36136:36136 + NKT * 32]
            OACC = AF32[:, 18400:18400 + NKT * 64].rearrange("p (t d) -> p t d", t=NKT)
            PSLC = AF32[:, 19424:19424 + NKT * 32].rearrange("p (t j) -> p t j", t=NKT)
            GATES = AF32[:, 19936:19936 + NKT * 48].rearrange("p (t c) -> p t c", t=NKT)
            RSTQ = AF32[:, 20704:21216]
            SC = AF32[:, 21216:21216 + NKT * 32].rearrange("p (t j) -> p t j", t=NKT)
            TOP8 = AF32[:, 21728:21728 + NKT * 8]
            RDEN = AF32[:, 21856:21860]
            COEF = AF32[:, 21860:21864]
            OC = XN[:].rearrange("p (t c) -> p t c", t=NKT)
            L_OC = L_N["oc"]
            wqg = wqg_d[0].rearrange("(k p) n -> p k n", p=128)
            wo = wo_d[0].rearrange("(k p) n -> p k n", p=128)

            rmsnorm((1 * 3 + 1) * 8)
            P.dma("pool", lambda e: e.dma_start(out=WG[:, :, :], in_=wqg[:, :, 1024:1072]), L_N["wg"], writes=[L_N["wg"]])

            def load_wq(m):
                b = n_ctr[0] % 2
                n_ctr[0] += 1
                P.dma("pool", lambda e, b=b, m=m: e.dma_start(out=WQ[b][:, :, :], in_=wqg[:, :, m * 128:(m + 1) * 128]), L_WQ[b], writes=[L_WQ[b]])
                return b
            pend = [load_wq(0)]
            for m in range(8):
                b = pend.pop(0)
                if m + 1 < 8:
                    pend.append(load_wq(m + 1))
                for tt in range(NT):
                    ts = slice(tt * 512, (tt + 1) * 512)
                    q = n_ctr[1] % 2
                    n_ctr[1] += 1
                    for k in range(8):
                        P.op("pe", lambda e, q=q, k=k, b=b, ts=ts: e.matmul(PS[q][:, :], lhsT=WQ[b][:, k, :], rhs=XN3[:, k, ts], start=(k == 0), stop=(k == 7)),
                             reads=[L_WQ[b], L_XN[k][tt]], writes=[L_PS[q]])
                    headnorm(PS[q][:, :], L_PS[q], 0, QT[:, m, ts], [L_QT[m]], SQQ, L_N["sqq"], RSTQ, L_N["rstq"], 512)
            for T in range(NKT):
                tt = T // 4
                for k in range(8):
                    P.op("pe", lambda e, k=k, T=T: e.matmul(PS[4][:, 0:48], lhsT=XN3[:, k, T * 128:(T + 1) * 128], rhs=WG[:, k, :], start=(k == 0), stop=(k == 7)),
                         reads=[L_XN[k][tt], L_N["wg"]], writes=[L_PS[4]])
                P.op("act", lambda e, T=T: e.activation(out=GATES[:, T, :], in_=PS[4][:, 0:48], func=AF.Sigmoid), reads=[L_PS[4]], writes=[L_N["gates"]])
            P.op("dve", lambda e: e.memset(BARD[:], 0.0), writes=[t for r in L_XN for t in r] + [L_OC])
            P.dma("sp", lambda e: e.dma_start(out=KCg, in_=KC_d), L_N["kc"], reads=[L_KCd], writes=[L_N["kc"]])
            P.dma("sp", lambda e: e.dma_start(out=VCg, in_=VC_d), L_N["vc"], reads=[L_VCd], writes=[L_N["vc"]])
            VSd4 = VS_d.rearrange("p (t s d) -> p t s d", t=NKT, s=8)

            def sbank():
                q = n_ctr[1] % 2
                n_ctr[1] += 1
                return q

            def obank():
                q = 2 + n_ctr[2] % 2
                n_ctr[2] += 1
                return q

            def finalize(po, lpo, W, qg, h, br, mode):
                pv = po[:, 0:4 * W].rearrange("p (t w) -> p t w", t=4)
                T0 = 4 * qg
                P.op("dve", lambda e: e.tensor_scalar(RDEN, pv[:, :, 64], 1e-30, None, op0=ALU.max), reads=[lpo], writes=[L_N["rden"]])
                P.op("dve", lambda e: e.reciprocal(RDEN, RDEN), reads=[L_N["rden"]], writes=[L_N["rden"]])
                P.op("dve", lambda e: e.tensor_tensor(out=COEF, in0=RDEN, in1=GATES[:, T0:T0 + 4, h * 3 + br], op=ALU.mult), reads=[L_N["rden"], L_N["gates"]], writes=[L_N["coef"]])
                for qt in range(4):
                    T = T0 + qt
                    if mode == "oc":
                        P.op("dve", lambda e, qt=qt, T=T: e.tensor_scalar(OC[:, T, h * 64:(h + 1) * 64], pv[:, qt, 0:64], COEF[:, qt:qt + 1], None, op0=ALU.mult),
                             reads=[lpo, L_N["coef"]], writes=[L_OC])
                    elif mode == "set":
                        P.op("dve", lambda e, qt=qt, T=T: e.tensor_scalar(OACC[:, T, :], pv[:, qt, 0:64], COEF[:, qt:qt + 1], None, op0=ALU.mult),
                             reads=[lpo, L_N["coef"]], writes=[L_N["oacc"]])
                    else:
                        P.op("dve", lambda e, qt=qt, T=T: e.scalar_tensor_tensor(out=OACC[:, T, :], in0=pv[:, qt, 0:64], scalar=COEF[:, qt:qt + 1], in1=OACC[:, T, :], op0=ALU.mult, op1=ALU.add),
                             reads=[lpo, L_N["coef"], L_N["oacc"]], writes=[L_N["oacc"]])
                return pv

            for g in range(4):
                P.dma("sp", lambda e, g=g: e.dma_start(out=KSLg, in_=KSL_d[g]), L_N["ksl"], reads=[L_KSLd[g]], writes=[L_N["ksl"]])
                P.dma("sp", lambda e, g=g: e.dma_start(out=KWINg, in_=KWIN_d[g]), L_N["kwin"], reads=[L_KWINd[g]], writes=[L_N["kwin"]])
                P.dma("sp", lambda e, g=g: e.dma_start(out=VSLg, in_=VSd4[:, :, g, :]), L_N["vsl"], reads=[L_VSd], writes=[L_N["vsl"]])
                P.dma("sp", lambda e, g=g: e.dma_start(out=VWINg, in_=VSd4[:, :, 4 + g, :]), L_N["vwin"], reads=[L_VSd], writes=[L_N["vwin"]])
                for r in range(4):
                    h = 4 * g + r
                    m, rows = h // 2, slice(64 * (h % 2), 64 * (h % 2) + 64)
                    P.dma("sp", lambda e, h=h: e.dma_start(out=TCMP, in_=bass.AP(BV_d.tensor, h * 4096 + OFFB - 2063, [[16, 128], [1, S]])), L_N["tcmp"], reads=[L_BVd], writes=[L_N["tcmp"]])
                    csteps = []
                    for qg in range(NQG):
                        qs = slice(qg * 512, (qg + 1) * 512)
                        sq_, oq = sbank(), obank()

                        def score(sq_=sq_, qs=qs, rows=rows, m=m, g=g):
                            P.op("pe", lambda e: e.matmul(PS[sq_][:, :], lhsT=KCg[rows, g * 128:(g + 1) * 128], rhs=QT[rows, m, qs], start=True, stop=False),
                                 reads=[L_N["kc"], L_QT[m]], writes=[L_PS[sq_]])
                            P.op("pe", lambda e: e.matmul(PS[sq_][:, :], lhsT=JM, rhs=TCMP[:, qs], start=False, stop=True), reads=[L_NCB, L_N["tcmp"]], writes=[L_PS[sq_]])
                            P.op("act", lambda e: e.activation(out=PT[sq_], in_=PS[sq_][:, :], func=AF.Exp, scale=0.125), reads=[L_PS[sq_]], writes=[L_PT[sq_]])

                        def pv_(sq_=sq_, oq=oq, qg=qg, g=g, h=h, r=r):
                            for qt in range(4):
                                P.op("pe", lambda e, qt=qt: e.matmul(PS[oq][:, qt * 97:(qt + 1) * 97], lhsT=PT[sq_][:, qt * 128:(qt + 1) * 128], rhs=VCg[:, g * 97:(g + 1) * 97], start=True, stop=True),
                                     reads=[L_PT[sq_], L_N["vc"]], writes=[L_PS[oq]])
                            pv = finalize(PS[oq], L_PS[oq], 97, qg, h, 0, "oc")
                            for qt in range(4):
                                T = 4 * qg + qt
                                if r == 0:
                                    P.op("dve", lambda e, qt=qt, T=T: e.tensor_scalar(PSLC[:, T, :], pv[:, qt, 65:97], RDEN[:, qt:qt + 1], None, op0=ALU.mult),
                                         reads=[L_PS[oq], L_N["rden"]], writes=[L_N["pslc"]])
                                else:
                                    P.op("dve", lambda e, qt=qt, T=T: e.scalar_tensor_tensor(out=PSLC[:, T, :], in0=pv[:, qt, 65:97], scalar=RDEN[:, qt:qt + 1], in1=PSLC[:, T, :], op0=ALU.mult, op1=ALU.add),
                                         reads=[L_PS[oq], L_N["rden"], L_N["pslc"]], writes=[L_N["pslc"]])
                        csteps.append((score, pv_))
                    for i, (sc_f, pv_f) in enumerate(csteps):
                        if i == 0:
                            sc_f()
                        if i + 1 < len(csteps):
                            csteps[i + 1][0]()
                        pv_f()
                SCf = AF32[:, 21216:21216 + NKT * 32]
                PSLCf = AF32[:, 19424:19424 + NKT * 32]
                P.op("dve", lambda e: e.tensor_tensor(out=SCf, in0=PSLCf, in1=SELTAB[:, 0:NKT * 32], op=ALU.mult), reads=[L_N["pslc"], L_SELTAB], writes=[L_N["sc"]])
                P.op("dve", lambda e: e.tensor_tensor(out=SCf, in0=SCf, in1=SELTAB[:, 512:512 + NKT * 32], op=ALU.add), reads=[L_N["sc"], L_SELTAB], writes=[L_N["sc"]])
                for T in range(NKT):
                    P.op("dve", lambda e, T=T: e.max(TOP8[:, T * 8:(T + 1) * 8], SC[:, T, :]), reads=[L_N["sc"]], writes=[L_N["top8"]])
                for T in range(NKT):
                    P.op("dve", lambda e, T=T: e.tensor_scalar(SC[:, T, :], SC[:, T, :], TOP8[:, T * 8 + 7:T * 8 + 8], None, op0=ALU.is_ge), reads=[L_N["sc"], L_N["top8"]], writes=[L_N["sc"]])
                P.op("dve", lambda e: e.tensor_scalar(SNB, SCf, -1.0, 30000.0, op0=ALU.add, op1=ALU.mult), reads=[L_N["sc"]], writes=[L_N["snb"]])
                PSB = PS[6][:, :].bitcast(BF16)
                for T4 in range(NKT // 4):
                    for ti in range(4):
                        T = T4 * 4 + ti
                        P.op("pe", lambda e, T=T, ti=ti: e.transpose(PSB[0:32, ti * 128:(ti + 1) * 128], SNB[:, T * 32:(T + 1) * 32], IDB), reads=[L_N["snb"], L_NCB], writes=[L_PS[6]])
                    P.op("act", lambda e, T4=T4: e.activation(out=SELT[:, T4 * 512:(T4 + 1) * 512], in_=PSB[0:32, 0:512], func=AF.Copy), reads=[L_PS[6]], writes=[L_N["selt"]])
                for r in range(4):
                    h = 4 * g + r
                    m, rows = h // 2, slice(64 * (h % 2), 64 * (h % 2) + 64)
                    P.dma("sp", lambda e, h=h: e.dma_start(out=TSEL, in_=bass.AP(BV_d.tensor, h * 4096 + OFFB - 511, [[1, 128], [1, 1152]])), L_N["tsel"], reads=[L_BVd], writes=[L_N["tsel"]])
                    P.dma("sp", lambda e, h=h: e.dma_start(out=TWIN, in_=bass.AP(BV_d.tensor, (16 + h) * 4096 + OFFB - 511, [[1, 128], [1, 1408]])), L_N["twin"], reads=[L_BVd], writes=[L_N["twin"]])
                    steps = []
                    for br, Kg, lK, Vg, lV, TB, lT in ((1, KSLg, L_N["ksl"], VSLg, L_N["vsl"], TSEL, L_N["tsel"]), (2, KWINg, L_N["kwin"], VWINg, L_N["vwin"], TWIN, L_N["twin"])):
                        for qg in range(NQG):
                            qs = slice(qg * 512, (qg + 1) * 512)
                            oq = obank()
                            kt_lo = 0 if br == 1 else max(0, 4 * qg - 4)
                            bank_used = [False]
                            for kt in range(kt_lo, 4 * qg + 4):
                                dl = 4 * qg - kt
                                col0 = 128 * ((min(dl, 2) if br == 1 else dl) + 3)
                                sq_ = sbank()
                                ks = slice(kt * 128, (kt + 1) * 128)
                                vq = [qt for qt in range(4) if not (4 * qg + qt < kt or (br == 2 and 4 * qg + qt > kt + 4))]
                                c0, c1 = 128 * vq[0], 128 * (vq[-1] + 1)
                                qsv = slice(qg * 512 + c0, qg * 512 + c1)

                                def score(sq_=sq_, rows=rows, m=m, qsv=qsv, c0=c0, c1=c1, ks=ks, Kg=Kg, lK=lK, col0=col0, TB=TB, lT=lT, br=br):
                                    P.op("pe", lambda e: e.matmul(PS[sq_][:, c0:c1], lhsT=Kg[rows, ks], rhs=QT[rows, m, qsv], start=True, stop=False),
                                         reads=[lK, L_QT[m]], writes=[L_PS[sq_]])
                                    P.op("pe", lambda e: e.matmul(PS[sq_][:, c0:c1], lhsT=JM, rhs=TB[:, col0 + c0:col0 + c1], start=False, stop=(br == 2)),
                                         reads=[L_NCB, lT], writes=[L_PS[sq_]])
                                    if br == 1:
                                        P.op("pe", lambda e: e.matmul(PS[sq_][:, c0:c1], lhsT=EBLK[0:32, ks], rhs=SELT[:, qsv], start=False, stop=True),
                                             reads=[L_EBLK, L_N["selt"]], writes=[L_PS[sq_]])
                                    P.op("act", lambda e: e.activation(out=PT[sq_][:, c0:c1], in_=PS[sq_][:, c0:c1], func=AF.Exp, scale=0.125), reads=[L_PS[sq_]], writes=[L_PT[sq_]])

                                pvl = []
                                for qt in range(4):
                                    T = 4 * qg + qt
                                    if T < kt or (br == 2 and T > kt + 4):
                                        continue
                                    first = not bank_used[0]
                                    bank_used[0] = True
                                    pvl.append((qt, first, kt == T))
                                last = (kt == 4 * qg + 3)

                                def pv_(sq_=sq_, oq=oq, kt=kt, Vg=Vg, lV=lV, pvl=pvl, last=last, qg=qg, h=h, br=br):
                                    for qt, first, stop_ in pvl:
                                        P.op("pe", lambda e, qt=qt, first=first, stop_=stop_: e.matmul(PS[oq][:, qt * 65:(qt + 1) * 65], lhsT=PT[sq_][:, qt * 128:(qt + 1) * 128], rhs=Vg[:, kt, :],
                                                                                               start=first, stop=stop_, skip_group_check=True),
                                             reads=[L_PT[sq_], lV], writes=[L_PS[oq]])
                                    if last:
                                        finalize(PS[oq], L_PS[oq], 65, qg, h, br, "set" if br == 1 else "add")
                                steps.append((score, pv_))
                    for i, (sc_f, pv_f) in enumerate(steps):
                        if i == 0:
                            sc_f()
                        if i + 1 < len(steps):
                            steps[i + 1][0]()
                        pv_f()
                    for qg in range(NQG):
                        T0 = 4 * qg
                        P.op("dve", lambda e, T0=T0, h=h: e.tensor_tensor(out=OC[:, T0:T0 + 4, h * 64:(h + 1) * 64], in0=OACC[:, T0:T0 + 4, :], in1=OC[:, T0:T0 + 4, h * 64:(h + 1) * 64], op=ALU.add),
                             reads=[L_N["oacc"], L_OC], writes=[L_OC])
            OT = QT
            PSB = PS[6][:, :].bitcast(BF16)
            PSB2 = PS[7][:, :].bitcast(BF16)
            for T in range(NKT):
                for mg in range(2):
                    pb, lpb = (PSB, L_PS[6]) if mg == 0 else (PSB2, L_PS[7])
                    for mi in range(4):
                        mm_ = mg * 4 + mi
                        P.op("pe", lambda e, pb=pb, mi=mi, T=T, mm_=mm_: e.transpose(pb[:, mi * 128:(mi + 1) * 128], OC[:, T, mm_ * 128:(mm_ + 1) * 128], IDB), reads=[L_OC, L_NCB], writes=[lpb])
                    eng = "act" if mg == 0 else "dve"
                    if mg == 0:
                        P.op("act", lambda e, pb=pb, T=T, mg=mg: e.activation(out=OT[:, mg * 4:(mg + 1) * 4, T * 128:(T + 1) * 128], in_=pb[:, 0:512].rearrange("p (m t) -> p m t", m=4), func=AF.Copy),
                             reads=[lpb], writes=L_QT[mg * 4:(mg + 1) * 4])
                    else:
                        P.op("dve", lambda e, pb=pb, T=T, mg=mg: e.tensor_copy(OT[:, mg * 4:(mg + 1) * 4, T * 128:(T + 1) * 128], pb[:, 0:512].rearrange("p (m t) -> p m t", m=4)),
                             reads=[lpb], writes=L_QT[mg * 4:(mg + 1) * 4])

            def load_wo(mo):
                b = n_ctr[0] % 2
                n_ctr[0] += 1
                P.dma("pool", lambda e, b=b, mo=mo: e.dma_start(out=WO[b][:, :, :], in_=wo[:, :, mo * 128:(mo + 1) * 128]), L_WO[b], writes=[L_WO[b]])
                return b
            pend = [load_wo(0)]
            for mo in range(8):
                b = pend.pop(0)
                if mo + 1 < 8:
                    pend.append(load_wo(mo + 1))
                for tt in range(NT):
                    ts = slice(tt * 512, (tt + 1) * 512)
                    q = sbank()
                    for k in range(8):
                        P.op("pe", lambda e, q=q, k=k, b=b, ts=ts: e.matmul(PS[q][:, :], lhsT=WO[b][:, k, :], rhs=OT[:, k, ts], start=(k == 0), stop=(k == 7)),
                             reads=[L_WO[b], L_QT[k]], writes=[L_PS[q]])
                    P.op("dve", lambda e, q=q, mo=mo, ts=ts: e.tensor_tensor(out=H3[:, mo, ts], in0=H3[:, mo, ts], in1=PS[q][:, :], op=ALU.add),
                         reads=[L_PS[q], L_H[mo][tt]], writes=[L_H[mo][tt]])
            P.op("dve", lambda e: e.memset(BARD[:], 0.0), writes=[t for r in L_XN for t in r] + [L_OC])

        if "nsa" in parts:
            nsa_setup()
        for s in range(NSEQ):
            arena_barrier()
            for t128 in range(S // 128):
                b = t128 % 2
                tt = t128 // 4
                row0 = s * S + t128 * 128
                P.dma("sp", lambda e, b=b, row0=row0: e.dma_start(out=XIN[b], in_=x_d[row0:row0 + 128, :]), L_XIN[b], writes=[L_XIN[b]])
                for cg in range(2):
                    pq = 6 + cg
                    for ci in range(4):
                        c = cg * 4 + ci
                        P.op("pe", lambda e, b=b, c=c, ci=ci, pq=pq: e.transpose(PS[pq][:, ci * 128:(ci + 1) * 128], XIN[b][:, c * 128:(c + 1) * 128], IDF[:]),
                             reads=[L_XIN[b], L_IDF], writes=[L_PS[pq]])
                    P.op("act" if cg == 0 else "dve",
                         (lambda e, cg=cg, pq=pq, t128=t128: e.activation(out=H3[:, cg * 4:(cg + 1) * 4, t128 * 128:(t128 + 1) * 128], in_=PS[pq][:, :].rearrange("p (c t) -> p c t", c=4), func=AF.Copy)) if cg == 0 else
                         (lambda e, cg=cg, pq=pq, t128=t128: e.tensor_copy(H3[:, cg * 4:(cg + 1) * 4, t128 * 128:(t128 + 1) * 128], PS[pq][:, :].rearrange("p (c t) -> p c t", c=4))),
                         reads=[L_PS[pq]], writes=[L_H[cg * 4 + ci][tt] for ci in range(4)])

            for layer in range(2):
                rmsnorm((layer * 3 + 0) * 8)
                ffn(0, layer)
                if layer == 0 and "s5" in parts:
                    s5_mixer()
                if layer == 1 and "nsa" in parts:
                    nsa_mixer()
                rmsnorm((layer * 3 + 2) * 8)
                ffn(1, layer)
                if layer == 0 and "nsa" in parts:
                    kv_phase()

            arena_barrier()
            for t128 in range(S // 128):
                b = t128 % 2
                tt = t128 // 4
                row0 = s * S + t128 * 128
                for cg in range(2):
                    pq = 6 + cg
                    for ci in range(4):
                        c = cg * 4 + ci
                        P.op("pe", lambda e, c=c, ci=ci, pq=pq, t128=t128: e.transpose(PS[pq][:, ci * 128:(ci + 1) * 128], H3[:, c, t128 * 128:(t128 + 1) * 128], IDF[:]),
                             reads=[L_H[c][tt], L_IDF], writes=[L_PS[pq]])
                    P.op("act" if cg == 0 else "dve",
                         (lambda e, b=b, cg=cg, pq=pq: e.activation(out=XIN[b][:, cg * 512:(cg + 1) * 512], in_=PS[pq][:, :], func=AF.Copy)) if cg == 0 else
                         (lambda e, b=b, cg=cg, pq=pq: e.tensor_copy(XIN[b][:, cg * 512:(cg + 1) * 512], PS[pq][:, :])),
                         reads=[L_PS[pq]], writes=[L_XIN[b]])
                P.dma("sp", lambda e, b=b, row0=row0: e.dma_start(out=y_d[row0:row0 + 128, :], in_=XIN[b]), L_XIN[b], reads=[L_XIN[b]], writes=[L_Y])

        P.wait_all_dma("sp")
        P.emit()
    return nc


def prep_inputs(inp):
    g = np.stack([inp["ffn1_norm"], inp["mix_norm"], inp["ffn2_norm"]], axis=1)
    gains = np.ascontiguousarray(g.reshape(2, 3, 8, 128).transpose(3, 0, 1, 2).reshape(128, 48)).astype(np.float32)
    gains = np.ascontiguousarray(np.concatenate([gains, inp["kv_norm"].reshape(8, 128).T], axis=1)).astype(np.float32)
    m = dict(host_consts())
    m["gains"] = gains
    a_re, a_im, ldt = inp["s5_a_re"][0], inp["s5_a_im"][0], inp["s5_log_dt"][0]
    b_re, b_im, c_re, c_im = inp["s5_b_re"][0], inp["s5_b_im"][0], inp["s5_c_re"][0], inp["s5_c_im"][0]
    pl, gl, hp, kc, nn = np.meshgrid(np.arange(4), np.arange(2), np.arange(16), np.arange(8), np.arange(64), indexing="ij")
    gg = 8 * kc + 2 * pl + gl
    R = np.stack([a_re[gg, nn], a_im[gg, nn], ldt[gg], b_re[gg, nn, hp], b_im[gg, nn, hp]], axis=0)
    m["s5R"] = np.ascontiguousarray(R.reshape(5, 128, 512).transpose(1, 0, 2).reshape(128, 2560)).astype(np.float32)
    gl2, n2, pr2 = np.meshgrid(np.arange(2), np.arange(64), np.arange(32), indexing="ij")
    g2 = 2 * pr2 + gl2
    S3 = np.stack([a_re[g2, n2], a_im[g2, n2], ldt[g2]], axis=0).reshape(3, 128, 32).transpose(1, 0, 2).reshape(128, 96)
    gl3, n3, pr3, h3 = np.meshgrid(np.arange(2), np.arange(64), np.arange(32), np.arange(16), indexing="ij")
    g3 = 2 * pr3 + gl3
    C2 = np.stack([c_re[g3, h3, n3], c_im[g3, h3, n3]], axis=0).reshape(2, 128, 512).transpose(1, 0, 2).reshape(128, 1024)
    m["s5S"] = np.ascontiguousarray(np.concatenate([S3, C2], axis=1)).astype(np.float32)
    p = np.arange(128)
    maskR = np.stack([((p // 16) % 2 == 0), ((p // 16) % 2 == 1)], axis=1).astype(np.float32)
    dsk = inp["s5_d"][0].reshape(8, 128).T
    m["s5M"] = np.ascontiguousarray(np.concatenate([maskR, dsk], axis=1)).astype(np.float32)
    m["s5_w_glu"] = np.ascontiguousarray(inp["s5_w_glu"], dtype=np.float32)
    for k in ("w_kv", "w_qg", "w_o", "cmp_k_w1", "cmp_v_w1", "cmp_k_w2", "cmp_v_w2", "rel_bias"):
        m[k] = np.ascontiguousarray(inp[k], dtype=np.float32)
    m["posT_k"] = np.ascontiguousarray(inp["cmp_pos_k"].T).astype(np.float32)
    m["posT_v"] = np.ascontiguousarray(inp["cmp_pos_v"].T).astype(np.float32)
    dup = lambda v: np.concatenate([v, v])
    ngn = np.zeros((128, 8), np.float32)
    ngn[:, 0] = dup(inp["q_norm"][0])
    ngn[:, 1] = dup(inp["k_norm_slc"])
    ngn[:, 2] = dup(inp["k_norm_win"])
    ngn[:, 3] = dup(inp["k_norm_cmp"])
    m["ngn"] = ngn
    for k in ("ffn1_w_in", "ffn2_w_in", "ffn1_w_out", "ffn2_w_out"):
        m[k] = np.ascontiguousarray(inp[k], dtype=np.float32)
    return m


_NC_CACHE = {}


def kernel(**inp):
    x = np.ascontiguousarray(inp["x"], dtype=np.float32)
    B, S, Dm = x.shape
    n = 8
    per = B // n
    shared = prep_inputs(inp)
    key = (per, S)
    if key not in _NC_CACHE:
        _NC_CACHE[key] = build(NSEQ=per, S=S, parts=("ffn", "s5", "nsa"))
    nc = _NC_CACHE[key]
    in_maps = []
    for i in range(n):
        m = dict(shared)
        m["x"] = x[i * per:(i + 1) * per].reshape(per * S, Dm)
        in_maps.append(m)
    res = run_bass_kernel_spmd(nc, in_maps, core_ids=list(range(n)))
    out = np.concatenate([r["y"].reshape(per, S, Dm) for r in res.results], axis=0)
    return out.astype(np.float32)
```
